# Optimizing a Trainium2 kernel written in Bass

```python
import jax, jax.numpy as jnp
from jax import lax
import numpy as np

D_MODEL = 1024
BATCH = 4
SEQ = 4096
DEPTH = 4

EXPAND = 2
D_INNER = EXPAND * D_MODEL
N_HEADS = 8
HEAD_QK = D_MODEL // N_HEADS
HEAD_V = D_INNER // N_HEADS
QK_W = N_HEADS * HEAD_QK
CONV_K = 4
CHUNK = 64
NORM_EPS = 1e-6
N_MIXERS = 2
N_GDN = (DEPTH + 1) // 2
N_MLSTM = DEPTH // 2
GDN_IN = 2 * QK_W + 2 * D_INNER + 2 * N_HEADS
MLSTM_IN = 2 * QK_W + 3 * D_INNER + 2 * N_HEADS

kernel_name = 'hybrid_gdn_mlstm_interleaved'


def rmsnorm(x, w):
    xf = x.astype(jnp.float32)
    y = xf * lax.rsqrt(jnp.mean(xf * xf, -1, keepdims=True) + NORM_EPS)
    return (y * w.astype(jnp.float32)).astype(x.dtype)


def l2norm(t):
    return t * lax.rsqrt(jnp.sum(t * t, -1, keepdims=True) + 1e-6)


def causal_dwconv(x, w):
    return lax.conv_general_dilated(x, w[:, None, :].astype(x.dtype), window_strides=(1,),
                                    padding=[(CONV_K - 1, 0)],
                                    dimension_numbers=('NWC', 'WIO', 'NWC'),
                                    feature_group_count=x.shape[-1])


def to_chunks(t):
    b, s, h = t.shape[:3]
    t = t.reshape((b, s // CHUNK, CHUNK, h) + t.shape[3:])
    return jnp.moveaxis(t, (1, 3), (0, 2))


def from_chunks(t):
    n, b, h, c, d = t.shape
    return jnp.moveaxis(t, (0, 2), (1, 3)).reshape(b, n * c, h, d)


def gated_delta_chunked(q, k, v, g, beta):
    q, k, v, g, beta = (to_chunks(t) for t in (q, k, v, g, beta))
    incl = jnp.tril(jnp.ones((CHUNK, CHUNK), bool))
    strict = jnp.tril(jnp.ones((CHUNK, CHUNK), bool), -1)
    gc = jnp.cumsum(g, -1)
    diff = gc[..., :, None] - gc[..., None, :]
    L = jnp.where(incl, jnp.exp(jnp.where(incl, diff, 0.0)), 0.0)
    kk = jnp.einsum('nbhid,nbhjd->nbhij', k, k)
    a = jnp.eye(CHUNK, dtype=jnp.float32) + jnp.where(strict, beta[..., :, None] * kk * L, 0.0)
    u = lax.linalg.triangular_solve(a, beta[..., None] * v, left_side=True, lower=True, unit_diagonal=True)
    wk = lax.linalg.triangular_solve(a, (beta * jnp.exp(gc))[..., None] * k, left_side=True, lower=True,
                                     unit_diagonal=True)
    attn = jnp.einsum('nbhid,nbhjd->nbhij', q, k) * L
    qg = q * jnp.exp(gc)[..., None]
    kd = k * jnp.exp(gc[..., -1:] - gc)[..., None]
    decay = jnp.exp(gc[..., -1])

    def step(S, xs):
        u_c, wk_c, attn_c, qg_c, kd_c, dec_c = xs
        w = u_c - jnp.einsum('bhcd,bhde->bhce', wk_c, S)
        o = jnp.einsum('bhcd,bhde->bhce', qg_c, S) + jnp.einsum('bhij,bhje->bhie', attn_c, w)
        S = dec_c[..., None, None] * S + jnp.einsum('bhcd,bhce->bhde', kd_c, w)
        return S, o

    nb, bb, hb = q.shape[:3]
    S0 = jnp.zeros((bb, hb, q.shape[-1], v.shape[-1]), jnp.float32)
    _, o = lax.scan(step, S0, (u, wk, attn, qg, kd, decay))
    return from_chunks(o)


def mlstm_chunked(q, k, v, ig, lf):
    q, k, v, ig, lf = (to_chunks(t) for t in (q, k, v, ig, lf))
    incl = jnp.tril(jnp.ones((CHUNK, CHUNK), bool))
    b = jnp.cumsum(lf, -1)
    D = jnp.where(incl, b[..., :, None] - b[..., None, :] + ig[..., None, :], -jnp.inf)
    g_end = b[..., -1:] - b + ig
    b_end = b[..., -1]

    def step(carry, xs):
        S, n, m = carry
        q_c, k_c, v_c, b_c, D_c, ge_c, be_c = xs
        m_t = jnp.maximum(b_c + m[..., None], jnp.max(D_c, -1))
        inter = jnp.exp(b_c + m[..., None] - m_t)
        A = jnp.einsum('bhid,bhjd->bhij', q_c, k_c) * jnp.exp(D_c - m_t[..., None])
        num = inter[..., None] * jnp.einsum('bhcd,bhde->bhce', q_c, S) + jnp.einsum('bhij,bhje->bhie', A, v_c)
        nq = inter * jnp.einsum('bhcd,bhd->bhc', q_c, n) + jnp.sum(A, -1)
        h = num / jnp.maximum(jnp.abs(nq), jnp.exp(-m_t))[..., None]
        m_new = jnp.maximum(be_c + m, jnp.max(ge_c, -1))
        w_state = jnp.exp(be_c + m - m_new)
        w_key = jnp.exp(ge_c - m_new[..., None])
        S = w_state[..., None, None] * S + jnp.einsum('bhcd,bhce->bhde', k_c * w_key[..., None], v_c)
        n = w_state[..., None] * n + jnp.einsum('bhcd,bhc->bhd', k_c, w_key)
        return (S, n, m_new), h

    bb, hb = q.shape[1:3]
    carry0 = (jnp.zeros((bb, hb, q.shape[-1], v.shape[-1]), jnp.float32),
              jnp.zeros((bb, hb, q.shape[-1]), jnp.float32),
              jnp.zeros((bb, hb), jnp.float32))
    _, h = lax.scan(step, carry0, (q, k, v, b, D, g_end, b_end))
    return from_chunks(h)


def gdn_mixer(h, w_in, conv_w, a_log, dt_bias, norm_w, w_out):
    B, T, _ = h.shape
    p = h @ w_in
    o1 = 2 * QK_W + D_INNER
    qkv, z, a, bt = jnp.split(p, [o1, o1 + D_INNER, o1 + D_INNER + N_HEADS], -1)
    qkv = jax.nn.silu(causal_dwconv(qkv, conv_w))
    q, k, v = jnp.split(qkv.astype(jnp.float32), [QK_W, 2 * QK_W], -1)
    q = l2norm(q.reshape(B, T, N_HEADS, HEAD_QK)) * (HEAD_QK ** -0.5)
    k = l2norm(k.reshape(B, T, N_HEADS, HEAD_QK))
    v = v.reshape(B, T, N_HEADS, HEAD_V)
    g = -jnp.exp(a_log.astype(jnp.float32)) * jax.nn.softplus(a.astype(jnp.float32) + dt_bias.astype(jnp.float32))
    beta = jax.nn.sigmoid(bt.astype(jnp.float32))
    o = gated_delta_chunked(q, k, v, g, beta)
    o = rmsnorm(o, norm_w) * jax.nn.silu(z.astype(jnp.float32).reshape(B, T, N_HEADS, HEAD_V))
    return o.reshape(B, T, D_INNER).astype(h.dtype) @ w_out


def mlstm_mixer(h, w_in, conv_w, i_bias, f_bias, norm_w, w_out):
    B, T, _ = h.shape
    p = h @ w_in
    c0 = 2 * QK_W
    qk, v, og, z, ig, fg = jnp.split(p, [c0, c0 + D_INNER, c0 + 2 * D_INNER, c0 + 3 * D_INNER,
                                         c0 + 3 * D_INNER + N_HEADS], -1)
    qk = jax.nn.silu(causal_dwconv(qk, conv_w))
    q, k = jnp.split(qk.astype(jnp.float32), [QK_W], -1)
    q = q.reshape(B, T, N_HEADS, HEAD_QK)
    k = k.reshape(B, T, N_HEADS, HEAD_QK) * (HEAD_QK ** -0.5)
    v = v.astype(jnp.float32).reshape(B, T, N_HEADS, HEAD_V)
    i_pre = ig.astype(jnp.float32) + i_bias.astype(jnp.float32)
    lf = jax.nn.log_sigmoid(fg.astype(jnp.float32) + f_bias.astype(jnp.float32))
    hc = mlstm_chunked(q, k, v, i_pre, lf)
    gate = jax.nn.sigmoid(og.astype(jnp.float32)) * jax.nn.silu(z.astype(jnp.float32))
    hc = rmsnorm(hc, norm_w).reshape(B, T, D_INNER) * gate
    return hc.astype(h.dtype) @ w_out


def setup_inputs(seed: int = 0) -> dict:
    key = jax.random.key(seed)
    ks = jax.random.split(key, 16)
    nrm = jax.random.normal
    x = nrm(ks[0], (BATCH, SEQ, D_MODEL), jnp.float32)
    norm_w = 1.0 + 0.02 * nrm(ks[1], (DEPTH, D_MODEL), jnp.float32)
    final_norm_w = 1.0 + 0.02 * nrm(ks[2], (D_MODEL,), jnp.float32)
    gdn_w_in = nrm(ks[3], (N_GDN, D_MODEL, GDN_IN), jnp.float32) * D_MODEL ** -0.5
    gdn_conv_w = nrm(ks[4], (N_GDN, CONV_K, 2 * QK_W + D_INNER), jnp.float32) * CONV_K ** -0.5
    gdn_a_log = jnp.log(jax.random.uniform(ks[5], (N_GDN, N_HEADS), jnp.float32, 1.0, 16.0))
    dt = jnp.exp(jax.random.uniform(ks[6], (N_GDN, N_HEADS), jnp.float32, jnp.log(1e-3), jnp.log(1e-1)))
    gdn_dt_bias = dt + jnp.log(-jnp.expm1(-dt))
    gdn_norm_w = 1.0 + 0.02 * nrm(ks[7], (N_GDN, HEAD_V), jnp.float32)
    gdn_w_out = nrm(ks[8], (N_GDN, D_INNER, D_MODEL), jnp.float32) * D_INNER ** -0.5
    mlstm_w_in = nrm(ks[9], (N_MLSTM, D_MODEL, MLSTM_IN), jnp.float32) * D_MODEL ** -0.5
    mlstm_conv_w = nrm(ks[10], (N_MLSTM, CONV_K, 2 * QK_W), jnp.float32) * CONV_K ** -0.5
    mlstm_i_bias = 0.1 * nrm(ks[11], (N_MLSTM, N_HEADS), jnp.float32)
    mlstm_f_bias = jnp.linspace(3.0, 6.0, N_HEADS, dtype=jnp.float32)[None, :] + 0.1 * nrm(ks[12], (N_MLSTM, N_HEADS), jnp.float32)
    mlstm_norm_w = 1.0 + 0.02 * nrm(ks[13], (N_MLSTM, HEAD_V), jnp.float32)
    mlstm_w_out = nrm(ks[14], (N_MLSTM, D_INNER, D_MODEL), jnp.float32) * D_INNER ** -0.5
    return {'x': x, 'norm_w': norm_w, 'final_norm_w': final_norm_w,
            'gdn_w_in': gdn_w_in, 'gdn_conv_w': gdn_conv_w, 'gdn_a_log': gdn_a_log, 'gdn_dt_bias': gdn_dt_bias,
            'gdn_norm_w': gdn_norm_w, 'gdn_w_out': gdn_w_out,
            'mlstm_w_in': mlstm_w_in, 'mlstm_conv_w': mlstm_conv_w, 'mlstm_i_bias': mlstm_i_bias,
            'mlstm_f_bias': mlstm_f_bias, 'mlstm_norm_w': mlstm_norm_w, 'mlstm_w_out': mlstm_w_out}


def reference(x, norm_w, final_norm_w, gdn_w_in, gdn_conv_w, gdn_a_log, gdn_dt_bias, gdn_norm_w, gdn_w_out,
              mlstm_w_in, mlstm_conv_w, mlstm_i_bias, mlstm_f_bias, mlstm_norm_w, mlstm_w_out):
    for i in range(DEPTH):
        h = rmsnorm(x, norm_w[i])
        j = i // N_MIXERS
        if i % N_MIXERS == 0:
            y = gdn_mixer(h, gdn_w_in[j], gdn_conv_w[j], gdn_a_log[j], gdn_dt_bias[j], gdn_norm_w[j], gdn_w_out[j])
        else:
            y = mlstm_mixer(h, mlstm_w_in[j], mlstm_conv_w[j], mlstm_i_bias[j], mlstm_f_bias[j], mlstm_norm_w[j],
                            mlstm_w_out[j])
        x = x + y
    return rmsnorm(x, final_norm_w)
```

```python
import math
from contextlib import ExitStack
import numpy as np
import concourse.bass as bass
import concourse.mybir as mybir
from concourse.bass_utils import run_bass_kernel_spmd

F32 = mybir.dt.float32
BF16 = mybir.dt.bfloat16
AF = mybir.ActivationFunctionType
ALU = mybir.AluOpType

D = 1024
NHEAD = 8
DK = 128
DV = 256
DINNER = 2048
DVA = 258
BLK = 512
NCH = BLK // 128
KC = D // 128
HG = 4
NHG = NHEAD // HG
EPS = 1e-6
LVLS = [2, 4, 8, 16, 32, 64, 128]
import os
STOP = int(os.environ.get('KSTOP', '99'))
SUB = int(os.environ.get('KSUB', '99'))
KX = int(os.environ.get('KX', '0'))

C_ID = 0
C_PM = 128
C_NM = 256
C_MT = 384
C_LV = 512
C_OC = C_LV + 6 * 128
C_SEL = C_OC + 64
C_ONE = C_SEL + 8 * 128
NCST = C_ONE + 128


def make_consts():
    c = np.zeros((128, NCST), np.float32)
    i = np.arange(128)[:, None]
    j = np.arange(128)[None, :]
    c[:, C_ID:C_ID + 128] = (i == j)
    c[:, C_PM:C_PM + 128] = np.where(i > j, 0.0, 1e30)
    c[:, C_NM:C_NM + 128] = np.where(j >= i, 0.0, -1e30)
    c[:, C_MT:C_MT + 128] = (j >= i)
    for li, s in enumerate(LVLS[:-1]):
        c[:, C_LV + li * 128:C_LV + (li + 1) * 128] = ((i // s) == (j // s))
    oc = np.zeros((128, 8, 8), np.float32)
    for h in range(8):
        oc[:, h, h] = 1.0
    c[:, C_OC:C_OC + 64] = oc.reshape(128, 64)
    sel = np.zeros((128, 8, 128), np.float32)
    for h in range(8):
        sel[h, h, :] = 1.0
    c[:, C_SEL:C_SEL + 1024] = sel.reshape(128, 1024)
    c[:, C_ONE:C_ONE + 128] = 1.0
    return c


class Trk:
    __slots__ = ("name", "w", "r", "sem", "semv")

    def __init__(self, name):
        self.name = name
        self.w = None
        self.r = []
        self.sem = None
        self.semv = 0


class Eng:
    def __init__(self, h, sem, name):
        self.h = h
        self.sem = sem
        self.name = name
        self.cnt = 0
        self.waited = {}


class Sched:
    def __init__(self, nc, es):
        self.nc = nc
        self.es = es
        self.eng = {}
        for nm, h in (("pe", nc.tensor), ("act", nc.scalar), ("dve", nc.vector), ("pool", nc.gpsimd), ("sp", nc.sync)):
            sem = es.enter_context(nc.semaphore("s_" + nm))
            self.eng[nm] = Eng(h, sem, nm)
        self.nsem = 5

    def _wait(self, e, evs):
        need = {}
        for ev in evs:
            if ev is None:
                continue
            sem, val = ev
            k = id(sem)
            if k not in need or need[k][1] < val:
                need[k] = (sem, val)
        for k, (sem, val) in need.items():
            if sem is e.sem and e.name == "pe":
                continue
            if e.waited.get(k, 0) >= val:
                continue
            e.h.wait_ge(sem, val)
            e.waited[k] = val

    def _deps(self, reads, writes):
        evs = []
        for t in reads:
            evs.append(t.w)
        for t in writes:
            evs.append(t.w)
            evs.extend(t.r)
        return evs

    def op(self, en, fn, reads=(), writes=(), signal=True):
        e = self.eng[en]
        self._wait(e, self._deps(reads, writes))
        ins = fn(e.h)
        if signal:
            e.cnt += 1
            ins.then_inc(e.sem, 1)
            ev = (e.sem, e.cnt)
        else:
            ev = (e.sem, e.cnt + 1)
        for t in reads:
            t.r.append(ev)
        for t in writes:
            t.w = ev
            t.r = []
        return ins

    def dma(self, en, out, in_, reads, writes, owner):
        e = self.eng[en]
        if owner.sem is None:
            owner.sem = self.es.enter_context(self.nc.semaphore("d_" + owner.name))
            self.nsem += 1
        self._wait(e, self._deps(reads, writes))
        owner.semv += 16
        e.h.dma_start(out=out, in_=in_).then_inc(owner.sem, 16)
        ev = (owner.sem, owner.semv)
        for t in reads:
            t.r.append(ev)
        for t in writes:
            t.w = ev
            t.r = []
        return ev


class Pool:
    def __init__(self, nc, es, name, shape, dtype, n, psum=False):
        self.items = []
        for i in range(n):
            if psum:
                t = es.enter_context(nc.psum_tensor(f"pp_{name}{i}", shape, dtype))
            else:
                t = es.enter_context(nc.sbuf_tensor(f"pl_{name}{i}", shape, dtype))
            self.items.append((t, Trk(f"{name}{i}")))
        self.i = 0

    def next(self):
        it = self.items[self.i % len(self.items)]
        self.i += 1
        return it


def layer_kind(l):
    return "gdn" if l % 2 == 0 else "mlstm"


def fm_tiles(kind):
    out = []
    for hg in range(NHG):
        tl = []
        for h in range(hg * HG, (hg + 1) * HG):
            tl.append(("q", h, 0))
        for h in range(hg * HG, (hg + 1) * HG):
            tl.append(("k", h, 0))
        if kind == "gdn":
            for h in range(hg * HG, (hg + 1) * HG):
                tl.append(("v", h, 0))
                tl.append(("v", h, 1))
        out.append(tl)
    return out


def tm_tiles(kind):
    out = []
    for hg in range(NHG):
        tl = []
        names = ["z"] if kind == "gdn" else ["v", "o", "z"]
        for nm in names:
            for t in range(HG // 2):
                tl.append((nm, hg * HG + 2 * t))
        out.append(tl)
    return out


def col_offsets(kind):
    if kind == "gdn":
        return {"q": 0, "k": 1024, "v": 2048, "z": 4096, "g0": 6144, "g1": 6152}
    return {"q": 0, "k": 1024, "v": 2048, "o": 4096, "z": 6144, "g0": 8192, "g1": 8200}


def layout_weights(kind, w_in, conv_w, w_out):
    co = col_offsets(kind)
    wk = w_in.reshape(KC, 128, -1)
    fm = []
    cws = []
    convoff = {"q": 0, "k": 1024, "v": 2048}
    for tl in fm_tiles(kind):
        for (nm, h, sub) in tl:
            width = 128 if nm in ("q", "k") else 256
            c0 = co[nm] + h * width + sub * 128
            fm.append(np.ascontiguousarray(wk[:, :, c0:c0 + 128].transpose(1, 0, 2)).reshape(128, KC * 128))
            cc = convoff[nm] + h * width + sub * 128
            cws.append(conv_w[:, cc:cc + 128].T)
    tm = []
    for tl in tm_tiles(kind):
        for (nm, h0) in tl:
            c0 = co[nm] + h0 * 256
            tm.append(np.ascontiguousarray(wk[:, :, c0:c0 + 512].transpose(1, 0, 2)).reshape(128, KC * 512))
    g = np.concatenate([wk[:, :, co["g0"]:co["g0"] + 8], wk[:, :, co["g1"]:co["g1"] + 8]], axis=2)
    g = np.ascontiguousarray(g.transpose(1, 0, 2)).reshape(128, KC * 16)
    wo = np.ascontiguousarray(w_out.reshape(16, 128, D).transpose(1, 0, 2)).reshape(128, 16 * D)
    cw = np.ascontiguousarray(np.stack(cws, axis=1)).reshape(128, -1)
    return (np.ascontiguousarray(np.stack(fm, 0)), np.ascontiguousarray(np.stack(tm, 0)),
            np.ascontiguousarray(g), wo, cw.astype(np.float32))


def build(T, kinds, final_norm=True):
    nc = bass.Bass("TRN2", target_bir_lowering=False)
    NL = len(kinds)
    NB = T // BLK
    x_d = nc.dram_tensor("x", [T, D], F32, kind="ExternalInput").ap()
    out_d = nc.dram_tensor("out", [T, D], F32, kind="ExternalOutput").ap()
    cst_d = nc.dram_tensor("cst", [128, NCST], F32, kind="ExternalInput").ap()
    nw_d = nc.dram_tensor("nw", [NL + 1, D], F32, kind="ExternalInput").ap()
    hnw_d = nc.dram_tensor("hnw", [NL, DV], F32, kind="ExternalInput").ap()
    gp_d = nc.dram_tensor("gp", [NL, 8, 2], F32, kind="ExternalInput").ap()
    wd = []
    for l, kind in enumerate(kinds):
        nfm = sum(len(t) for t in fm_tiles(kind))
        ntm = sum(len(t) for t in tm_tiles(kind))
        d = {}
        d["nfm"], d["ntm"] = nfm, ntm
        d["fm32"] = nc.dram_tensor(f"wfm{l}", [nfm, 128, KC * 128], F32, kind="ExternalInput").ap()
        d["tm32"] = nc.dram_tensor(f"wtm{l}", [ntm, 128, KC * 512], F32, kind="ExternalInput").ap()
        d["g32"] = nc.dram_tensor(f"wg{l}", [128, KC * 16], F32, kind="ExternalInput").ap()
        d["o32"] = nc.dram_tensor(f"wo{l}", [128, 16 * D], F32, kind="ExternalInput").ap()
        d["cw"] = nc.dram_tensor(f"cw{l}", [128, nfm * 4], F32, kind="ExternalInput").ap()
        d["fm"] = nc.dram_tensor(f"sfm{l}", [nfm, 128, KC * 128], BF16, kind="Internal").ap()
        d["tm"] = nc.dram_tensor(f"stm{l}", [ntm, 128, KC * 512], BF16, kind="Internal").ap()
        d["g"] = nc.dram_tensor(f"sg{l}", [128, KC * 16], BF16, kind="Internal").ap()
        d["o"] = nc.dram_tensor(f"so{l}", [128, 16 * D], BF16, kind="Internal").ap()
        for k in ("fm", "tm", "g", "o"):
            d["t_" + k] = Trk(f"c{k}{l}")
        wd.append(d)

    with ExitStack() as es:
        E = es.enter_context
        S = Sched(nc, es)

        def sb(name, shape, dt):
            return E(nc.sbuf_tensor("sb_" + name, shape, dt))

        cst = sb("cst", [128, NCST], F32); t_cst = Trk("cst")
        identb = sb("identb", [128, 128], BF16); t_identb = Trk("identb")
        nwb = sb("nwb", [128, D], F32); t_nwb = Trk("nwb")
        fwb = sb("fwb", [128, D], F32); t_fwb = Trk("fwb")
        hnwb = sb("hnwb", [128, DV], F32); t_hnwb = Trk("hnwb")
        gp = sb("gp", [8, 4], F32); t_gp = Trk("gp")
        cwt = sb("cwt", [128, 32 * 4], F32); t_cwt = Trk("cwt")
        wg = sb("wg", [128, KC * 16], BF16); t_wg = Trk("wg")
        xs = sb("xs", [128, NCH, D], F32); t_xsc = [Trk(f"xs{c}") for c in range(NCH)]
        junk = sb("junk", [128, D], BF16); t_junk = Trk("junk")
        junk32 = sb("junk32", [128, BLK], F32); t_junk32 = Trk("junk32")
        st4 = sb("st4", [128, 16], F32); t_st4 = Trk("st4")
        hT = sb("hT", [128, KC, BLK], BF16); t_hT = Trk("hT")
        carry = sb("carry", [128, 32, 3], F32); t_carry = [Trk(f"carry{i}") for i in range(32)]
        hn_p = Pool(nc, es, "hn", [128, D], BF16, 2)
        wbuf = sb("wbuf", [128, 16384], BF16)
        gtr = [Trk(f"wg{i}") for i in range(16)]

        class Carve:
            def __init__(self, items):
                self.items = items
                self.i = 0

            def next(self):
                it = self.items[self.i % len(self.items)]
                self.i += 1
                return it

        wfm_p = Carve([(wbuf[:, i * 1024:(i + 1) * 1024].rearrange("p (k c) -> p k c", k=KC), [gtr[i]]) for i in range(4)])
        wtm_p = Carve([(wbuf[:, 4096 + j * 4096:4096 + (j + 1) * 4096].rearrange("p (k c) -> p k c", k=KC), gtr[4 + 4 * j:8 + 4 * j]) for j in range(2)])
        wo_h = [(wbuf[:, k * 8192:(k + 1) * 8192].rearrange("p (k c) -> p k c", k=16), gtr[8 * k:8 * k + 8]) for k in range(2)]
        pc_p = Pool(nc, es, "pc", [128, BLK + 3], F32, 2)
        acc_p = Pool(nc, es, "acc", [128, BLK], F32, 2)
        fmA = sb("fmA", [128, 16, BLK], BF16)
        fmT = [fmA[:, i, :] for i in range(16)]
        t_fmT = [Trk(f"fmT{i}") for i in range(16)]
        khT = [sb(f"khT{i}", [128, BLK], BF16) for i in range(HG)]
        t_khT = [Trk(f"khT{i}") for i in range(HG)]
        qgT = [sb(f"qgT{i}", [128, BLK], BF16) for i in range(HG)]
        t_qgT = [Trk(f"qgT{i}") for i in range(HG)]
        vaug = [sb(f"vaug{c}", [128, HG, DVA], BF16) for c in range(NCH)]
        t_vaug = [[Trk(f"vaug{c}_{h}") for h in range(HG)] for c in range(NCH)]
        zg = [sb(f"zg{c}", [128, DINNER], BF16) for c in range(NCH)]
        t_zg = [[Trk(f"zg{c}_{h}") for h in range(NHEAD)] for c in range(NCH)]
        Sst = sb("Sst", [128, NHEAD, DVA], F32); t_S = [Trk(f"S{h}") for h in range(NHEAD)]
        Sbf = sb("Sbf", [128, NHEAD, DVA], BF16); t_Sbf = [Trk(f"Sbf{h}") for h in range(NHEAD)]
        rw = sb("rw", [8, 8, BLK], F32); t_rw = Trk("rw")
        rtmp = sb("rtmp", [8, 5, BLK], F32); t_rtmp = Trk("rtmp")
        rsm = sb("rsm", [8, 16], F32); t_rsm = Trk("rsm")
        mprev = sb("mprev", [8, 2], F32); t_mprev = Trk("mprev")
        tmsc = [sb(f"tmsc{c}", [128, 8, 8], F32) for c in range(NCH)]
        t_tmsc = [Trk(f"tmsc{c}") for c in range(NCH)]
        L_p = Pool(nc, es, "Lm", [128, 128], F32, 8)
        A_p = Pool(nc, es, "Am", [128, 128], F32, 4)
        P_p = Pool(nc, es, "Pm", [128, 128], F32, 4)
        Y_p = Pool(nc, es, "Ym", [128, 128], F32, 8)
        YT_p = Pool(nc, es, "YTm", [128, 128], F32, 8)
        mbf_p = Pool(nc, es, "mbf", [128, 128], BF16, 20)
        wv_p = Pool(nc, es, "wv", [128, DVA], BF16, 8)
        sm_p = Pool(nc, es, "sm", [128, 8], F32, 8)
        ps_p = Pool(nc, es, "ps", [128, 512], F32, 8, psum=True)
        print("SBUF bytes remaining", nc.sbuf_bytes_remaining)

        ident = cst[:, C_ID:C_ID + 128]
        ones = cst[:, C_ONE:C_ONE + 128]

        def sel(h):
            return cst[0:8, C_SEL + h * 128:C_SEL + (h + 1) * 128]

        def onescol(h):
            return cst[:, C_OC + h * 8:C_OC + (h + 1) * 8]

        S.dma("sp", cst[:], cst_d[:, :], [], [t_cst], t_cst)
        S.op("dve", lambda e: e.tensor_copy(out=identb[:], in_=ident), [t_cst], [t_identb])
        S.dma("sp", fwb[:], nw_d[NL:NL + 1, :].partition_broadcast(128), [], [t_fwb], t_fwb)
        def emit_casts(l):
            d = wd[l]
            for i in range(d["nfm"]):
                S.dma("pool", d["fm"][i], d["fm32"][i], [], [d["t_fm"]], d["t_fm"])
            for i in range(d["ntm"]):
                for hh in range(2):
                    S.dma("pool", d["tm"][i][:, hh * 2048:(hh + 1) * 2048], d["tm32"][i][:, hh * 2048:(hh + 1) * 2048],
                          [], [d["t_tm"]], d["t_tm"])
            S.dma("pool", d["g"], d["g32"], [], [d["t_g"]], d["t_g"])
            for i in range(8):
                S.dma("pool", d["o"][:, i * 2048:(i + 1) * 2048], d["o32"][:, i * 2048:(i + 1) * 2048],
                      [], [d["t_o"]], d["t_o"])
        emit_casts(0)

        t_blkc = [[Trk(f"blk{b}_{c}") for c in range(NCH)] for b in range(NB)]

        def store_all(b):
            for c in range(NCH):
                S.dma("sp", ov[b][:, c, :], xs[:, c, :], [t_xsc[c]], [t_blkc[b][c]], t_blkc[b][c])
        xv = x_d.rearrange("(b c p) d -> b p c d", p=128, c=NCH)
        ov = out_d.rearrange("(b c p) d -> b p c d", p=128, c=NCH)

        def rms_stats(src_ap_fn, n, trk_src, scale_n):
            for c in range(n):
                S.op("act", lambda e, c=c: e.activation(out=junk[:, 0:src_ap_fn(c).shape[-1]], in_=src_ap_fn(c), func=AF.Square,
                                                        accum_out=st4[:, c:c + 1]),
                     [trk_src[c], t_st4], [t_junk, t_st4])
            S.op("act", lambda e: e.activation(out=st4[:, 4:4 + n], in_=st4[:, 0:n], func=AF.Sqrt, scale=1.0 / scale_n, bias=EPS),
                 [t_st4], [t_st4])
            S.op("dve", lambda e: e.reciprocal(out=st4[:, 8:8 + n], in_=st4[:, 4:4 + n]), [t_st4], [t_st4])

        for l, kind in enumerate(kinds):
            d = wd[l]
            gdn = kind == "gdn"
            last = l == NL - 1
            fmt = fm_tiles(kind)
            tmt = tm_tiles(kind)
            nfm_hg = len(fmt[0])
            S.dma("sp", nwb[:], nw_d[l:l + 1, :].partition_broadcast(128), [], [t_nwb], t_nwb)
            S.dma("sp", hnwb[:], hnw_d[l:l + 1, :].partition_broadcast(128), [], [t_hnwb], t_hnwb)
            S.dma("sp", gp[:, 0:2], gp_d[l], [], [t_gp], t_gp)
            S.dma("sp", cwt[:, 0:d["nfm"] * 4], d["cw"][:, :], [], [t_cwt], t_cwt)
            S.dma("sp", wg[:], d["g"][:, :], [d["t_g"]], [t_wg], t_wg)
            if gdn:
                S.op("act", lambda e: e.activation(out=gp[:, 2:3], in_=gp[:, 1:2], func=AF.Exp), [t_gp], [t_gp])
                S.op("dve", lambda e: e.tensor_scalar_mul(out=gp[:, 2:3], in0=gp[:, 2:3], scalar1=-1.0), [t_gp], [t_gp])
            for h in range(NHEAD):
                S.op("pool", lambda e, h=h: e.memset(Sst[:, h, :], 0.0), [], [t_S[h]])
                S.op("pool", lambda e, h=h: e.memset(Sbf[:, h, :], 0.0), [], [t_Sbf[h]])
            for i in range(32):
                S.op("pool", lambda e, i=i: e.memset(carry[:, i, :], 0.0), [], [t_carry[i]])
            S.op("pool", lambda e: e.memset(mprev[:], 0.0), [], [t_mprev])
            if not gdn:
                for c in range(NCH):
                    for h in range(HG):
                        S.op("pool", lambda e, c=c, h=h: e.memset(vaug[c][:, h, DV:DV + 1], 1.0), [], [t_vaug[c][h]])
                        S.op("pool", lambda e, c=c, h=h: e.memset(vaug[c][:, h, DV + 1:DV + 2], 0.0), [], [t_vaug[c][h]])

            if l + 1 < NL:
                emit_casts(l + 1)
            for b in range(NB):
                src = xv if l == 0 else ov
                for c in range(NCH):
                    S.dma("sp", xs[:, c, :], src[b][:, c, :], [t_blkc[b][c]] if l > 0 else [], [t_xsc[c]], t_xsc[c])
                rms_stats(lambda c: xs[:, c, :], NCH, t_xsc, D)
                for c in range(NCH):
                    hn, t_hn = hn_p.next()
                    S.op("dve", lambda e, c=c, hn=hn: e.scalar_tensor_tensor(out=hn[:], in0=xs[:, c, :], scalar=st4[:, 8 + c:9 + c],
                                                                             in1=nwb[:], op0=ALU.mult, op1=ALU.mult),
                         [t_xsc[c], t_st4, t_nwb], [t_hn])
                    ps, t_ps = ps_p.next()
                    psb = ps[:].bitcast(BF16)
                    for kc in range(KC):
                        S.op("pe", lambda e, kc=kc, hn=hn, psb=psb: e.transpose(psb[:, kc * 128:(kc + 1) * 128], hn[:, kc * 128:(kc + 1) * 128], identb[:]),
                             [t_hn, t_identb], [t_ps], signal=(kc == KC - 1))
                    S.op("act", lambda e, c=c, psb=psb: e.activation(out=hT[:, :, c * 128:(c + 1) * 128],
                                                                     in_=psb.rearrange("p (k t) -> p k t", k=KC), func=AF.Copy),
                         [t_ps], [t_hT])

                if STOP < 2:
                    store_all(b)
                    continue
                psg = []
                for gi in range(2):
                    ps, t_ps = ps_p.next()
                    for kc in range(KC):
                        S.op("pe", lambda e, kc=kc, gi=gi, ps=ps: e.matmul(ps[0:8, :], lhsT=wg[:, kc * 16 + gi * 8:kc * 16 + gi * 8 + 8], rhs=hT[:, kc, :],
                                                                          start=(kc == 0), stop=(kc == KC - 1)),
                             [t_wg, t_hT], [t_ps], signal=(kc == KC - 1))
                    psg.append((ps, t_ps))
                R = lambda q: rw[:, q, :]
                Tm = lambda q: rtmp[:, q, :]
                if gdn:
                    (pa, t_pa), (pb, t_pb) = psg
                    S.op("act", lambda e: e.activation(out=Tm(0), in_=pa[0:8, :], func=AF.Identity, bias=gp[:, 0:1]), [t_pa, t_gp], [t_rtmp])
                    S.op("dve", lambda e: e.scalar_tensor_tensor(out=Tm(1), in0=Tm(0), scalar=-1.0, in1=Tm(0), op0=ALU.mult, op1=ALU.max), [t_rtmp], [t_rtmp])
                    S.op("act", lambda e: e.activation(out=Tm(1), in_=Tm(1), func=AF.Exp, scale=-1.0), [t_rtmp], [t_rtmp])
                    S.op("act", lambda e: e.activation(out=Tm(1), in_=Tm(1), func=AF.Ln, bias=1.0), [t_rtmp], [t_rtmp])
                    S.op("dve", lambda e: e.scalar_tensor_tensor(out=Tm(2), in0=Tm(0), scalar=0.0, in1=Tm(1), op0=ALU.max, op1=ALU.add), [t_rtmp], [t_rtmp])
                    S.op("dve", lambda e: e.tensor_scalar_mul(out=Tm(2), in0=Tm(2), scalar1=gp[:, 2:3]), [t_rtmp, t_gp], [t_rtmp])
                    S.op("act", lambda e: e.activation(out=R(2), in_=pb[0:8, :], func=AF.Sigmoid), [t_pb], [t_rw])
                    for c in range(NCH):
                        cs = slice(c * 128, (c + 1) * 128)
                        S.op("dve", lambda e, cs=cs: e.tensor_tensor_scan(out=rw[:, 3, cs], data0=ones[0:8, :], data1=rtmp[:, 2, cs], initial=0.0,
                                                                         op0=ALU.mult, op1=ALU.add), [t_rtmp, t_cst], [t_rw])
                    S.op("act", lambda e: e.activation(out=R(7), in_=R(3), func=AF.Exp), [t_rw], [t_rw])
                    S.op("dve", lambda e: e.tensor_tensor(out=R(0), in0=R(2), in1=R(7), op=ALU.mult), [t_rw], [t_rw])
                    for c in range(NCH):
                        cs = slice(c * 128, (c + 1) * 128)
                        S.op("act", lambda e, cs=cs, c=c: e.activation(out=rw[:, 1, cs], in_=rw[:, 3, cs], func=AF.Exp, scale=-1.0,
                                                                       bias=rw[:, 3, c * 128 + 127:c * 128 + 128]), [t_rw], [t_rw])
                    S.op("dve", lambda e: e.tensor_scalar_mul(out=R(4), in0=R(3), scalar1=-1.0), [t_rw], [t_rw])
                else:
                    (pa, t_pa), (pb, t_pb) = psg
                    S.op("act", lambda e: e.activation(out=Tm(0), in_=pa[0:8, :], func=AF.Identity, bias=gp[:, 0:1]), [t_pa, t_gp], [t_rtmp])
                    S.op("act", lambda e: e.activation(out=Tm(1), in_=pb[0:8, :], func=AF.Identity, bias=gp[:, 1:2]), [t_pb, t_gp], [t_rtmp])
                    S.op("dve", lambda e: e.scalar_tensor_tensor(out=Tm(2), in0=Tm(1), scalar=-1.0, in1=Tm(1), op0=ALU.mult, op1=ALU.max), [t_rtmp], [t_rtmp])
                    S.op("act", lambda e: e.activation(out=Tm(2), in_=Tm(2), func=AF.Exp, scale=-1.0), [t_rtmp], [t_rtmp])
                    S.op("act", lambda e: e.activation(out=Tm(2), in_=Tm(2), func=AF.Ln, bias=1.0), [t_rtmp], [t_rtmp])
                    S.op("dve", lambda e: e.scalar_tensor_tensor(out=Tm(3), in0=Tm(1), scalar=0.0, in1=Tm(2), op0=ALU.min, op1=ALU.subtract), [t_rtmp], [t_rtmp])
                    for c in range(NCH):
                        cs = slice(c * 128, (c + 1) * 128)
                        S.op("dve", lambda e, cs=cs: e.tensor_tensor_scan(out=rtmp[:, 2, cs], data0=ones[0:8, :], data1=rtmp[:, 3, cs], initial=0.0,
                                                                         op0=ALU.mult, op1=ALU.add), [t_rtmp, t_cst], [t_rtmp])
                    S.op("dve", lambda e: e.tensor_tensor(out=Tm(3), in0=Tm(0), in1=Tm(2), op=ALU.subtract), [t_rtmp], [t_rtmp])
                    lns = math.log(DK ** -0.5)
                    for c in range(NCH):
                        cs = slice(c * 128, (c + 1) * 128)
                        S.op("dve", lambda e, cs=cs: e.reduce_max(out=rsm[:, 0:1], in_=rtmp[:, 3, cs], axis=mybir.AxisListType.X), [t_rtmp], [t_rsm])
                        S.op("dve", lambda e: e.tensor_tensor(out=rsm[:, 1:2], in0=rsm[:, 0:1], in1=mprev[:, 0:1], op=ALU.max), [t_rsm, t_mprev], [t_rsm])
                        S.op("dve", lambda e: e.tensor_scalar(out=rsm[:, 2:3], in0=rsm[:, 1:2], scalar1=-1.0, scalar2=lns, op0=ALU.mult, op1=ALU.add), [t_rsm], [t_rsm])
                        S.op("dve", lambda e: e.tensor_tensor(out=rsm[:, 3:4], in0=mprev[:, 0:1], in1=rsm[:, 1:2], op=ALU.subtract), [t_rsm, t_mprev], [t_rsm])
                        S.op("dve", lambda e: e.tensor_scalar_mul(out=rsm[:, 4:5], in0=rsm[:, 1:2], scalar1=-1.0), [t_rsm], [t_rsm])
                        S.op("act", lambda e, cs=cs: e.activation(out=rw[:, 0, cs], in_=rtmp[:, 3, cs], func=AF.Exp, bias=rsm[:, 2:3]), [t_rtmp, t_rsm], [t_rw])
                        S.op("act", lambda e, cs=cs: e.activation(out=rw[:, 1, cs], in_=rtmp[:, 3, cs], func=AF.Exp, scale=0.0, bias=rsm[:, 3:4]), [t_rtmp, t_rsm], [t_rw])
                        S.op("act", lambda e, cs=cs: e.activation(out=rw[:, 2, cs], in_=rtmp[:, 2, cs], func=AF.Exp, scale=-1.0, bias=rsm[:, 4:5]), [t_rtmp, t_rsm], [t_rw])
                        S.op("dve", lambda e, c=c: e.tensor_tensor(out=mprev[:, 0:1], in0=rtmp[:, 2, c * 128 + 127:c * 128 + 128], in1=rsm[:, 1:2], op=ALU.add),
                             [t_rtmp, t_rsm], [t_mprev])

                for hg in range(NHG if STOP >= 3 else 0):
                    heads = list(range(hg * HG, (hg + 1) * HG))
                    fidx = {}
                    for ti, (nm, h, sub) in enumerate(fmt[hg]):
                        gti = hg * nfm_hg + ti
                        fidx[(nm, h, sub)] = ti
                        wt, g_wt = wfm_p.next()
                        S.dma("sp", wt, d["fm"][gti].rearrange("p (k c) -> p k c", k=KC), [d["t_fm"]], g_wt, g_wt[0])
                        ps, t_ps = ps_p.next()
                        for kc in range(KC):
                            S.op("pe", lambda e, kc=kc, wt=wt, ps=ps: e.matmul(ps[:, :], lhsT=wt[:, kc, :], rhs=hT[:, kc, :], start=(kc == 0), stop=(kc == KC - 1)),
                                 g_wt + [t_hT], [t_ps], signal=(kc == KC - 1))
                        pc, t_pc = pc_p.next()
                        acc, t_acc = acc_p.next()
                        S.op("act", lambda e, pc=pc, ps=ps: e.activation(out=pc[:, 3:BLK + 3], in_=ps[:, :], func=AF.Copy), [t_ps], [t_pc])
                        S.op("pool", lambda e, pc=pc, gti=gti: e.tensor_copy(out=pc[:, 0:3], in_=carry[:, gti, :]), [t_carry[gti]], [t_pc])
                        S.op("pool", lambda e, pc=pc, gti=gti: e.tensor_copy(out=carry[:, gti, :], in_=pc[:, BLK:BLK + 3]), [t_pc], [t_carry[gti]])
                        S.op("dve", lambda e, pc=pc, acc=acc, gti=gti: e.tensor_scalar_mul(out=acc[:], in0=pc[:, 3:BLK + 3], scalar1=cwt[:, gti * 4 + 3:gti * 4 + 4]),
                             [t_pc, t_cwt], [t_acc])
                        for j in range(3):
                            S.op("dve", lambda e, pc=pc, acc=acc, gti=gti, j=j: e.scalar_tensor_tensor(out=acc[:], in0=pc[:, j:BLK + j], scalar=cwt[:, gti * 4 + j:gti * 4 + j + 1],
                                                                                                     in1=acc[:], op0=ALU.mult, op1=ALU.add),
                                 [t_pc, t_cwt, t_acc], [t_acc])
                        S.op("act", lambda e, acc=acc, ti=ti: e.activation(out=fmT[ti], in_=acc[:], func=AF.Silu), [t_acc], [t_fmT[ti]])

                    ntm_hg = len(tmt[hg])
                    if STOP < 4:
                        continue
                    for ti, (nm, h0) in enumerate(tmt[hg]):
                        gti = hg * ntm_hg + ti
                        wt, g_wt = wtm_p.next()
                        for hh in range(4):
                            S.dma("sp", wt[:, hh * 2:(hh + 1) * 2, :], d["tm"][gti][:, hh * 1024:(hh + 1) * 1024].rearrange("p (k c) -> p k c", k=2),
                                  [d["t_tm"]], g_wt, g_wt[0])
                        for c in range(NCH):
                            ps, t_ps = ps_p.next()
                            for kc in range(KC):
                                S.op("pe", lambda e, kc=kc, c=c, wt=wt, ps=ps: e.matmul(ps[:, :], lhsT=hT[:, kc, c * 128:(c + 1) * 128], rhs=wt[:, kc, :],
                                                                                       start=(kc == 0), stop=(kc == KC - 1)),
                                     g_wt + [t_hT], [t_ps], signal=(kc == KC - 1))
                            zs = zg[c][:, h0 * DV:(h0 + 2) * DV]
                            tz = [t_zg[c][h0], t_zg[c][h0 + 1]]
                            if nm == "z" and gdn:
                                S.op("act", lambda e, ps=ps, zs=zs: e.activation(out=zs, in_=ps[:, :], func=AF.Silu), [t_ps], tz)
                            elif nm == "v":
                                hl = h0 - hg * HG
                                S.op("act", lambda e, ps=ps, c=c, hl=hl: e.activation(out=vaug[c][:, hl:hl + 2, 0:DV], in_=ps[:, :].rearrange("p (h d) -> p h d", h=2), func=AF.Copy),
                                     [t_ps], [t_vaug[c][hl], t_vaug[c][hl + 1]])
                            elif nm == "o":
                                S.op("act", lambda e, ps=ps, zs=zs: e.activation(out=zs, in_=ps[:, :], func=AF.Sigmoid), [t_ps], tz)
                            else:
                                acc, t_acc = acc_p.next()
                                S.op("act", lambda e, ps=ps, acc=acc: e.activation(out=acc[:], in_=ps[:, :], func=AF.Silu), [t_ps], [t_acc])
                                S.op("dve", lambda e, acc=acc, zs=zs: e.tensor_tensor(out=zs, in0=zs, in1=acc[:], op=ALU.mult), [t_acc] + tz, tz)
                            if nm == "z":
                                for hh in range(2):
                                    zh = zg[c][:, (h0 + hh) * DV:(h0 + hh + 1) * DV]
                                    S.op("pool", lambda e, zh=zh: e.tensor_tensor(out=zh, in0=zh, in1=hnwb[:], op=ALU.mult), [t_hnwb, tz[hh]], [tz[hh]])

                    if STOP < 5:
                        continue
                    if gdn:
                        for qi, nm in ((6, "k"), (5, "q")):
                            ps, t_ps = ps_p.next()
                            for hi, h in enumerate(heads):
                                ti = fidx[(nm, h, 0)]
                                S.op("act", lambda e, ti=ti: e.activation(out=junk32[:], in_=fmT[ti], func=AF.Square), [t_fmT[ti]], [t_junk32])
                                S.op("pe", lambda e, h=h, ps=ps: e.matmul(ps[0:8, :], lhsT=onescol(h), rhs=junk32[:], start=(hi == 0), stop=(hi == HG - 1)),
                                     [t_cst, t_junk32], [t_ps])
                            S.op("act", lambda e, ps=ps: e.activation(out=Tm(4), in_=ps[0:8, :], func=AF.Sqrt, bias=1e-6), [t_ps], [t_rtmp])
                            S.op("dve", lambda e, qi=qi: e.reciprocal(out=R(qi), in_=Tm(4)), [t_rtmp], [t_rw])
                        S.op("dve", lambda e: e.tensor_scalar_mul(out=R(5), in0=R(5), scalar1=DK ** -0.5), [t_rw], [t_rw])
                        for hi, h in enumerate(heads):
                            ps, t_ps = ps_p.next()
                            S.op("pe", lambda e, h=h, ps=ps: e.matmul(ps[:, :], lhsT=sel(h), rhs=R(6), start=True, stop=True), [t_cst, t_rw], [t_ps])
                            tk = fidx[("k", h, 0)]
                            S.op("dve", lambda e, hi=hi, tk=tk, ps=ps: e.tensor_tensor(out=khT[hi][:], in0=fmT[tk], in1=ps[:, :], op=ALU.mult),
                                 [t_fmT[tk], t_ps], [t_khT[hi]])
                            ps, t_ps = ps_p.next()
                            S.op("pe", lambda e, h=h, ps=ps: e.matmul(ps[:, :], lhsT=sel(h), rhs=R(7), start=True, stop=True), [t_cst, t_rw], [t_ps])
                            tq = fidx[("q", h, 0)]
                            S.op("dve", lambda e, hi=hi, tq=tq, ps=ps: e.tensor_tensor(out=qgT[hi][:], in0=fmT[tq], in1=ps[:, :], op=ALU.mult),
                                 [t_fmT[tq], t_ps], [t_qgT[hi]])
                    nq = 6 if gdn else 3
                    for c in range(NCH):
                        ps, t_ps = ps_p.next()
                        for q in range(nq):
                            S.op("pe", lambda e, q=q, c=c, ps=ps: e.transpose(ps[:, q * 8:(q + 1) * 8], rw[:, q, c * 128:(c + 1) * 128], cst[0:8, C_ID:C_ID + 8]),
                                 [t_rw, t_cst], [t_ps], signal=(q == nq - 1))
                        S.op("act", lambda e, c=c, ps=ps: e.activation(out=tmsc[c][:, 0:nq, :], in_=ps[:, 0:nq * 8].rearrange("p (q h) -> p q h", q=nq), func=AF.Copy),
                             [t_ps], [t_tmsc[c]])

                    for c in range(NCH if STOP >= 6 else 0):
                        cs = slice(c * 128, (c + 1) * 128)
                        tsc = t_tmsc[c]

                        def scf(q, h, c=c):
                            return tmsc[c][:, q, h:h + 1]

                        HX = [dict(h=h, hi=hi) for hi, h in enumerate(heads)]
                        if gdn:
                            for X in HX:
                                h, hi = X["h"], X["hi"]
                                psL, t_psL = ps_p.next()
                                S.op("pe", lambda e: e.matmul(psL[:, 0:128], lhsT=sel(h), rhs=rw[:, 3, cs], start=True, stop=False), [t_cst, t_rw], [t_psL], signal=False)
                                S.op("pe", lambda e: e.matmul(psL[:, 0:128], lhsT=ident, rhs=cst[:, C_PM:C_PM + 128], start=False, stop=True), [t_cst], [t_psL], signal=False)
                                S.op("pe", lambda e: e.matmul(psL[:, 128:256], lhsT=sel(h), rhs=rw[:, 3, cs], start=True, stop=False), [t_cst, t_rw], [t_psL], signal=False)
                                S.op("pe", lambda e: e.matmul(psL[:, 128:256], lhsT=ident, rhs=cst[:, C_NM:C_NM + 128], start=False, stop=True), [t_cst], [t_psL])
                                Ls, t_Ls = L_p.next()
                                LT, t_LT = L_p.next()
                                dec, t_dec = sm_p.next()
                                S.op("act", lambda e: e.activation(out=Ls[:], in_=psL[:, 0:128], func=AF.Exp, scale=-1.0, bias=scf(3, h)), [t_psL, tsc], [t_Ls])
                                S.op("act", lambda e: e.activation(out=LT[:], in_=psL[:, 128:256], func=AF.Exp, bias=scf(4, h)), [t_psL, tsc], [t_LT])
                                S.op("act", lambda e: e.activation(out=dec[:, 0:1], in_=psL[:, 255:256], func=AF.Exp), [t_psL], [t_dec])
                                X.update(Ls=Ls, t_Ls=t_Ls, LT=LT, t_LT=t_LT, dec=dec, t_dec=t_dec)
                            for X in HX:
                                h, hi = X["h"], X["hi"]
                                tq = fidx[("q", h, 0)]
                                psk, t_psk = ps_p.next()
                                S.op("pe", lambda e: e.matmul(psk[:, 0:128], lhsT=khT[hi][:, cs], rhs=khT[hi][:, cs], start=True, stop=True), [t_khT[hi]], [t_psk], signal=False)
                                S.op("pe", lambda e: e.matmul(psk[:, 128:256], lhsT=khT[hi][:, cs], rhs=fmT[tq][:, cs], start=True, stop=True), [t_khT[hi], t_fmT[tq]], [t_psk])
                                A, t_A = A_p.next()
                                Ls, t_Ls, LT, t_LT = X["Ls"], X["t_Ls"], X["LT"], X["t_LT"]
                                S.op("dve", lambda e: e.scalar_tensor_tensor(out=A[:], in0=psk[:, 0:128], scalar=scf(2, h), in1=Ls[:], op0=ALU.mult, op1=ALU.mult),
                                     [t_psk, tsc, t_Ls], [t_A])
                                S.op("pool", lambda e: e.tensor_tensor(out=A[:], in0=A[:], in1=ident, op=ALU.add), [t_A, t_cst], [t_A])
                                attnT, t_attnT = mbf_p.next()
                                S.op("dve", lambda e: e.tensor_tensor(out=attnT[:], in0=psk[:, 128:256], in1=LT[:], op=ALU.mult), [t_psk, t_LT], [t_attnT])
                                X.update(A=A, t_A=t_A, attnT=attnT, t_attnT=t_attnT, Y=None, t_Y=t_cst, YT=None, t_YT=t_cst)
                            for li, s2 in enumerate(LVLS):
                                lastl = li == len(LVLS) - 1
                                for X in HX:
                                    psP, t_psP = ps_p.next()
                                    Yap = ident if X["Y"] is None else X["Y"][:]
                                    A, t_A = X["A"], X["t_A"]
                                    S.op("pe", lambda e: e.matmul(psP[:, 0:128], lhsT=A[:], rhs=Yap, start=True, stop=True), [t_A, X["t_Y"]], [t_psP])
                                    P, t_P = P_p.next()
                                    if lastl:
                                        S.op("act", lambda e: e.activation(out=P[:], in_=psP[:, 0:128], func=AF.Copy), [t_psP], [t_P])
                                    else:
                                        mk = cst[:, C_LV + li * 128:C_LV + (li + 1) * 128]
                                        S.op("dve", lambda e: e.tensor_tensor(out=P[:], in0=psP[:, 0:128], in1=mk, op=ALU.mult), [t_psP, t_cst], [t_P])
                                    X.update(psP=psP, t_psP=t_psP, P=P, t_P=t_P, Yap=Yap)
                                for X in HX:
                                    psP, t_psP, P, t_P, Yap = X["psP"], X["t_psP"], X["P"], X["t_P"], X["Yap"]
                                    YTap = ident if X["YT"] is None else X["YT"][:]
                                    S.op("pe", lambda e: e.matmul(psP[:, 128:256], lhsT=YTap, rhs=P[:], start=True, stop=True), [X["t_YT"], t_P], [t_psP])
                                    if lastl:
                                        TT, t_TT = mbf_p.next()
                                        S.op("dve", lambda e: e.scalar_tensor_tensor(out=TT[:], in0=Yap, scalar=2.0, in1=psP[:, 128:256], op0=ALU.mult, op1=ALU.subtract),
                                             [X["t_Y"], t_psP], [t_TT])
                                        X.update(TT=TT, t_TT=t_TT)
                                    else:
                                        Yn, t_Yn = Y_p.next()
                                        S.op("dve", lambda e: e.scalar_tensor_tensor(out=Yn[:], in0=Yap, scalar=2.0, in1=psP[:, 128:256], op0=ALU.mult, op1=ALU.subtract),
                                             [X["t_Y"], t_psP], [t_Yn])
                                        X.update(Y=Yn, t_Y=t_Yn)
                                if not lastl:
                                    for X in HX:
                                        psP, t_psP = X["psP"], X["t_psP"]
                                        Yn, t_Yn = X["Y"], X["t_Y"]
                                        S.op("pe", lambda e: e.transpose(psP[:, 256:384], Yn[:], ident), [t_Yn, t_cst], [t_psP])
                                        YTn, t_YTn = YT_p.next()
                                        S.op("act", lambda e: e.activation(out=YTn[:], in_=psP[:, 256:384], func=AF.Copy), [t_psP], [t_YTn])
                                        X.update(YT=YTn, t_YT=t_YTn)
                            for X in HX:
                                h, hi = X["h"], X["hi"]
                                tv0 = fidx[("v", h, 0)]
                                tv1 = fidx[("v", h, 1)]
                                psK, t_psK = ps_p.next()
                                psKb = psK[:].bitcast(BF16)
                                S.op("pe", lambda e: e.transpose(psKb[:, 0:128], khT[hi][:, cs], identb[:]), [t_khT[hi], t_identb], [t_psK], signal=False)
                                S.op("pe", lambda e: e.transpose(psKb[:, 128:256], fmT[tv0][:, cs], identb[:]), [t_fmT[tv0], t_identb], [t_psK], signal=False)
                                S.op("pe", lambda e: e.transpose(psKb[:, 256:384], fmT[tv1][:, cs], identb[:]), [t_fmT[tv1], t_identb], [t_psK])
                                bgK, t_bgK = mbf_p.next()
                                kd, t_kd = mbf_p.next()
                                bV, t_bV = wv_p.next()
                                S.op("act", lambda e: e.activation(out=bgK[:], in_=psKb[:, 0:128], func=AF.Copy, scale=scf(0, h)), [t_psK, tsc], [t_bgK])
                                S.op("act", lambda e: e.activation(out=kd[:], in_=psKb[:, 0:128], func=AF.Copy, scale=scf(1, h)), [t_psK, tsc], [t_kd])
                                S.op("act", lambda e: e.activation(out=bV[:, 0:DV], in_=psKb[:, 128:384], func=AF.Copy, scale=scf(2, h)), [t_psK, tsc], [t_bV])
                                X.update(bgK=bgK, t_bgK=t_bgK, kd=kd, t_kd=t_kd, bV=bV, t_bV=t_bV)
                            for X in HX:
                                psW, t_psW = ps_p.next()
                                bgK, t_bgK, TT, t_TT = X["bgK"], X["t_bgK"], X["TT"], X["t_TT"]
                                S.op("pe", lambda e: e.matmul(psW[:, 0:128], lhsT=bgK[:], rhs=TT[:], start=True, stop=True), [t_bgK, t_TT], [t_psW])
                                nwk, t_nwk = mbf_p.next()
                                S.op("act", lambda e: e.activation(out=nwk[:], in_=psW[:, 0:128], func=AF.Copy, scale=-1.0), [t_psW], [t_nwk])
                                X.update(psW=psW, t_psW=t_psW, nwk=nwk, t_nwk=t_nwk)
                            for X in HX:
                                h = X["h"]
                                psW, t_psW, TT, t_TT, bV, t_bV, nwk, t_nwk = X["psW"], X["t_psW"], X["TT"], X["t_TT"], X["bV"], X["t_bV"], X["nwk"], X["t_nwk"]
                                S.op("pe", lambda e: e.matmul(psW[:, 256:512], lhsT=TT[:], rhs=bV[:, 0:DV], start=True, stop=False), [t_TT, t_bV], [t_psW], signal=False)
                                S.op("pe", lambda e: e.matmul(psW[:, 256:512], lhsT=nwk[:], rhs=Sbf[:, h, 0:DV], start=False, stop=True), [t_nwk, t_Sbf[h]], [t_psW])
                                wv, t_wv = wv_p.next()
                                S.op("act", lambda e: e.activation(out=wv[:, 0:DV], in_=psW[:, 256:512], func=AF.Copy), [t_psW], [t_wv])
                                X.update(wv=wv, t_wv=t_wv)
                            for X in HX:
                                h, hi = X["h"], X["hi"]
                                og_h = zg[c][:, h * DV:(h + 1) * DV]
                                attnT, t_attnT, wv, t_wv, kd, t_kd, dec, t_dec = X["attnT"], X["t_attnT"], X["wv"], X["t_wv"], X["kd"], X["t_kd"], X["dec"], X["t_dec"]
                                psO, t_psO = ps_p.next()
                                S.op("pe", lambda e: e.matmul(psO[:, 0:DV], lhsT=qgT[hi][:, cs], rhs=Sbf[:, h, 0:DV], start=True, stop=False), [t_qgT[hi], t_Sbf[h]], [t_psO], signal=False)
                                S.op("pe", lambda e: e.matmul(psO[:, 0:DV], lhsT=attnT[:], rhs=wv[:, 0:DV], start=False, stop=True), [t_attnT, t_wv], [t_psO], signal=False)
                                S.op("pe", lambda e: e.matmul(psO[:, 256:512], lhsT=kd[:], rhs=wv[:, 0:DV], start=True, stop=True), [t_kd, t_wv], [t_psO])
                                acc, t_acc = acc_p.next()
                                S.op("act", lambda e: e.activation(out=acc[:, 0:DV], in_=psO[:, 256:512], func=AF.Copy), [t_psO], [t_acc])
                                S.op("dve", lambda e: e.scalar_tensor_tensor(out=Sst[:, h, 0:DV], in0=Sst[:, h, 0:DV], scalar=dec[:, 0:1], in1=acc[:, 0:DV], op0=ALU.mult, op1=ALU.add),
                                     [t_S[h], t_dec, t_acc], [t_S[h]])
                                S.op("pool", lambda e: e.tensor_copy(out=Sbf[:, h, 0:DV], in_=Sst[:, h, 0:DV]), [t_S[h]], [t_Sbf[h]])
                                sm, t_sm = sm_p.next()
                                S.op("act", lambda e: e.activation(out=junk[:, 0:DV], in_=psO[:, 0:DV], func=AF.Square, scale=scf(5, h), accum_out=sm[:, 0:1]),
                                     [t_psO, tsc, t_sm], [t_junk, t_sm])
                                S.op("act", lambda e: e.activation(out=sm[:, 1:2], in_=sm[:, 0:1], func=AF.Sqrt, scale=1.0 / DV, bias=EPS), [t_sm], [t_sm])
                                S.op("dve", lambda e: e.reciprocal(out=sm[:, 2:3], in_=sm[:, 1:2]), [t_sm], [t_sm])
                                S.op("dve", lambda e: e.tensor_tensor(out=sm[:, 3:4], in0=sm[:, 2:3], in1=scf(5, h), op=ALU.mult), [t_sm, tsc], [t_sm])
                                S.op("dve", lambda e: e.scalar_tensor_tensor(out=og_h, in0=psO[:, 0:DV], scalar=sm[:, 3:4], in1=og_h, op0=ALU.mult, op1=ALU.mult),
                                     [t_psO, t_sm, t_zg[c][h]], [t_zg[c][h]])
                        else:
                            for X in HX:
                                h, hi = X["h"], X["hi"]
                                tq = fidx[("q", h, 0)]
                                tk = fidx[("k", h, 0)]
                                psA, t_psA = ps_p.next()
                                psT, t_psT = ps_p.next()
                                psAb = psT[:].bitcast(BF16)
                                S.op("pe", lambda e: e.matmul(psA[:, 0:128], lhsT=fmT[tk][:, cs], rhs=fmT[tq][:, cs], start=True, stop=True), [t_fmT[tk], t_fmT[tq]], [t_psA])
                                S.op("pe", lambda e: e.transpose(psAb[:, 0:128], fmT[tk][:, cs], identb[:]), [t_fmT[tk], t_identb], [t_psT])
                                AT, t_AT = mbf_p.next()
                                S.op("dve", lambda e: e.scalar_tensor_tensor(out=AT[:], in0=psA[:, 0:128], scalar=scf(0, h), in1=cst[:, C_MT:C_MT + 128], op0=ALU.mult, op1=ALU.mult),
                                     [t_psA, tsc, t_cst], [t_AT])
                                kw, t_kw = mbf_p.next()
                                S.op("act", lambda e: e.activation(out=kw[:], in_=psAb[:, 0:128], func=AF.Copy, scale=scf(0, h)), [t_psT, tsc], [t_kw])
                                S.op("dve", lambda e: e.tensor_scalar_mul(out=Sst[:, h, :], in0=Sst[:, h, :], scalar1=scf(1, h)), [t_S[h], tsc], [t_S[h]])
                                S.op("pool", lambda e: e.tensor_copy(out=Sbf[:, h, :], in_=Sst[:, h, :]), [t_S[h]], [t_Sbf[h]])
                                X.update(AT=AT, t_AT=t_AT, kw=kw, t_kw=t_kw, tq=tq)
                            for X in HX:
                                h, hi = X["h"], X["hi"]
                                og_h = zg[c][:, h * DV:(h + 1) * DV]
                                AT, t_AT, kw, t_kw, tq = X["AT"], X["t_AT"], X["kw"], X["t_kw"], X["tq"]
                                psO, t_psO = ps_p.next()
                                S.op("pe", lambda e: e.matmul(psO[:, 0:DVA], lhsT=fmT[tq][:, cs], rhs=Sbf[:, h, :], start=True, stop=False), [t_fmT[tq], t_Sbf[h]], [t_psO], signal=False)
                                S.op("pe", lambda e: e.matmul(psO[:, 0:DVA], lhsT=AT[:], rhs=vaug[c][:, hi, :], start=False, stop=True), [t_AT, t_vaug[c][hi]], [t_psO])
                                psS, t_psS = ps_p.next()
                                S.op("pe", lambda e: e.matmul(psS[:, 0:DVA], lhsT=kw[:], rhs=vaug[c][:, hi, :], start=True, stop=True), [t_kw, t_vaug[c][hi]], [t_psS])
                                acc, t_acc = acc_p.next()
                                S.op("act", lambda e: e.activation(out=acc[:, 0:DVA], in_=psS[:, 0:DVA], func=AF.Copy), [t_psS], [t_acc])
                                S.op("dve", lambda e: e.tensor_tensor(out=Sst[:, h, :], in0=Sst[:, h, :], in1=acc[:, 0:DVA], op=ALU.add), [t_S[h], t_acc], [t_S[h]])
                                sm, t_sm = sm_p.next()
                                S.op("act", lambda e: e.activation(out=sm[:, 4:5], in_=psO[:, DV:DV + 1], func=AF.Abs), [t_psO], [t_sm])
                                S.op("dve", lambda e: e.tensor_tensor(out=sm[:, 4:5], in0=sm[:, 4:5], in1=scf(2, h), op=ALU.max), [t_sm, tsc], [t_sm])
                                S.op("dve", lambda e: e.reciprocal(out=sm[:, 5:6], in_=sm[:, 4:5]), [t_sm], [t_sm])
                                S.op("act", lambda e: e.activation(out=junk[:, 0:DV], in_=psO[:, 0:DV], func=AF.Square, scale=sm[:, 5:6], accum_out=sm[:, 0:1]),
                                     [t_psO, t_sm], [t_junk, t_sm])
                                S.op("act", lambda e: e.activation(out=sm[:, 1:2], in_=sm[:, 0:1], func=AF.Sqrt, scale=1.0 / DV, bias=EPS), [t_sm], [t_sm])
                                S.op("dve", lambda e: e.reciprocal(out=sm[:, 2:3], in_=sm[:, 1:2]), [t_sm], [t_sm])
                                S.op("dve", lambda e: e.tensor_tensor(out=sm[:, 3:4], in0=sm[:, 2:3], in1=sm[:, 5:6], op=ALU.mult), [t_sm], [t_sm])
                                S.op("dve", lambda e: e.scalar_tensor_tensor(out=og_h, in0=psO[:, 0:DV], scalar=sm[:, 3:4], in1=og_h, op0=ALU.mult, op1=ALU.mult),
                                     [t_psO, t_sm, t_zg[c][h]], [t_zg[c][h]])

                if STOP < 7:
                    store_all(b)
                    continue
                wov = d["o"].rearrange("p (k n) -> p k n", k=16)
                for n in range(2):
                    wo, g_wo = wo_h[n]
                    for q4 in range(4):
                        S.dma("sp", wo[:, q4 * 4:(q4 + 1) * 4, :], wov[:, q4 * 4:(q4 + 1) * 4, n * 512:(n + 1) * 512], [d["t_o"]], g_wo, g_wo[0])
                for c in range(NCH):
                    ogT = fmA[:, 4 * c:4 * c + 4, :].rearrange("p a (b t) -> p (a b) t", t=128)
                    for half in range(2):
                        ps, t_ps = ps_p.next()
                        psb = ps[:].bitcast(BF16)
                        for k8 in range(8):
                            kc = half * 8 + k8
                            S.op("pe", lambda e, psb=psb, k8=k8, kc=kc, c=c: e.transpose(psb[:, k8 * 128:(k8 + 1) * 128], zg[c][:, kc * 128:(kc + 1) * 128], identb[:]),
                                 [t_zg[c][kc // 2], t_identb], [t_ps], signal=(k8 == 7))
                        S.op("act", lambda e, psb=psb, ogT=ogT, half=half: e.activation(out=ogT[:, half * 8:(half + 1) * 8, :], in_=psb.rearrange("p (k t) -> p k t", k=8), func=AF.Copy),
                             [t_ps], [t_fmT[4 * c + 2 * half], t_fmT[4 * c + 2 * half + 1]])
                for n in range(2):
                    wo, g_wo = wo_h[n]
                    for c in range(NCH):
                        ogT = fmA[:, 4 * c:4 * c + 4, :].rearrange("p a (b t) -> p (a b) t", t=128)
                        ps, t_ps = ps_p.next()
                        for kc in range(16):
                            S.op("pe", lambda e, ps=ps, ogT=ogT, kc=kc: e.matmul(ps[:, :], lhsT=ogT[:, kc, :], rhs=wo[:, kc, :], start=(kc == 0), stop=(kc == 15)),
                                 [t_fmT[4 * c + kc // 4]] + g_wo, [t_ps], signal=(kc == 15))
                        S.op("dve", lambda e, ps=ps, c=c, n=n: e.tensor_tensor(out=xs[:, c, n * 512:(n + 1) * 512], in0=ps[:, :], in1=xs[:, c, n * 512:(n + 1) * 512], op=ALU.add),
                             [t_xsc[c], t_ps], [t_xsc[c]])
                        if n == 1 and not (last and final_norm):
                            S.dma("sp", ov[b][:, c, :], xs[:, c, :], [t_xsc[c]], [t_blkc[b][c]], t_blkc[b][c])
                if last and final_norm:
                    rms_stats(lambda c: xs[:, c, :], NCH, t_xsc, D)
                    for c in range(NCH):
                        S.op("dve", lambda e, c=c: e.scalar_tensor_tensor(out=xs[:, c, :], in0=xs[:, c, :], scalar=st4[:, 8 + c:9 + c], in1=fwb[:], op0=ALU.mult, op1=ALU.mult),
                             [t_xsc[c], t_st4, t_fwb], [t_xsc[c]])
                        S.dma("sp", ov[b][:, c, :], xs[:, c, :], [t_xsc[c]], [t_blkc[b][c]], t_blkc[b][c])

        e = S.eng["sp"]
        for b in range(NB):
            for c in range(NCH):
                S._wait(e, [t_blkc[b][c].w])
        print("semaphores used", S.nsem)
    return nc


def prep_inputs(T, kinds, x_b, p):
    NL = len(kinds)
    m = {"x": np.ascontiguousarray(x_b.reshape(T, D)), "cst": make_consts()}
    nw = np.concatenate([p["norm_w"][:NL], p["final_norm_w"][None, :]], 0).astype(np.float32)
    m["nw"] = np.ascontiguousarray(nw)
    hnw = np.zeros((NL, DV), np.float32)
    gp = np.zeros((NL, 8, 2), np.float32)
    for l, kind in enumerate(kinds):
        j = l // 2
        if kind == "gdn":
            hnw[l] = p["gdn_norm_w"][j]
            gp[l, :, 0] = p["gdn_dt_bias"][j]
            gp[l, :, 1] = p["gdn_a_log"][j]
            fm, tm, g, wo, cw = layout_weights(kind, p["gdn_w_in"][j], p["gdn_conv_w"][j], p["gdn_w_out"][j])
        else:
            hnw[l] = p["mlstm_norm_w"][j]
            gp[l, :, 0] = p["mlstm_i_bias"][j]
            gp[l, :, 1] = p["mlstm_f_bias"][j]
            fm, tm, g, wo, cw = layout_weights(kind, p["mlstm_w_in"][j], p["mlstm_conv_w"][j], p["mlstm_w_out"][j])
        m[f"wfm{l}"], m[f"wtm{l}"], m[f"wg{l}"], m[f"wo{l}"], m[f"cw{l}"] = fm, tm, g, wo, cw
    m["hnw"] = hnw
    m["gp"] = gp
    return m


def run(x, p, kinds, final_norm=True, n_cores=8):
    B, T, _ = x.shape
    nc = build(T, kinds, final_norm)
    base = prep_inputs(T, kinds, x[0], p)
    in_maps = []
    for c in range(n_cores):
        m = dict(base)
        m["x"] = np.ascontiguousarray(x[c % B].reshape(T, D))
        in_maps.append(m)
    res = run_bass_kernel_spmd(nc, in_maps, core_ids=list(range(n_cores)))
    return np.stack([res.results[b]["out"].reshape(T, D) for b in range(B)], 0)


def kernel(**inputs):
    p = {k: np.asarray(v, dtype=np.float32) for k, v in inputs.items()}
    x = p.pop("x")
    kinds = [layer_kind(l) for l in range(4)]
    return run(x, p, kinds).astype(np.float32)
```

```python
import math
from contextlib import ExitStack
import numpy as np
import concourse.bass as bass
import concourse.mybir as mybir
from concourse.bass_utils import run_bass_kernel_spmd

F32 = mybir.dt.float32
BF16 = mybir.dt.bfloat16
F32R = mybir.dt.float32r
AF = mybir.ActivationFunctionType
ALU = mybir.AluOpType

D = 1024
NHEAD = 8
DK = 128
DV = 256
DINNER = 2048
DVA = 258
BLK = 512
NCH = BLK // 128
KC = D // 128
HG = 4
NHG = NHEAD // HG
EPS = 1e-6
LVLS = [2, 4, 8, 16, 32, 64, 128]
import os
STOP = int(os.environ.get('KSTOP', '99'))
SUB = int(os.environ.get('KSUB', '99'))
KX = int(os.environ.get('KX', '0'))

C_ID = 0
C_PM = 128
C_NM = 256
C_MT = 384
C_LV = 512
C_OC = C_LV + 6 * 128
C_SEL = C_OC + 64
C_ONE = C_SEL + 8 * 128
NCST = C_ONE + 128


def make_consts():
    c = np.zeros((128, NCST), np.float32)
    i = np.arange(128)[:, None]
    j = np.arange(128)[None, :]
    c[:, C_ID:C_ID + 128] = (i == j)
    c[:, C_PM:C_PM + 128] = np.where(i > j, 0.0, 1e30)
    c[:, C_NM:C_NM + 128] = np.where(j >= i, 0.0, -1e30)
    c[:, C_MT:C_MT + 128] = (j >= i)
    for li, s in enumerate(LVLS[:-1]):
        c[:, C_LV + li * 128:C_LV + (li + 1) * 128] = ((i // s) == (j // s))
    oc = np.zeros((128, 8, 8), np.float32)
    for h in range(8):
        oc[:, h, h] = 1.0
    c[:, C_OC:C_OC + 64] = oc.reshape(128, 64)
    sel = np.zeros((128, 8, 128), np.float32)
    for h in range(8):
        sel[h, h, :] = 1.0
    c[:, C_SEL:C_SEL + 1024] = sel.reshape(128, 1024)
    c[:, C_ONE:C_ONE + 128] = 1.0
    return c


class Trk:
    __slots__ = ("name", "w", "r", "sem", "semv")

    def __init__(self, name):
        self.name = name
        self.w = None
        self.r = []
        self.sem = None
        self.semv = 0


class Eng:
    def __init__(self, h, sem, name):
        self.h = h
        self.sem = sem
        self.name = name
        self.cnt = 0
        self.waited = {}


class Sched:
    def __init__(self, nc, es):
        self.nc = nc
        self.es = es
        self.eng = {}
        for nm, h in (("pe", nc.tensor), ("act", nc.scalar), ("dve", nc.vector), ("pool", nc.gpsimd), ("sp", nc.sync)):
            sem = es.enter_context(nc.semaphore("s_" + nm))
            self.eng[nm] = Eng(h, sem, nm)
        self.nsem = 5

    def _wait(self, e, evs):
        need = {}
        for ev in evs:
            if ev is None:
                continue
            sem, val = ev
            k = id(sem)
            if k not in need or need[k][1] < val:
                need[k] = (sem, val)
        for k, (sem, val) in need.items():
            if sem is e.sem and e.name == "pe":
                continue
            if e.waited.get(k, 0) >= val:
                continue
            e.h.wait_ge(sem, val)
            e.waited[k] = val

    def _deps(self, reads, writes):
        evs = []
        for t in reads:
            evs.append(t.w)
        for t in writes:
            evs.append(t.w)
            evs.extend(t.r)
        return evs

    def op(self, en, fn, reads=(), writes=(), signal=True):
        e = self.eng[en]
        self._wait(e, self._deps(reads, writes))
        ins = fn(e.h)
        if signal:
            e.cnt += 1
            ins.then_inc(e.sem, 1)
            ev = (e.sem, e.cnt)
        else:
            ev = (e.sem, e.cnt + 1)
        for t in reads:
            t.r.append(ev)
        for t in writes:
            t.w = ev
            t.r = []
        return ins

    def dma(self, en, out, in_, reads, writes, owner):
        e = self.eng[en]
        if owner.sem is None:
            owner.sem = self.es.enter_context(self.nc.semaphore("d_" + owner.name))
            self.nsem += 1
        self._wait(e, self._deps(reads, writes))
        owner.semv += 16
        e.h.dma_start(out=out, in_=in_).then_inc(owner.sem, 16)
        ev = (owner.sem, owner.semv)
        for t in reads:
            t.r.append(ev)
        for t in writes:
            t.w = ev
            t.r = []
        return ev


class Pool:
    def __init__(self, nc, es, name, shape, dtype, n, psum=False):
        self.items = []
        for i in range(n):
            if psum:
                t = es.enter_context(nc.psum_tensor(f"pp_{name}{i}", shape, dtype))
            else:
                t = es.enter_context(nc.sbuf_tensor(f"pl_{name}{i}", shape, dtype))
            self.items.append((t, Trk(f"{name}{i}")))
        self.i = 0

    def next(self):
        it = self.items[self.i % len(self.items)]
        self.i += 1
        return it


def layer_kind(l):
    return "gdn" if l % 2 == 0 else "mlstm"


def fm_tiles(kind):
    out = []
    for hg in range(NHG):
        tl = []
        for h in range(hg * HG, (hg + 1) * HG):
            tl.append(("q", h, 0))
        for h in range(hg * HG, (hg + 1) * HG):
            tl.append(("k", h, 0))
        if kind == "gdn":
            for h in range(hg * HG, (hg + 1) * HG):
                tl.append(("v", h, 0))
                tl.append(("v", h, 1))
        out.append(tl)
    return out


def tm_tiles(kind):
    out = []
    for hg in range(NHG):
        tl = []
        names = ["z"] if kind == "gdn" else ["v", "o", "z"]
        for nm in names:
            for t in range(HG // 2):
                tl.append((nm, hg * HG + 2 * t))
        out.append(tl)
    return out


def col_offsets(kind):
    if kind == "gdn":
        return {"q": 0, "k": 1024, "v": 2048, "z": 4096, "g0": 6144, "g1": 6152}
    return {"q": 0, "k": 1024, "v": 2048, "o": 4096, "z": 6144, "g0": 8192, "g1": 8200}


def layout_weights(kind, w_in, conv_w, w_out):
    co = col_offsets(kind)
    wk = w_in.reshape(KC, 128, -1)
    fm = []
    cws = []
    convoff = {"q": 0, "k": 1024, "v": 2048}
    for tl in fm_tiles(kind):
        for (nm, h, sub) in tl:
            width = 128 if nm in ("q", "k") else 256
            c0 = co[nm] + h * width + sub * 128
            fm.append(np.ascontiguousarray(wk[:, :, c0:c0 + 128].transpose(1, 0, 2)).reshape(128, KC * 128))
            cc = convoff[nm] + h * width + sub * 128
            cws.append(conv_w[:, cc:cc + 128].T)
    tm = []
    for tl in tm_tiles(kind):
        for (nm, h0) in tl:
            c0 = co[nm] + h0 * 256
            tm.append(np.ascontiguousarray(wk[:, :, c0:c0 + 512].transpose(1, 0, 2)).reshape(128, KC * 512))
    g = np.concatenate([wk[:, :, co["g0"]:co["g0"] + 8], wk[:, :, co["g1"]:co["g1"] + 8]], axis=2)
    g = np.ascontiguousarray(g.transpose(1, 0, 2)).reshape(128, KC * 16)
    wo = np.ascontiguousarray(w_out.reshape(16, 128, D).transpose(1, 0, 2)).reshape(128, 16 * D)
    cw = np.ascontiguousarray(np.stack(cws, axis=1)).reshape(128, -1)
    return (np.ascontiguousarray(np.stack(fm, 0)), np.ascontiguousarray(np.stack(tm, 0)),
            np.ascontiguousarray(g), wo, cw.astype(np.float32))


def build(T, kinds, final_norm=True):
    nc = bass.Bass("TRN2", target_bir_lowering=False)
    NL = len(kinds)
    NB = T // BLK
    x_d = nc.dram_tensor("x", [T, D], F32, kind="ExternalInput").ap()
    out_d = nc.dram_tensor("out", [T, D], F32, kind="ExternalOutput").ap()
    cst_d = nc.dram_tensor("cst", [128, NCST], F32, kind="ExternalInput").ap()
    nw_d = nc.dram_tensor("nw", [NL + 1, D], F32, kind="ExternalInput").ap()
    hnw_d = nc.dram_tensor("hnw", [NL, DV], F32, kind="ExternalInput").ap()
    gp_d = nc.dram_tensor("gp", [NL, 8, 2], F32, kind="ExternalInput").ap()
    wd = []
    for l, kind in enumerate(kinds):
        nfm = sum(len(t) for t in fm_tiles(kind))
        ntm = sum(len(t) for t in tm_tiles(kind))
        d = {}
        d["nfm"], d["ntm"] = nfm, ntm
        d["fm32"] = nc.dram_tensor(f"wfm{l}", [nfm, 128, KC * 128], F32, kind="ExternalInput").ap()
        d["tm32"] = nc.dram_tensor(f"wtm{l}", [ntm, 128, KC * 512], F32, kind="ExternalInput").ap()
        d["g32"] = nc.dram_tensor(f"wg{l}", [128, KC * 16], F32, kind="ExternalInput").ap()
        d["o32"] = nc.dram_tensor(f"wo{l}", [128, 16 * D], F32, kind="ExternalInput").ap()
        d["cw"] = nc.dram_tensor(f"cw{l}", [128, nfm * 4], F32, kind="ExternalInput").ap()
        d["fm"] = nc.dram_tensor(f"sfm{l}", [nfm, 128, KC * 128], BF16, kind="Internal").ap()
        d["tm"] = nc.dram_tensor(f"stm{l}", [ntm, 128, KC * 512], BF16, kind="Internal").ap()
        d["g"] = nc.dram_tensor(f"sg{l}", [128, KC * 16], BF16, kind="Internal").ap()
        d["o"] = nc.dram_tensor(f"so{l}", [128, 16 * D], BF16, kind="Internal").ap()
        for k in ("fm", "tm", "g", "o"):
            d["t_" + k] = Trk(f"c{k}{l}")
        wd.append(d)

    with ExitStack() as es:
        E = es.enter_context
        S = Sched(nc, es)

        def sb(name, shape, dt):
            return E(nc.sbuf_tensor("sb_" + name, shape, dt))

        cst = sb("cst", [128, NCST], F32); t_cst = Trk("cst")
        identb = sb("identb", [128, 128], BF16); t_identb = Trk("identb")
        identr = sb("identr", [128, 128], F32R); t_identr = Trk("identr")
        nwb = sb("nwb", [128, D], F32); t_nwb = Trk("nwb")
        fwb = sb("fwb", [128, D], F32); t_fwb = Trk("fwb")
        hnwb = sb("hnwb", [128, DV], F32); t_hnwb = Trk("hnwb")
        gp = sb("gp", [8, 4], F32); t_gp = Trk("gp")
        cwt = sb("cwt", [128, 32 * 4], F32); t_cwt = Trk("cwt")
        wg = sb("wg", [128, KC * 16], BF16); t_wg = Trk("wg")
        xs = sb("xs", [128, NCH, D], F32); t_xsc = [Trk(f"xs{c}") for c in range(NCH)]
        junk = sb("junk", [128, D], BF16); t_junk = Trk("junk")
        junk32 = sb("junk32", [128, BLK], F32); t_junk32 = Trk("junk32")
        st4 = sb("st4", [128, 16], F32); t_st4 = Trk("st4")
        hT = sb("hT", [128, KC, BLK], BF16); t_hT = Trk("hT")
        carry = sb("carry", [128, 32, 3], F32); t_carry = [Trk(f"carry{i}") for i in range(32)]
        hn_p = Pool(nc, es, "hn", [128, D], BF16, 2)
        wbuf = sb("wbuf", [128, 16384], BF16)
        gtr = [Trk(f"wg{i}") for i in range(16)]

        class Carve:
            def __init__(self, items):
                self.items = items
                self.i = 0

            def next(self):
                it = self.items[self.i % len(self.items)]
                self.i += 1
                return it

        wfm_p = Carve([(wbuf[:, i * 1024:(i + 1) * 1024].rearrange("p (k c) -> p k c", k=KC), [gtr[i]]) for i in range(4)])
        wtm_p = Carve([(wbuf[:, 4096 + j * 4096:4096 + (j + 1) * 4096].rearrange("p (k c) -> p k c", k=KC), gtr[4 + 4 * j:8 + 4 * j]) for j in range(2)])
        wo_h = [(wbuf[:, k * 8192:(k + 1) * 8192].rearrange("p (k c) -> p k c", k=16), gtr[8 * k:8 * k + 8]) for k in range(2)]
        pc_p = Pool(nc, es, "pc", [128, BLK + 3], F32, 2)
        acc_p = Pool(nc, es, "acc", [128, BLK], F32, 2)
        fmA = sb("fmA", [128, 16, BLK], BF16)
        fmT = [fmA[:, i, :] for i in range(16)]
        t_fmT = [Trk(f"fmT{i}") for i in range(16)]
        khT = [sb(f"khT{i}", [128, BLK], BF16) for i in range(HG)]
        t_khT = [Trk(f"khT{i}") for i in range(HG)]
        qgT = [sb(f"qgT{i}", [128, BLK], BF16) for i in range(HG)]
        t_qgT = [Trk(f"qgT{i}") for i in range(HG)]
        vaug = [sb(f"vaug{c}", [128, HG, DVA], BF16) for c in range(NCH)]
        t_vaug = [[Trk(f"vaug{c}_{h}") for h in range(HG)] for c in range(NCH)]
        zg = [sb(f"zg{c}", [128, DINNER], BF16) for c in range(NCH)]
        t_zg = [[Trk(f"zg{c}_{h}") for h in range(NHEAD)] for c in range(NCH)]
        Sst = sb("Sst", [128, NHEAD, DVA], F32); t_S = [Trk(f"S{h}") for h in range(NHEAD)]
        Sbf = sb("Sbf", [128, NHEAD, DVA], BF16); t_Sbf = [Trk(f"Sbf{h}") for h in range(NHEAD)]
        rw = sb("rw", [8, 8, BLK], F32); t_rw = Trk("rw")
        rtmp = sb("rtmp", [8, 5, BLK], F32); t_rtmp = Trk("rtmp")
        rsm = sb("rsm", [8, 16], F32); t_rsm = Trk("rsm")
        mprev = sb("mprev", [8, 2], F32); t_mprev = Trk("mprev")
        tmsc = [sb(f"tmsc{c}", [128, 8, 8], F32) for c in range(NCH)]
        t_tmsc = [Trk(f"tmsc{c}") for c in range(NCH)]
        L_p = Pool(nc, es, "Lm", [128, 128], F32, 8)
        A_p = Pool(nc, es, "Am", [128, 128], F32R, 4)
        P_p = Pool(nc, es, "Pm", [128, 128], F32R, 4)
        Y_p = Pool(nc, es, "Ym", [128, 128], F32R, 8)
        YT_p = Pool(nc, es, "YTm", [128, 128], F32R, 8)
        mbf_p = Pool(nc, es, "mbf", [128, 128], BF16, 20)
        wv_p = Pool(nc, es, "wv", [128, DVA], BF16, 8)
        sm_p = Pool(nc, es, "sm", [128, 8], F32, 8)
        ps_p = Pool(nc, es, "ps", [128, 512], F32, 8, psum=True)
        print("SBUF bytes remaining", nc.sbuf_bytes_remaining)

        ident = cst[:, C_ID:C_ID + 128]
        ones = cst[:, C_ONE:C_ONE + 128]

        def sel(h):
            return cst[0:8, C_SEL + h * 128:C_SEL + (h + 1) * 128]

        def onescol(h):
            return cst[:, C_OC + h * 8:C_OC + (h + 1) * 8]

        S.dma("sp", cst[:], cst_d[:, :], [], [t_cst], t_cst)
        S.op("dve", lambda e: e.tensor_copy(out=identb[:], in_=ident), [t_cst], [t_identb])
        S.op("dve", lambda e: e.tensor_copy(out=identr[:], in_=ident), [t_cst], [t_identr])
        S.dma("sp", fwb[:], nw_d[NL:NL + 1, :].partition_broadcast(128), [], [t_fwb], t_fwb)
        def emit_casts(l):
            d = wd[l]
            for i in range(d["nfm"]):
                S.dma("pool", d["fm"][i], d["fm32"][i], [], [d["t_fm"]], d["t_fm"])
            for i in range(d["ntm"]):
                for hh in range(2):
                    S.dma("pool", d["tm"][i][:, hh * 2048:(hh + 1) * 2048], d["tm32"][i][:, hh * 2048:(hh + 1) * 2048],
                          [], [d["t_tm"]], d["t_tm"])
            S.dma("pool", d["g"], d["g32"], [], [d["t_g"]], d["t_g"])
            for i in range(8):
                S.dma("pool", d["o"][:, i * 2048:(i + 1) * 2048], d["o32"][:, i * 2048:(i + 1) * 2048],
                      [], [d["t_o"]], d["t_o"])
        emit_casts(0)

        t_blkc = [[Trk(f"blk{b}_{c}") for c in range(NCH)] for b in range(NB)]

        def store_all(b):
            for c in range(NCH):
                S.dma("sp", ov[b][:, c, :], xs[:, c, :], [t_xsc[c]], [t_blkc[b][c]], t_blkc[b][c])
        xv = x_d.rearrange("(b c p) d -> b p c d", p=128, c=NCH)
        ov = out_d.rearrange("(b c p) d -> b p c d", p=128, c=NCH)

        def rms_stats(src_ap_fn, n, trk_src, scale_n):
            for c in range(n):
                S.op("act", lambda e, c=c: e.activation(out=junk[:, 0:src_ap_fn(c).shape[-1]], in_=src_ap_fn(c), func=AF.Square,
                                                        accum_out=st4[:, c:c + 1]),
                     [trk_src[c], t_st4], [t_junk, t_st4])
            S.op("act", lambda e: e.activation(out=st4[:, 4:4 + n], in_=st4[:, 0:n], func=AF.Sqrt, scale=1.0 / scale_n, bias=EPS),
                 [t_st4], [t_st4])
            S.op("dve", lambda e: e.reciprocal(out=st4[:, 8:8 + n], in_=st4[:, 4:4 + n]), [t_st4], [t_st4])

        for l, kind in enumerate(kinds):
            d = wd[l]
            gdn = kind == "gdn"
            last = l == NL - 1
            fmt = fm_tiles(kind)
            tmt = tm_tiles(kind)
            nfm_hg = len(fmt[0])
            S.dma("sp", nwb[:], nw_d[l:l + 1, :].partition_broadcast(128), [], [t_nwb], t_nwb)
            S.dma("sp", hnwb[:], hnw_d[l:l + 1, :].partition_broadcast(128), [], [t_hnwb], t_hnwb)
            S.dma("sp", gp[:, 0:2], gp_d[l], [], [t_gp], t_gp)
            S.dma("sp", cwt[:, 0:d["nfm"] * 4], d["cw"][:, :], [], [t_cwt], t_cwt)
            S.dma("sp", wg[:], d["g"][:, :], [d["t_g"]], [t_wg], t_wg)
            if gdn:
                S.op("act", lambda e: e.activation(out=gp[:, 2:3], in_=gp[:, 1:2], func=AF.Exp), [t_gp], [t_gp])
                S.op("dve", lambda e: e.tensor_scalar_mul(out=gp[:, 2:3], in0=gp[:, 2:3], scalar1=-1.0), [t_gp], [t_gp])
            for h in range(NHEAD):
                S.op("pool", lambda e, h=h: e.memset(Sst[:, h, :], 0.0), [], [t_S[h]])
                S.op("pool", lambda e, h=h: e.memset(Sbf[:, h, :], 0.0), [], [t_Sbf[h]])
            for i in range(32):
                S.op("pool", lambda e, i=i: e.memset(carry[:, i, :], 0.0), [], [t_carry[i]])
            S.op("pool", lambda e: e.memset(mprev[:], 0.0), [], [t_mprev])
            if not gdn:
                for c in range(NCH):
                    for h in range(HG):
                        S.op("pool", lambda e, c=c, h=h: e.memset(vaug[c][:, h, DV:DV + 1], 1.0), [], [t_vaug[c][h]])
                        S.op("pool", lambda e, c=c, h=h: e.memset(vaug[c][:, h, DV + 1:DV + 2], 0.0), [], [t_vaug[c][h]])

            if l + 1 < NL:
                emit_casts(l + 1)
            for b in range(NB):
                src = xv if l == 0 else ov
                for c in range(NCH):
                    S.dma("sp", xs[:, c, :], src[b][:, c, :], [t_blkc[b][c]] if l > 0 else [], [t_xsc[c]], t_xsc[c])
                rms_stats(lambda c: xs[:, c, :], NCH, t_xsc, D)
                for c in range(NCH):
                    hn, t_hn = hn_p.next()
                    S.op("dve", lambda e, c=c, hn=hn: e.scalar_tensor_tensor(out=hn[:], in0=xs[:, c, :], scalar=st4[:, 8 + c:9 + c],
                                                                             in1=nwb[:], op0=ALU.mult, op1=ALU.mult),
                         [t_xsc[c], t_st4, t_nwb], [t_hn])
                    ps, t_ps = ps_p.next()
                    psb = ps[:].bitcast(BF16)
                    for kc in range(KC):
                        S.op("pe", lambda e, kc=kc, hn=hn, psb=psb: e.transpose(psb[:, kc * 128:(kc + 1) * 128], hn[:, kc * 128:(kc + 1) * 128], identb[:]),
                             [t_hn, t_identb], [t_ps], signal=(kc == KC - 1))
                    S.op("act", lambda e, c=c, psb=psb: e.activation(out=hT[:, :, c * 128:(c + 1) * 128],
                                                                     in_=psb.rearrange("p (k t) -> p k t", k=KC), func=AF.Copy),
                         [t_ps], [t_hT])

                if STOP < 2:
                    store_all(b)
                    continue
                psg = []
                for gi in range(2):
                    ps, t_ps = ps_p.next()
                    for kc in range(KC):
                        S.op("pe", lambda e, kc=kc, gi=gi, ps=ps: e.matmul(ps[0:8, :], lhsT=wg[:, kc * 16 + gi * 8:kc * 16 + gi * 8 + 8], rhs=hT[:, kc, :],
                                                                          start=(kc == 0), stop=(kc == KC - 1)),
                             [t_wg, t_hT], [t_ps], signal=(kc == KC - 1))
                    psg.append((ps, t_ps))
                R = lambda q: rw[:, q, :]
                Tm = lambda q: rtmp[:, q, :]
                if gdn:
                    (pa, t_pa), (pb, t_pb) = psg
                    S.op("act", lambda e: e.activation(out=Tm(0), in_=pa[0:8, :], func=AF.Identity, bias=gp[:, 0:1]), [t_pa, t_gp], [t_rtmp])
                    S.op("dve", lambda e: e.scalar_tensor_tensor(out=Tm(1), in0=Tm(0), scalar=-1.0, in1=Tm(0), op0=ALU.mult, op1=ALU.max), [t_rtmp], [t_rtmp])
                    S.op("act", lambda e: e.activation(out=Tm(1), in_=Tm(1), func=AF.Exp, scale=-1.0), [t_rtmp], [t_rtmp])
                    S.op("act", lambda e: e.activation(out=Tm(1), in_=Tm(1), func=AF.Ln, bias=1.0), [t_rtmp], [t_rtmp])
                    S.op("dve", lambda e: e.scalar_tensor_tensor(out=Tm(2), in0=Tm(0), scalar=0.0, in1=Tm(1), op0=ALU.max, op1=ALU.add), [t_rtmp], [t_rtmp])
                    S.op("dve", lambda e: e.tensor_scalar_mul(out=Tm(2), in0=Tm(2), scalar1=gp[:, 2:3]), [t_rtmp, t_gp], [t_rtmp])
                    S.op("act", lambda e: e.activation(out=R(2), in_=pb[0:8, :], func=AF.Sigmoid), [t_pb], [t_rw])
                    for c in range(NCH):
                        cs = slice(c * 128, (c + 1) * 128)
                        S.op("dve", lambda e, cs=cs: e.tensor_tensor_scan(out=rw[:, 3, cs], data0=ones[0:8, :], data1=rtmp[:, 2, cs], initial=0.0,
                                                                         op0=ALU.mult, op1=ALU.add), [t_rtmp, t_cst], [t_rw])
                    S.op("act", lambda e: e.activation(out=R(7), in_=R(3), func=AF.Exp), [t_rw], [t_rw])
                    S.op("dve", lambda e: e.tensor_tensor(out=R(0), in0=R(2), in1=R(7), op=ALU.mult), [t_rw], [t_rw])
                    for c in range(NCH):
                        cs = slice(c * 128, (c + 1) * 128)
                        S.op("act", lambda e, cs=cs, c=c: e.activation(out=rw[:, 1, cs], in_=rw[:, 3, cs], func=AF.Exp, scale=-1.0,
                                                                       bias=rw[:, 3, c * 128 + 127:c * 128 + 128]), [t_rw], [t_rw])
                    S.op("dve", lambda e: e.tensor_scalar_mul(out=R(4), in0=R(3), scalar1=-1.0), [t_rw], [t_rw])
                else:
                    (pa, t_pa), (pb, t_pb) = psg
                    S.op("act", lambda e: e.activation(out=Tm(0), in_=pa[0:8, :], func=AF.Identity, bias=gp[:, 0:1]), [t_pa, t_gp], [t_rtmp])
                    S.op("act", lambda e: e.activation(out=Tm(1), in_=pb[0:8, :], func=AF.Identity, bias=gp[:, 1:2]), [t_pb, t_gp], [t_rtmp])
                    S.op("dve", lambda e: e.scalar_tensor_tensor(out=Tm(2), in0=Tm(1), scalar=-1.0, in1=Tm(1), op0=ALU.mult, op1=ALU.max), [t_rtmp], [t_rtmp])
                    S.op("act", lambda e: e.activation(out=Tm(2), in_=Tm(2), func=AF.Exp, scale=-1.0), [t_rtmp], [t_rtmp])
                    S.op("act", lambda e: e.activation(out=Tm(2), in_=Tm(2), func=AF.Ln, bias=1.0), [t_rtmp], [t_rtmp])
                    S.op("dve", lambda e: e.scalar_tensor_tensor(out=Tm(3), in0=Tm(1), scalar=0.0, in1=Tm(2), op0=ALU.min, op1=ALU.subtract), [t_rtmp], [t_rtmp])
                    for c in range(NCH):
                        cs = slice(c * 128, (c + 1) * 128)
                        S.op("dve", lambda e, cs=cs: e.tensor_tensor_scan(out=rtmp[:, 2, cs], data0=ones[0:8, :], data1=rtmp[:, 3, cs], initial=0.0,
                                                                         op0=ALU.mult, op1=ALU.add), [t_rtmp, t_cst], [t_rtmp])
                    S.op("dve", lambda e: e.tensor_tensor(out=Tm(3), in0=Tm(0), in1=Tm(2), op=ALU.subtract), [t_rtmp], [t_rtmp])
                    lns = math.log(DK ** -0.5)
                    for c in range(NCH):
                        cs = slice(c * 128, (c + 1) * 128)
                        S.op("dve", lambda e, cs=cs: e.reduce_max(out=rsm[:, 0:1], in_=rtmp[:, 3, cs], axis=mybir.AxisListType.X), [t_rtmp], [t_rsm])
                        S.op("dve", lambda e: e.tensor_tensor(out=rsm[:, 1:2], in0=rsm[:, 0:1], in1=mprev[:, 0:1], op=ALU.max), [t_rsm, t_mprev], [t_rsm])
                        S.op("dve", lambda e: e.tensor_scalar(out=rsm[:, 2:3], in0=rsm[:, 1:2], scalar1=-1.0, scalar2=lns, op0=ALU.mult, op1=ALU.add), [t_rsm], [t_rsm])
                        S.op("dve", lambda e: e.tensor_tensor(out=rsm[:, 3:4], in0=mprev[:, 0:1], in1=rsm[:, 1:2], op=ALU.subtract), [t_rsm, t_mprev], [t_rsm])
                        S.op("dve", lambda e: e.tensor_scalar_mul(out=rsm[:, 4:5], in0=rsm[:, 1:2], scalar1=-1.0), [t_rsm], [t_rsm])
                        S.op("act", lambda e, cs=cs: e.activation(out=rw[:, 0, cs], in_=rtmp[:, 3, cs], func=AF.Exp, bias=rsm[:, 2:3]), [t_rtmp, t_rsm], [t_rw])
                        S.op("act", lambda e, cs=cs: e.activation(out=rw[:, 1, cs], in_=rtmp[:, 3, cs], func=AF.Exp, scale=0.0, bias=rsm[:, 3:4]), [t_rtmp, t_rsm], [t_rw])
                        S.op("act", lambda e, cs=cs: e.activation(out=rw[:, 2, cs], in_=rtmp[:, 2, cs], func=AF.Exp, scale=-1.0, bias=rsm[:, 4:5]), [t_rtmp, t_rsm], [t_rw])
                        S.op("dve", lambda e, c=c: e.tensor_tensor(out=mprev[:, 0:1], in0=rtmp[:, 2, c * 128 + 127:c * 128 + 128], in1=rsm[:, 1:2], op=ALU.add),
                             [t_rtmp, t_rsm], [t_mprev])

                for hg in range(NHG if STOP >= 3 else 0):
                    heads = list(range(hg * HG, (hg + 1) * HG))
                    fidx = {}
                    for ti, (nm, h, sub) in enumerate(fmt[hg]):
                        gti = hg * nfm_hg + ti
                        fidx[(nm, h, sub)] = ti
                        wt, g_wt = wfm_p.next()
                        S.dma("sp", wt, d["fm"][gti].rearrange("p (k c) -> p k c", k=KC), [d["t_fm"]], g_wt, g_wt[0])
                        ps, t_ps = ps_p.next()
                        for kc in range(KC):
                            S.op("pe", lambda e, kc=kc, wt=wt, ps=ps: e.matmul(ps[:, :], lhsT=wt[:, kc, :], rhs=hT[:, kc, :], start=(kc == 0), stop=(kc == KC - 1)),
                                 g_wt + [t_hT], [t_ps], signal=(kc == KC - 1))
                        pc, t_pc = pc_p.next()
                        acc, t_acc = acc_p.next()
                        S.op("act", lambda e, pc=pc, ps=ps: e.activation(out=pc[:, 3:BLK + 3], in_=ps[:, :], func=AF.Copy), [t_ps], [t_pc])
                        S.op("pool", lambda e, pc=pc, gti=gti: e.tensor_copy(out=pc[:, 0:3], in_=carry[:, gti, :]), [t_carry[gti]], [t_pc])
                        S.op("pool", lambda e, pc=pc, gti=gti: e.tensor_copy(out=carry[:, gti, :], in_=pc[:, BLK:BLK + 3]), [t_pc], [t_carry[gti]])
                        S.op("dve", lambda e, pc=pc, acc=acc, gti=gti: e.tensor_scalar_mul(out=acc[:], in0=pc[:, 3:BLK + 3], scalar1=cwt[:, gti * 4 + 3:gti * 4 + 4]),
                             [t_pc, t_cwt], [t_acc])
                        for j in range(3):
                            S.op("dve", lambda e, pc=pc, acc=acc, gti=gti, j=j: e.scalar_tensor_tensor(out=acc[:], in0=pc[:, j:BLK + j], scalar=cwt[:, gti * 4 + j:gti * 4 + j + 1],
                                                                                                     in1=acc[:], op0=ALU.mult, op1=ALU.add),
                                 [t_pc, t_cwt, t_acc], [t_acc])
                        S.op("act", lambda e, acc=acc, ti=ti: e.activation(out=fmT[ti], in_=acc[:], func=AF.Silu), [t_acc], [t_fmT[ti]])

                    ntm_hg = len(tmt[hg])
                    if STOP < 4:
                        continue
                    for ti, (nm, h0) in enumerate(tmt[hg]):
                        gti = hg * ntm_hg + ti
                        wt, g_wt = wtm_p.next()
                        for hh in range(4):
                            S.dma("sp", wt[:, hh * 2:(hh + 1) * 2, :], d["tm"][gti][:, hh * 1024:(hh + 1) * 1024].rearrange("p (k c) -> p k c", k=2),
                                  [d["t_tm"]], g_wt, g_wt[0])
                        for c in range(NCH):
                            ps, t_ps = ps_p.next()
                            for kc in range(KC):
                                S.op("pe", lambda e, kc=kc, c=c, wt=wt, ps=ps: e.matmul(ps[:, :], lhsT=hT[:, kc, c * 128:(c + 1) * 128], rhs=wt[:, kc, :],
                                                                                       start=(kc == 0), stop=(kc == KC - 1)),
                                     g_wt + [t_hT], [t_ps], signal=(kc == KC - 1))
                            zs = zg[c][:, h0 * DV:(h0 + 2) * DV]
                            tz = [t_zg[c][h0], t_zg[c][h0 + 1]]
                            if nm == "z" and gdn:
                                S.op("act", lambda e, ps=ps, zs=zs: e.activation(out=zs, in_=ps[:, :], func=AF.Silu), [t_ps], tz)
                            elif nm == "v":
                                hl = h0 - hg * HG
                                S.op("act", lambda e, ps=ps, c=c, hl=hl: e.activation(out=vaug[c][:, hl:hl + 2, 0:DV], in_=ps[:, :].rearrange("p (h d) -> p h d", h=2), func=AF.Copy),
                                     [t_ps], [t_vaug[c][hl], t_vaug[c][hl + 1]])
                            elif nm == "o":
                                S.op("act", lambda e, ps=ps, zs=zs: e.activation(out=zs, in_=ps[:, :], func=AF.Sigmoid), [t_ps], tz)
                            else:
                                acc, t_acc = acc_p.next()
                                S.op("act", lambda e, ps=ps, acc=acc: e.activation(out=acc[:], in_=ps[:, :], func=AF.Silu), [t_ps], [t_acc])
                                S.op("dve", lambda e, acc=acc, zs=zs: e.tensor_tensor(out=zs, in0=zs, in1=acc[:], op=ALU.mult), [t_acc] + tz, tz)
                            if nm == "z":
                                for hh in range(2):
                                    zh = zg[c][:, (h0 + hh) * DV:(h0 + hh + 1) * DV]
                                    S.op("pool", lambda e, zh=zh: e.tensor_tensor(out=zh, in0=zh, in1=hnwb[:], op=ALU.mult), [t_hnwb, tz[hh]], [tz[hh]])

                    if STOP < 5:
                        continue
                    if gdn:
                        for qi, nm in ((6, "k"), (5, "q")):
                            ps, t_ps = ps_p.next()
                            for hi, h in enumerate(heads):
                                ti = fidx[(nm, h, 0)]
                                S.op("act", lambda e, ti=ti: e.activation(out=junk32[:], in_=fmT[ti], func=AF.Square), [t_fmT[ti]], [t_junk32])
                                S.op("pe", lambda e, h=h, ps=ps: e.matmul(ps[0:8, :], lhsT=onescol(h), rhs=junk32[:], start=(hi == 0), stop=(hi == HG - 1)),
                                     [t_cst, t_junk32], [t_ps])
                            S.op("act", lambda e, ps=ps: e.activation(out=Tm(4), in_=ps[0:8, :], func=AF.Sqrt, bias=1e-6), [t_ps], [t_rtmp])
                            S.op("dve", lambda e, qi=qi: e.reciprocal(out=R(qi), in_=Tm(4)), [t_rtmp], [t_rw])
                        S.op("dve", lambda e: e.tensor_scalar_mul(out=R(5), in0=R(5), scalar1=DK ** -0.5), [t_rw], [t_rw])
                        for hi, h in enumerate(heads):
                            ps, t_ps = ps_p.next()
                            S.op("pe", lambda e, h=h, ps=ps: e.matmul(ps[:, :], lhsT=sel(h), rhs=R(6), start=True, stop=True), [t_cst, t_rw], [t_ps])
                            tk = fidx[("k", h, 0)]
                            S.op("dve", lambda e, hi=hi, tk=tk, ps=ps: e.tensor_tensor(out=khT[hi][:], in0=fmT[tk], in1=ps[:, :], op=ALU.mult),
                                 [t_fmT[tk], t_ps], [t_khT[hi]])
                            ps, t_ps = ps_p.next()
                            S.op("pe", lambda e, h=h, ps=ps: e.matmul(ps[:, :], lhsT=sel(h), rhs=R(7), start=True, stop=True), [t_cst, t_rw], [t_ps])
                            tq = fidx[("q", h, 0)]
                            S.op("dve", lambda e, hi=hi, tq=tq, ps=ps: e.tensor_tensor(out=qgT[hi][:], in0=fmT[tq], in1=ps[:, :], op=ALU.mult),
                                 [t_fmT[tq], t_ps], [t_qgT[hi]])
                    nq = 6 if gdn else 3
                    for c in range(NCH):
                        ps, t_ps = ps_p.next()
                        for q in range(nq):
                            S.op("pe", lambda e, q=q, c=c, ps=ps: e.transpose(ps[:, q * 8:(q + 1) * 8], rw[:, q, c * 128:(c + 1) * 128], cst[0:8, C_ID:C_ID + 8]),
                                 [t_rw, t_cst], [t_ps], signal=(q == nq - 1))
                        S.op("act", lambda e, c=c, ps=ps: e.activation(out=tmsc[c][:, 0:nq, :], in_=ps[:, 0:nq * 8].rearrange("p (q h) -> p q h", q=nq), func=AF.Copy),
                             [t_ps], [t_tmsc[c]])

                    for c in range(NCH if STOP >= 6 else 0):
                        cs = slice(c * 128, (c + 1) * 128)
                        tsc = t_tmsc[c]

                        def scf(q, h, c=c):
                            return tmsc[c][:, q, h:h + 1]

                        HX = [dict(h=h, hi=hi) for hi, h in enumerate(heads)]
                        if gdn:
                            for X in HX:
                                h, hi = X["h"], X["hi"]
                                psL, t_psL = ps_p.next()
                                S.op("pe", lambda e: e.matmul(psL[:, 0:128], lhsT=sel(h), rhs=rw[:, 3, cs], start=True, stop=False), [t_cst, t_rw], [t_psL], signal=False)
                                S.op("pe", lambda e: e.matmul(psL[:, 0:128], lhsT=ident, rhs=cst[:, C_PM:C_PM + 128], start=False, stop=True), [t_cst], [t_psL], signal=False)
                                S.op("pe", lambda e: e.matmul(psL[:, 128:256], lhsT=sel(h), rhs=rw[:, 3, cs], start=True, stop=False), [t_cst, t_rw], [t_psL], signal=False)
                                S.op("pe", lambda e: e.matmul(psL[:, 128:256], lhsT=ident, rhs=cst[:, C_NM:C_NM + 128], start=False, stop=True), [t_cst], [t_psL])
                                Ls, t_Ls = L_p.next()
                                LT, t_LT = L_p.next()
                                dec, t_dec = sm_p.next()
                                S.op("act", lambda e: e.activation(out=Ls[:], in_=psL[:, 0:128], func=AF.Exp, scale=-1.0, bias=scf(3, h)), [t_psL, tsc], [t_Ls])
                                S.op("act", lambda e: e.activation(out=LT[:], in_=psL[:, 128:256], func=AF.Exp, bias=scf(4, h)), [t_psL, tsc], [t_LT])
                                S.op("act", lambda e: e.activation(out=dec[:, 0:1], in_=psL[:, 255:256], func=AF.Exp), [t_psL], [t_dec])
                                X.update(Ls=Ls, t_Ls=t_Ls, LT=LT, t_LT=t_LT, dec=dec, t_dec=t_dec)
                            for X in HX:
                                h, hi = X["h"], X["hi"]
                                tq = fidx[("q", h, 0)]
                                psk, t_psk = ps_p.next()
                                S.op("pe", lambda e: e.matmul(psk[:, 0:128], lhsT=khT[hi][:, cs], rhs=khT[hi][:, cs], start=True, stop=True), [t_khT[hi]], [t_psk], signal=False)
                                S.op("pe", lambda e: e.matmul(psk[:, 128:256], lhsT=khT[hi][:, cs], rhs=fmT[tq][:, cs], start=True, stop=True), [t_khT[hi], t_fmT[tq]], [t_psk])
                                A, t_A = A_p.next()
                                Ls, t_Ls, LT, t_LT = X["Ls"], X["t_Ls"], X["LT"], X["t_LT"]
                                S.op("dve", lambda e: e.scalar_tensor_tensor(out=A[:], in0=psk[:, 0:128], scalar=scf(2, h), in1=Ls[:], op0=ALU.mult, op1=ALU.mult),
                                     [t_psk, tsc, t_Ls], [t_A])
                                S.op("pool", lambda e: e.tensor_tensor(out=A[:], in0=A[:], in1=ident, op=ALU.add), [t_A, t_cst], [t_A])
                                attnT, t_attnT = mbf_p.next()
                                S.op("dve", lambda e: e.tensor_tensor(out=attnT[:], in0=psk[:, 128:256], in1=LT[:], op=ALU.mult), [t_psk, t_LT], [t_attnT])
                                X.update(A=A, t_A=t_A, attnT=attnT, t_attnT=t_attnT, Y=None, t_Y=t_identr, YT=None, t_YT=t_identr)
                            for li, s2 in enumerate(LVLS):
                                lastl = li == len(LVLS) - 1
                                for X in HX:
                                    psP, t_psP = ps_p.next()
                                    Yap = identr[:] if X["Y"] is None else X["Y"][:]
                                    A, t_A = X["A"], X["t_A"]
                                    S.op("pe", lambda e: e.matmul(psP[:, 0:128], lhsT=A[:], rhs=Yap, start=True, stop=True), [t_A, X["t_Y"]], [t_psP])
                                    P, t_P = P_p.next()
                                    if lastl:
                                        S.op("act", lambda e: e.activation(out=P[:], in_=psP[:, 0:128], func=AF.Copy), [t_psP], [t_P])
                                    else:
                                        mk = cst[:, C_LV + li * 128:C_LV + (li + 1) * 128]
                                        S.op("dve", lambda e: e.tensor_tensor(out=P[:], in0=psP[:, 0:128], in1=mk, op=ALU.mult), [t_psP, t_cst], [t_P])
                                    X.update(psP=psP, t_psP=t_psP, P=P, t_P=t_P, Yap=Yap)
                                for X in HX:
                                    psP, t_psP, P, t_P, Yap = X["psP"], X["t_psP"], X["P"], X["t_P"], X["Yap"]
                                    YTap = identr[:] if X["YT"] is None else X["YT"][:]
                                    S.op("pe", lambda e: e.matmul(psP[:, 128:256], lhsT=YTap, rhs=P[:], start=True, stop=True), [X["t_YT"], t_P], [t_psP])
                                    if lastl:
                                        TT, t_TT = mbf_p.next()
                                        S.op("dve", lambda e: e.scalar_tensor_tensor(out=TT[:], in0=Yap, scalar=2.0, in1=psP[:, 128:256], op0=ALU.mult, op1=ALU.subtract),
                                             [X["t_Y"], t_psP], [t_TT])
                                        X.update(TT=TT, t_TT=t_TT)
                                    else:
                                        Yn, t_Yn = Y_p.next()
                                        S.op("dve", lambda e: e.scalar_tensor_tensor(out=Yn[:], in0=Yap, scalar=2.0, in1=psP[:, 128:256], op0=ALU.mult, op1=ALU.subtract),
                                             [X["t_Y"], t_psP], [t_Yn])
                                        X.update(Y=Yn, t_Y=t_Yn)
                                if not lastl:
                                    for X in HX:
                                        psP, t_psP = X["psP"], X["t_psP"]
                                        Yn, t_Yn = X["Y"], X["t_Y"]
                                        psPr = psP[:].bitcast(F32R)
                                        S.op("pe", lambda e: e.transpose(psPr[:, 256:384], Yn[:], identr[:]), [t_Yn, t_identr], [t_psP])
                                        YTn, t_YTn = YT_p.next()
                                        S.op("act", lambda e: e.activation(out=YTn[:], in_=psPr[:, 256:384], func=AF.Copy), [t_psP], [t_YTn])
                                        X.update(YT=YTn, t_YT=t_YTn)
                            for X in HX:
                                h, hi = X["h"], X["hi"]
                                tv0 = fidx[("v", h, 0)]
                                tv1 = fidx[("v", h, 1)]
                                psK, t_psK = ps_p.next()
                                psKb = psK[:].bitcast(BF16)
                                S.op("pe", lambda e: e.transpose(psKb[:, 0:128], khT[hi][:, cs], identb[:]), [t_khT[hi], t_identb], [t_psK], signal=False)
                                S.op("pe", lambda e: e.transpose(psKb[:, 128:256], fmT[tv0][:, cs], identb[:]), [t_fmT[tv0], t_identb], [t_psK], signal=False)
                                S.op("pe", lambda e: e.transpose(psKb[:, 256:384], fmT[tv1][:, cs], identb[:]), [t_fmT[tv1], t_identb], [t_psK])
                                bgK, t_bgK = mbf_p.next()
                                kd, t_kd = mbf_p.next()
                                bV, t_bV = wv_p.next()
                                S.op("act", lambda e: e.activation(out=bgK[:], in_=psKb[:, 0:128], func=AF.Copy, scale=scf(0, h)), [t_psK, tsc], [t_bgK])
                                S.op("act", lambda e: e.activation(out=kd[:], in_=psKb[:, 0:128], func=AF.Copy, scale=scf(1, h)), [t_psK, tsc], [t_kd])
                                S.op("act", lambda e: e.activation(out=bV[:, 0:DV], in_=psKb[:, 128:384], func=AF.Copy, scale=scf(2, h)), [t_psK, tsc], [t_bV])
                                X.update(bgK=bgK, t_bgK=t_bgK, kd=kd, t_kd=t_kd, bV=bV, t_bV=t_bV)
                            for X in HX:
                                psW, t_psW = ps_p.next()
                                bgK, t_bgK, TT, t_TT = X["bgK"], X["t_bgK"], X["TT"], X["t_TT"]
                                S.op("pe", lambda e: e.matmul(psW[:, 0:128], lhsT=bgK[:], rhs=TT[:], start=True, stop=True), [t_bgK, t_TT], [t_psW])
                                nwk, t_nwk = mbf_p.next()
                                S.op("act", lambda e: e.activation(out=nwk[:], in_=psW[:, 0:128], func=AF.Copy, scale=-1.0), [t_psW], [t_nwk])
                                X.update(psW=psW, t_psW=t_psW, nwk=nwk, t_nwk=t_nwk)
                            for X in HX:
                                h = X["h"]
                                psW, t_psW, TT, t_TT, bV, t_bV, nwk, t_nwk = X["psW"], X["t_psW"], X["TT"], X["t_TT"], X["bV"], X["t_bV"], X["nwk"], X["t_nwk"]
                                S.op("pe", lambda e: e.matmul(psW[:, 256:512], lhsT=TT[:], rhs=bV[:, 0:DV], start=True, stop=False), [t_TT, t_bV], [t_psW], signal=False)
                                S.op("pe", lambda e: e.matmul(psW[:, 256:512], lhsT=nwk[:], rhs=Sbf[:, h, 0:DV], start=False, stop=True), [t_nwk, t_Sbf[h]], [t_psW])
                                wv, t_wv = wv_p.next()
                                S.op("act", lambda e: e.activation(out=wv[:, 0:DV], in_=psW[:, 256:512], func=AF.Copy), [t_psW], [t_wv])
                                X.update(wv=wv, t_wv=t_wv)
                            for X in HX:
                                h, hi = X["h"], X["hi"]
                                og_h = zg[c][:, h * DV:(h + 1) * DV]
                                attnT, t_attnT, wv, t_wv, kd, t_kd, dec, t_dec = X["attnT"], X["t_attnT"], X["wv"], X["t_wv"], X["kd"], X["t_kd"], X["dec"], X["t_dec"]
                                psO, t_psO = ps_p.next()
                                S.op("pe", lambda e: e.matmul(psO[:, 0:DV], lhsT=qgT[hi][:, cs], rhs=Sbf[:, h, 0:DV], start=True, stop=False), [t_qgT[hi], t_Sbf[h]], [t_psO], signal=False)
                                S.op("pe", lambda e: e.matmul(psO[:, 0:DV], lhsT=attnT[:], rhs=wv[:, 0:DV], start=False, stop=True), [t_attnT, t_wv], [t_psO], signal=False)
                                S.op("pe", lambda e: e.matmul(psO[:, 256:512], lhsT=kd[:], rhs=wv[:, 0:DV], start=True, stop=True), [t_kd, t_wv], [t_psO])
                                acc, t_acc = acc_p.next()
                                S.op("act", lambda e: e.activation(out=acc[:, 0:DV], in_=psO[:, 256:512], func=AF.Copy), [t_psO], [t_acc])
                                S.op("dve", lambda e: e.scalar_tensor_tensor(out=Sst[:, h, 0:DV], in0=Sst[:, h, 0:DV], scalar=dec[:, 0:1], in1=acc[:, 0:DV], op0=ALU.mult, op1=ALU.add),
                                     [t_S[h], t_dec, t_acc], [t_S[h]])
                                S.op("pool", lambda e: e.tensor_copy(out=Sbf[:, h, 0:DV], in_=Sst[:, h, 0:DV]), [t_S[h]], [t_Sbf[h]])
                                sm, t_sm = sm_p.next()
                                S.op("act", lambda e: e.activation(out=junk[:, 0:DV], in_=psO[:, 0:DV], func=AF.Square, scale=scf(5, h), accum_out=sm[:, 0:1]),
                                     [t_psO, tsc, t_sm], [t_junk, t_sm])
                                S.op("act", lambda e: e.activation(out=sm[:, 1:2], in_=sm[:, 0:1], func=AF.Sqrt, scale=1.0 / DV, bias=EPS), [t_sm], [t_sm])
                                S.op("dve", lambda e: e.reciprocal(out=sm[:, 2:3], in_=sm[:, 1:2]), [t_sm], [t_sm])
                                S.op("dve", lambda e: e.tensor_tensor(out=sm[:, 3:4], in0=sm[:, 2:3], in1=scf(5, h), op=ALU.mult), [t_sm, tsc], [t_sm])
                                S.op("dve", lambda e: e.scalar_tensor_tensor(out=og_h, in0=psO[:, 0:DV], scalar=sm[:, 3:4], in1=og_h, op0=ALU.mult, op1=ALU.mult),
                                     [t_psO, t_sm, t_zg[c][h]], [t_zg[c][h]])
                        else:
                            for X in HX:
                                h, hi = X["h"], X["hi"]
                                tq = fidx[("q", h, 0)]
                                tk = fidx[("k", h, 0)]
                                psA, t_psA = ps_p.next()
                                psT, t_psT = ps_p.next()
                                psAb = psT[:].bitcast(BF16)
                                S.op("pe", lambda e: e.matmul(psA[:, 0:128], lhsT=fmT[tk][:, cs], rhs=fmT[tq][:, cs], start=True, stop=True), [t_fmT[tk], t_fmT[tq]], [t_psA])
                                S.op("pe", lambda e: e.transpose(psAb[:, 0:128], fmT[tk][:, cs], identb[:]), [t_fmT[tk], t_identb], [t_psT])
                                AT, t_AT = mbf_p.next()
                                S.op("dve", lambda e: e.scalar_tensor_tensor(out=AT[:], in0=psA[:, 0:128], scalar=scf(0, h), in1=cst[:, C_MT:C_MT + 128], op0=ALU.mult, op1=ALU.mult),
                                     [t_psA, tsc, t_cst], [t_AT])
                                kw, t_kw = mbf_p.next()
                                S.op("act", lambda e: e.activation(out=kw[:], in_=psAb[:, 0:128], func=AF.Copy, scale=scf(0, h)), [t_psT, tsc], [t_kw])
                                S.op("dve", lambda e: e.tensor_scalar_mul(out=Sst[:, h, :], in0=Sst[:, h, :], scalar1=scf(1, h)), [t_S[h], tsc], [t_S[h]])
                                S.op("pool", lambda e: e.tensor_copy(out=Sbf[:, h, :], in_=Sst[:, h, :]), [t_S[h]], [t_Sbf[h]])
                                X.update(AT=AT, t_AT=t_AT, kw=kw, t_kw=t_kw, tq=tq)
                            for X in HX:
                                h, hi = X["h"], X["hi"]
                                og_h = zg[c][:, h * DV:(h + 1) * DV]
                                AT, t_AT, kw, t_kw, tq = X["AT"], X["t_AT"], X["kw"], X["t_kw"], X["tq"]
                                psO, t_psO = ps_p.next()
                                S.op("pe", lambda e: e.matmul(psO[:, 0:DVA], lhsT=fmT[tq][:, cs], rhs=Sbf[:, h, :], start=True, stop=False), [t_fmT[tq], t_Sbf[h]], [t_psO], signal=False)
                                S.op("pe", lambda e: e.matmul(psO[:, 0:DVA], lhsT=AT[:], rhs=vaug[c][:, hi, :], start=False, stop=True), [t_AT, t_vaug[c][hi]], [t_psO])
                                psS, t_psS = ps_p.next()
                                S.op("pe", lambda e: e.matmul(psS[:, 0:DVA], lhsT=kw[:], rhs=vaug[c][:, hi, :], start=True, stop=True), [t_kw, t_vaug[c][hi]], [t_psS])
                                acc, t_acc = acc_p.next()
                                S.op("act", lambda e: e.activation(out=acc[:, 0:DVA], in_=psS[:, 0:DVA], func=AF.Copy), [t_psS], [t_acc])
                                S.op("dve", lambda e: e.tensor_tensor(out=Sst[:, h, :], in0=Sst[:, h, :], in1=acc[:, 0:DVA], op=ALU.add), [t_S[h], t_acc], [t_S[h]])
                                sm, t_sm = sm_p.next()
                                S.op("act", lambda e: e.activation(out=sm[:, 4:5], in_=psO[:, DV:DV + 1], func=AF.Abs), [t_psO], [t_sm])
                                S.op("dve", lambda e: e.tensor_tensor(out=sm[:, 4:5], in0=sm[:, 4:5], in1=scf(2, h), op=ALU.max), [t_sm, tsc], [t_sm])
                                S.op("dve", lambda e: e.reciprocal(out=sm[:, 5:6], in_=sm[:, 4:5]), [t_sm], [t_sm])
                                S.op("act", lambda e: e.activation(out=junk[:, 0:DV], in_=psO[:, 0:DV], func=AF.Square, scale=sm[:, 5:6], accum_out=sm[:, 0:1]),
                                     [t_psO, t_sm], [t_junk, t_sm])
                                S.op("act", lambda e: e.activation(out=sm[:, 1:2], in_=sm[:, 0:1], func=AF.Sqrt, scale=1.0 / DV, bias=EPS), [t_sm], [t_sm])
                                S.op("dve", lambda e: e.reciprocal(out=sm[:, 2:3], in_=sm[:, 1:2]), [t_sm], [t_sm])
                                S.op("dve", lambda e: e.tensor_tensor(out=sm[:, 3:4], in0=sm[:, 2:3], in1=sm[:, 5:6], op=ALU.mult), [t_sm], [t_sm])
                                S.op("dve", lambda e: e.scalar_tensor_tensor(out=og_h, in0=psO[:, 0:DV], scalar=sm[:, 3:4], in1=og_h, op0=ALU.mult, op1=ALU.mult),
                                     [t_psO, t_sm, t_zg[c][h]], [t_zg[c][h]])

                if STOP < 7:
                    store_all(b)
                    continue
                wov = d["o"].rearrange("p (k n) -> p k n", k=16)
                for n in range(2):
                    wo, g_wo = wo_h[n]
                    for q4 in range(4):
                        S.dma("sp", wo[:, q4 * 4:(q4 + 1) * 4, :], wov[:, q4 * 4:(q4 + 1) * 4, n * 512:(n + 1) * 512], [d["t_o"]], g_wo, g_wo[0])
                for c in range(NCH):
                    ogT = fmA[:, 4 * c:4 * c + 4, :].rearrange("p a (b t) -> p (a b) t", t=128)
                    for half in range(2):
                        ps, t_ps = ps_p.next()
                        psb = ps[:].bitcast(BF16)
                        for k8 in range(8):
                            kc = half * 8 + k8
                            S.op("pe", lambda e, psb=psb, k8=k8, kc=kc, c=c: e.transpose(psb[:, k8 * 128:(k8 + 1) * 128], zg[c][:, kc * 128:(kc + 1) * 128], identb[:]),
                                 [t_zg[c][kc // 2], t_identb], [t_ps], signal=(k8 == 7))
                        S.op("act", lambda e, psb=psb, ogT=ogT, half=half: e.activation(out=ogT[:, half * 8:(half + 1) * 8, :], in_=psb.rearrange("p (k t) -> p k t", k=8), func=AF.Copy),
                             [t_ps], [t_fmT[4 * c + 2 * half], t_fmT[4 * c + 2 * half + 1]])
                for c in range(NCH):
                    for n in range(2):
                        wo, g_wo = wo_h[n]
                        ogT = fmA[:, 4 * c:4 * c + 4, :].rearrange("p a (b t) -> p (a b) t", t=128)
                        ps, t_ps = ps_p.next()
                        for kc in range(16):
                            S.op("pe", lambda e, ps=ps, ogT=ogT, kc=kc: e.matmul(ps[:, :], lhsT=ogT[:, kc, :], rhs=wo[:, kc, :], start=(kc == 0), stop=(kc == 15)),
                                 [t_fmT[4 * c + kc // 4]] + g_wo, [t_ps], signal=(kc == 15))
                        S.op("dve", lambda e, ps=ps, c=c, n=n: e.tensor_tensor(out=xs[:, c, n * 512:(n + 1) * 512], in0=ps[:, :], in1=xs[:, c, n * 512:(n + 1) * 512], op=ALU.add),
                             [t_xsc[c], t_ps], [t_xsc[c]])
                        if n == 1 and not (last and final_norm):
                            S.dma("sp", ov[b][:, c, :], xs[:, c, :], [t_xsc[c]], [t_blkc[b][c]], t_blkc[b][c])
                if last and final_norm:
                    rms_stats(lambda c: xs[:, c, :], NCH, t_xsc, D)
                    for c in range(NCH):
                        S.op("dve", lambda e, c=c: e.scalar_tensor_tensor(out=xs[:, c, :], in0=xs[:, c, :], scalar=st4[:, 8 + c:9 + c], in1=fwb[:], op0=ALU.mult, op1=ALU.mult),
                             [t_xsc[c], t_st4, t_fwb], [t_xsc[c]])
                        S.dma("sp", ov[b][:, c, :], xs[:, c, :], [t_xsc[c]], [t_blkc[b][c]], t_blkc[b][c])

        e = S.eng["sp"]
        for b in range(NB):
            for c in range(NCH):
                S._wait(e, [t_blkc[b][c].w])
        print("semaphores used", S.nsem)
    return nc


def prep_inputs(T, kinds, x_b, p):
    NL = len(kinds)
    m = {"x": np.ascontiguousarray(x_b.reshape(T, D)), "cst": make_consts()}
    nw = np.concatenate([p["norm_w"][:NL], p["final_norm_w"][None, :]], 0).astype(np.float32)
    m["nw"] = np.ascontiguousarray(nw)
    hnw = np.zeros((NL, DV), np.float32)
    gp = np.zeros((NL, 8, 2), np.float32)
    for l, kind in enumerate(kinds):
        j = l // 2
        if kind == "gdn":
            hnw[l] = p["gdn_norm_w"][j]
            gp[l, :, 0] = p["gdn_dt_bias"][j]
            gp[l, :, 1] = p["gdn_a_log"][j]
            fm, tm, g, wo, cw = layout_weights(kind, p["gdn_w_in"][j], p["gdn_conv_w"][j], p["gdn_w_out"][j])
        else:
            hnw[l] = p["mlstm_norm_w"][j]
            gp[l, :, 0] = p["mlstm_i_bias"][j]
            gp[l, :, 1] = p["mlstm_f_bias"][j]
            fm, tm, g, wo, cw = layout_weights(kind, p["mlstm_w_in"][j], p["mlstm_conv_w"][j], p["mlstm_w_out"][j])
        m[f"wfm{l}"], m[f"wtm{l}"], m[f"wg{l}"], m[f"wo{l}"], m[f"cw{l}"] = fm, tm, g, wo, cw
    m["hnw"] = hnw
    m["gp"] = gp
    return m


def run(x, p, kinds, final_norm=True, n_cores=8):
    B, T, _ = x.shape
    nc = build(T, kinds, final_norm)
    base = prep_inputs(T, kinds, x[0], p)
    in_maps = []
    for c in range(n_cores):
        m = dict(base)
        m["x"] = np.ascontiguousarray(x[c % B].reshape(T, D))
        in_maps.append(m)
    res = run_bass_kernel_spmd(nc, in_maps, core_ids=list(range(n_cores)))
    return np.stack([res.results[b]["out"].reshape(T, D) for b in range(B)], 0)


def kernel(**inputs):
    p = {k: np.asarray(v, dtype=np.float32) for k, v in inputs.items()}
    x = p.pop("x")
    kinds = [layer_kind(l) for l in range(4)]
    return run(x, p, kinds).astype(np.float32)
```

```python
import math
from contextlib import ExitStack
import numpy as np
import concourse.bass as bass
import concourse.mybir as mybir
from concourse.bass_utils import run_bass_kernel_spmd

F32 = mybir.dt.float32
BF16 = mybir.dt.bfloat16
F32R = mybir.dt.float32r
AF = mybir.ActivationFunctionType
ALU = mybir.AluOpType

D = 1024
NHEAD = 8
DK = 128
DV = 256
DINNER = 2048
DVA = 258
BLK = 512
NCH = BLK // 128
KC = D // 128
HG = 4
NHG = NHEAD // HG
EPS = 1e-6
LVLS = [2, 4, 8, 16, 32, 64, 128]
import os
STOP = int(os.environ.get('KSTOP', '99'))
SUB = int(os.environ.get('KSUB', '99'))
KX = int(os.environ.get('KX', '0'))

C_ID = 0
C_PM = 128
C_NM = 256
C_MT = 384
C_LV = 512
C_OC = C_LV + 6 * 128
C_SEL = C_OC + 64
C_ONE = C_SEL + 8 * 128
NCST = C_ONE + 128


def make_consts():
    c = np.zeros((128, NCST), np.float32)
    i = np.arange(128)[:, None]
    j = np.arange(128)[None, :]
    c[:, C_ID:C_ID + 128] = (i == j)
    c[:, C_PM:C_PM + 128] = np.where(i > j, 0.0, 1e30)
    c[:, C_NM:C_NM + 128] = np.where(j >= i, 0.0, -1e30)
    c[:, C_MT:C_MT + 128] = (j >= i)
    for li, s in enumerate(LVLS[:-1]):
        c[:, C_LV + li * 128:C_LV + (li + 1) * 128] = ((i // s) == (j // s))
    oc = np.zeros((128, 8, 8), np.float32)
    for h in range(8):
        oc[:, h, h] = 1.0
    c[:, C_OC:C_OC + 64] = oc.reshape(128, 64)
    sel = np.zeros((128, 8, 128), np.float32)
    for h in range(8):
        sel[h, h, :] = 1.0
    c[:, C_SEL:C_SEL + 1024] = sel.reshape(128, 1024)
    c[:, C_ONE:C_ONE + 128] = 1.0
    return c


class Trk:
    __slots__ = ("name", "w", "r", "sem", "semv")

    def __init__(self, name):
        self.name = name
        self.w = None
        self.r = []
        self.sem = None
        self.semv = 0


class Eng:
    def __init__(self, h, sem, name):
        self.h = h
        self.sem = sem
        self.name = name
        self.cnt = 0
        self.waited = {}


class Sched:
    def __init__(self, nc, es):
        self.nc = nc
        self.es = es
        self.eng = {}
        for nm, h in (("pe", nc.tensor), ("act", nc.scalar), ("dve", nc.vector), ("pool", nc.gpsimd), ("sp", nc.sync)):
            sem = es.enter_context(nc.semaphore("s_" + nm))
            self.eng[nm] = Eng(h, sem, nm)
        self.nsem = 5

    def _wait(self, e, evs):
        need = {}
        for ev in evs:
            if ev is None:
                continue
            sem, val = ev
            k = id(sem)
            if k not in need or need[k][1] < val:
                need[k] = (sem, val)
        for k, (sem, val) in need.items():
            if sem is e.sem and e.name == "pe":
                continue
            if e.waited.get(k, 0) >= val:
                continue
            e.h.wait_ge(sem, val)
            e.waited[k] = val

    def _deps(self, reads, writes):
        evs = []
        for t in reads:
            evs.append(t.w)
        for t in writes:
            evs.append(t.w)
            evs.extend(t.r)
        return evs

    def op(self, en, fn, reads=(), writes=(), signal=True):
        e = self.eng[en]
        self._wait(e, self._deps(reads, writes))
        ins = fn(e.h)
        if signal:
            e.cnt += 1
            ins.then_inc(e.sem, 1)
            ev = (e.sem, e.cnt)
        else:
            ev = (e.sem, e.cnt + 1)
        for t in reads:
            t.r.append(ev)
        for t in writes:
            t.w = ev
            t.r = []
        return ins

    def dma(self, en, out, in_, reads, writes, owner):
        e = self.eng[en]
        if owner.sem is None:
            owner.sem = self.es.enter_context(self.nc.semaphore("d_" + owner.name))
            self.nsem += 1
        self._wait(e, self._deps(reads, writes))
        owner.semv += 16
        e.h.dma_start(out=out, in_=in_).then_inc(owner.sem, 16)
        ev = (owner.sem, owner.semv)
        for t in reads:
            t.r.append(ev)
        for t in writes:
            t.w = ev
            t.r = []
        return ev


class Pool:
    def __init__(self, nc, es, name, shape, dtype, n, psum=False):
        self.items = []
        for i in range(n):
            if psum:
                t = es.enter_context(nc.psum_tensor(f"pp_{name}{i}", shape, dtype))
            else:
                t = es.enter_context(nc.sbuf_tensor(f"pl_{name}{i}", shape, dtype))
            self.items.append((t, Trk(f"{name}{i}")))
        self.i = 0

    def next(self):
        it = self.items[self.i % len(self.items)]
        self.i += 1
        return it


def layer_kind(l):
    return "gdn" if l % 2 == 0 else "mlstm"


def fm_tiles(kind):
    out = []
    for hg in range(NHG):
        tl = []
        for h in range(hg * HG, (hg + 1) * HG):
            tl.append(("q", h, 0))
        for h in range(hg * HG, (hg + 1) * HG):
            tl.append(("k", h, 0))
        if kind == "gdn":
            for h in range(hg * HG, (hg + 1) * HG):
                tl.append(("v", h, 0))
                tl.append(("v", h, 1))
        out.append(tl)
    return out


def tm_tiles(kind):
    out = []
    for hg in range(NHG):
        tl = []
        names = ["z"] if kind == "gdn" else ["v", "o", "z"]
        for nm in names:
            for t in range(HG // 2):
                tl.append((nm, hg * HG + 2 * t))
        out.append(tl)
    return out


def col_offsets(kind):
    if kind == "gdn":
        return {"q": 0, "k": 1024, "v": 2048, "z": 4096, "g0": 6144, "g1": 6152}
    return {"q": 0, "k": 1024, "v": 2048, "o": 4096, "z": 6144, "g0": 8192, "g1": 8200}


def layout_weights(kind, w_in, conv_w, w_out):
    co = col_offsets(kind)
    wk = w_in.reshape(KC, 128, -1)
    fm = []
    cws = []
    convoff = {"q": 0, "k": 1024, "v": 2048}
    for tl in fm_tiles(kind):
        for (nm, h, sub) in tl:
            width = 128 if nm in ("q", "k") else 256
            c0 = co[nm] + h * width + sub * 128
            fm.append(np.ascontiguousarray(wk[:, :, c0:c0 + 128].transpose(1, 0, 2)).reshape(128, KC * 128))
            cc = convoff[nm] + h * width + sub * 128
            cws.append(conv_w[:, cc:cc + 128].T)
    tm = []
    for tl in tm_tiles(kind):
        for (nm, h0) in tl:
            c0 = co[nm] + h0 * 256
            tm.append(np.ascontiguousarray(wk[:, :, c0:c0 + 512].transpose(1, 0, 2)).reshape(128, KC * 512))
    g = np.concatenate([wk[:, :, co["g0"]:co["g0"] + 8], wk[:, :, co["g1"]:co["g1"] + 8]], axis=2)
    g = np.ascontiguousarray(g.transpose(1, 0, 2)).reshape(128, KC * 16)
    wo = np.ascontiguousarray(w_out.reshape(16, 128, D).transpose(1, 0, 2)).reshape(128, 16 * D)
    cw = np.ascontiguousarray(np.stack(cws, axis=1)).reshape(128, -1)
    return (np.ascontiguousarray(np.stack(fm, 0)), np.ascontiguousarray(np.stack(tm, 0)),
            np.ascontiguousarray(g), wo, cw.astype(np.float32))


def build(T, kinds, final_norm=True):
    nc = bass.Bass("TRN2", target_bir_lowering=False)
    NL = len(kinds)
    NB = T // BLK
    x_d = nc.dram_tensor("x", [T, D], F32, kind="ExternalInput").ap()
    out_d = nc.dram_tensor("out", [T, D], F32, kind="ExternalOutput").ap()
    cst_d = nc.dram_tensor("cst", [128, NCST], F32, kind="ExternalInput").ap()
    nw_d = nc.dram_tensor("nw", [NL + 1, D], F32, kind="ExternalInput").ap()
    hnw_d = nc.dram_tensor("hnw", [NL, DV], F32, kind="ExternalInput").ap()
    gp_d = nc.dram_tensor("gp", [NL, 8, 2], F32, kind="ExternalInput").ap()
    wd = []
    for l, kind in enumerate(kinds):
        nfm = sum(len(t) for t in fm_tiles(kind))
        ntm = sum(len(t) for t in tm_tiles(kind))
        d = {}
        d["nfm"], d["ntm"] = nfm, ntm
        d["fm32"] = nc.dram_tensor(f"wfm{l}", [nfm, 128, KC * 128], F32, kind="ExternalInput").ap()
        d["tm32"] = nc.dram_tensor(f"wtm{l}", [ntm, 128, KC * 512], F32, kind="ExternalInput").ap()
        d["g32"] = nc.dram_tensor(f"wg{l}", [128, KC * 16], F32, kind="ExternalInput").ap()
        d["o32"] = nc.dram_tensor(f"wo{l}", [128, 16 * D], F32, kind="ExternalInput").ap()
        d["cw"] = nc.dram_tensor(f"cw{l}", [128, nfm * 4], F32, kind="ExternalInput").ap()
        d["fm"] = nc.dram_tensor(f"sfm{l}", [nfm, 128, KC * 128], BF16, kind="Internal").ap()
        d["tm"] = nc.dram_tensor(f"stm{l}", [ntm, 128, KC * 512], BF16, kind="Internal").ap()
        d["g"] = nc.dram_tensor(f"sg{l}", [128, KC * 16], BF16, kind="Internal").ap()
        d["o"] = nc.dram_tensor(f"so{l}", [128, 16 * D], BF16, kind="Internal").ap()
        for k in ("fm", "tm", "g", "o"):
            d["t_" + k] = Trk(f"c{k}{l}")
        wd.append(d)

    with ExitStack() as es:
        E = es.enter_context
        S = Sched(nc, es)

        def sb(name, shape, dt):
            return E(nc.sbuf_tensor("sb_" + name, shape, dt))

        cst = sb("cst", [128, NCST], F32); t_cst = Trk("cst")
        identb = sb("identb", [128, 128], BF16); t_identb = Trk("identb")
        identr = sb("identr", [128, 128], F32R); t_identr = Trk("identr")
        ocb = sb("ocb", [128, 64], BF16); t_ocb = Trk("ocb")
        nwb = sb("nwb", [128, D], F32); t_nwb = Trk("nwb")
        fwb = sb("fwb", [128, D], F32); t_fwb = Trk("fwb")
        hnwb = sb("hnwb", [128, DV], F32); t_hnwb = Trk("hnwb")
        gp = sb("gp", [8, 4], F32); t_gp = Trk("gp")
        cwt = sb("cwt", [128, 32 * 4], F32); t_cwt = Trk("cwt")
        wg = sb("wg", [128, KC * 16], BF16); t_wg = Trk("wg")
        xs = sb("xs", [128, NCH, D], F32); t_xsc = [Trk(f"xs{c}") for c in range(NCH)]
        junk = sb("junk", [128, D], BF16); t_junk = Trk("junk")
        junk32 = sb("junk32", [128, BLK], F32); t_junk32 = Trk("junk32")
        st4 = sb("st4", [128, 16], F32); t_st4 = Trk("st4")
        hT = sb("hT", [128, KC, BLK], BF16); t_hT = Trk("hT")
        carry = sb("carry", [128, 32, 3], F32); t_carry = [Trk(f"carry{i}") for i in range(32)]
        hn_p = Pool(nc, es, "hn", [128, D], BF16, 2)
        wbuf = sb("wbuf", [128, 16384], BF16)
        gtr = [Trk(f"wg{i}") for i in range(16)]

        class Carve:
            def __init__(self, items):
                self.items = items
                self.i = 0

            def next(self):
                it = self.items[self.i % len(self.items)]
                self.i += 1
                return it

        wfm_p = Carve([(wbuf[:, i * 1024:(i + 1) * 1024].rearrange("p (k c) -> p k c", k=KC), [gtr[i]]) for i in range(4)])
        wtm_p = Carve([(wbuf[:, 4096 + j * 4096:4096 + (j + 1) * 4096].rearrange("p (k c) -> p k c", k=KC), gtr[4 + 4 * j:8 + 4 * j]) for j in range(2)])
        wo_h = [(wbuf[:, k * 8192:(k + 1) * 8192].rearrange("p (k c) -> p k c", k=16), gtr[8 * k:8 * k + 8]) for k in range(2)]
        pc_p = Pool(nc, es, "pc", [128, BLK + 3], F32, 2)
        acc_p = Pool(nc, es, "acc", [128, BLK], F32, 2)
        fmA = sb("fmA", [128, 16, BLK], BF16)
        fmT = [fmA[:, i, :] for i in range(16)]
        t_fmT = [Trk(f"fmT{i}") for i in range(16)]
        khT = [sb(f"khT{i}", [128, BLK], BF16) for i in range(HG)]
        t_khT = [Trk(f"khT{i}") for i in range(HG)]
        qgT = [sb(f"qgT{i}", [128, BLK], BF16) for i in range(HG)]
        t_qgT = [Trk(f"qgT{i}") for i in range(HG)]
        vaug = [sb(f"vaug{c}", [128, HG, DVA], BF16) for c in range(NCH)]
        t_vaug = [[Trk(f"vaug{c}_{h}") for h in range(HG)] for c in range(NCH)]
        zg = [sb(f"zg{c}", [128, DINNER], BF16) for c in range(NCH)]
        t_zg = [[Trk(f"zg{c}_{h}") for h in range(NHEAD)] for c in range(NCH)]
        Sst = sb("Sst", [128, NHEAD, DVA], F32); t_S = [Trk(f"S{h}") for h in range(NHEAD)]
        Sbf = sb("Sbf", [128, NHEAD, DVA], BF16); t_Sbf = [Trk(f"Sbf{h}") for h in range(NHEAD)]
        rw = sb("rw", [8, 8, BLK], F32); t_rw = Trk("rw")
        rtmp = sb("rtmp", [8, 5, BLK], F32); t_rtmp = Trk("rtmp")
        rsm = sb("rsm", [8, 16], F32); t_rsm = Trk("rsm")
        mprev = sb("mprev", [8, 2], F32); t_mprev = Trk("mprev")
        tmsc = [sb(f"tmsc{c}", [128, 8, 8], F32) for c in range(NCH)]
        t_tmsc = [Trk(f"tmsc{c}") for c in range(NCH)]
        L_p = Pool(nc, es, "Lm", [128, 128], F32, 8)
        A_p = Pool(nc, es, "Am", [128, 128], F32R, 4)
        P_p = Pool(nc, es, "Pm", [128, 128], F32R, 4)
        Y_p = Pool(nc, es, "Ym", [128, 128], F32R, 8)
        YT_p = Pool(nc, es, "YTm", [128, 128], F32R, 8)
        mbf_p = Pool(nc, es, "mbf", [128, 128], BF16, 20)
        wv_p = Pool(nc, es, "wv", [128, DVA], BF16, 8)
        sm_p = Pool(nc, es, "sm", [128, 8], F32, 8)
        ps_p = Pool(nc, es, "ps", [128, 512], F32, 8, psum=True)
        print("SBUF bytes remaining", nc.sbuf_bytes_remaining)

        ident = cst[:, C_ID:C_ID + 128]
        ones = cst[:, C_ONE:C_ONE + 128]

        def sel(h):
            return cst[0:8, C_SEL + h * 128:C_SEL + (h + 1) * 128]

        def onescol(h):
            return cst[:, C_OC + h * 8:C_OC + (h + 1) * 8]

        S.dma("sp", cst[:], cst_d[:, :], [], [t_cst], t_cst)
        S.op("dve", lambda e: e.tensor_copy(out=identb[:], in_=ident), [t_cst], [t_identb])
        S.op("dve", lambda e: e.tensor_copy(out=identr[:], in_=ident), [t_cst], [t_identr])
        S.op("dve", lambda e: e.tensor_copy(out=ocb[:], in_=cst[:, C_OC:C_OC + 64]), [t_cst], [t_ocb])
        S.dma("sp", fwb[:], nw_d[NL:NL + 1, :].partition_broadcast(128), [], [t_fwb], t_fwb)
        def emit_casts(l):
            d = wd[l]
            for i in range(d["nfm"]):
                S.dma("pool", d["fm"][i], d["fm32"][i], [], [d["t_fm"]], d["t_fm"])
            for i in range(d["ntm"]):
                for hh in range(2):
                    S.dma("pool", d["tm"][i][:, hh * 2048:(hh + 1) * 2048], d["tm32"][i][:, hh * 2048:(hh + 1) * 2048],
                          [], [d["t_tm"]], d["t_tm"])
            S.dma("pool", d["g"], d["g32"], [], [d["t_g"]], d["t_g"])
            for i in range(8):
                S.dma("pool", d["o"][:, i * 2048:(i + 1) * 2048], d["o32"][:, i * 2048:(i + 1) * 2048],
                      [], [d["t_o"]], d["t_o"])
        emit_casts(0)

        t_blkc = [[Trk(f"blk{b}_{c}") for c in range(NCH)] for b in range(NB)]

        def store_all(b):
            for c in range(NCH):
                S.dma("sp", ov[b][:, c, :], xs[:, c, :], [t_xsc[c]], [t_blkc[b][c]], t_blkc[b][c])
        xv = x_d.rearrange("(b c p) d -> b p c d", p=128, c=NCH)
        ov = out_d.rearrange("(b c p) d -> b p c d", p=128, c=NCH)

        def rms_stats(src_ap_fn, n, trk_src, scale_n):
            for c in range(n):
                S.op("act", lambda e, c=c: e.activation(out=junk[:, 0:src_ap_fn(c).shape[-1]], in_=src_ap_fn(c), func=AF.Square,
                                                        accum_out=st4[:, c:c + 1]),
                     [trk_src[c], t_st4], [t_junk, t_st4])
            S.op("act", lambda e: e.activation(out=st4[:, 4:4 + n], in_=st4[:, 0:n], func=AF.Sqrt, scale=1.0 / scale_n, bias=EPS),
                 [t_st4], [t_st4])
            S.op("dve", lambda e: e.reciprocal(out=st4[:, 8:8 + n], in_=st4[:, 4:4 + n]), [t_st4], [t_st4])

        for l, kind in enumerate(kinds):
            d = wd[l]
            gdn = kind == "gdn"
            last = l == NL - 1
            fmt = fm_tiles(kind)
            tmt = tm_tiles(kind)
            nfm_hg = len(fmt[0])
            S.dma("sp", nwb[:], nw_d[l:l + 1, :].partition_broadcast(128), [], [t_nwb], t_nwb)
            S.dma("sp", hnwb[:], hnw_d[l:l + 1, :].partition_broadcast(128), [], [t_hnwb], t_hnwb)
            S.dma("sp", gp[:, 0:2], gp_d[l], [], [t_gp], t_gp)
            S.dma("sp", cwt[:, 0:d["nfm"] * 4], d["cw"][:, :], [], [t_cwt], t_cwt)
            S.dma("sp", wg[:], d["g"][:, :], [d["t_g"]], [t_wg], t_wg)
            if gdn:
                S.op("act", lambda e: e.activation(out=gp[:, 2:3], in_=gp[:, 1:2], func=AF.Exp), [t_gp], [t_gp])
                S.op("dve", lambda e: e.tensor_scalar_mul(out=gp[:, 2:3], in0=gp[:, 2:3], scalar1=-1.0), [t_gp], [t_gp])
            for h in range(NHEAD):
                S.op("pool", lambda e, h=h: e.memset(Sst[:, h, :], 0.0), [], [t_S[h]])
                S.op("pool", lambda e, h=h: e.memset(Sbf[:, h, :], 0.0), [], [t_Sbf[h]])
            for i in range(32):
                S.op("pool", lambda e, i=i: e.memset(carry[:, i, :], 0.0), [], [t_carry[i]])
            S.op("pool", lambda e: e.memset(mprev[:], 0.0), [], [t_mprev])
            if not gdn:
                for c in range(NCH):
                    for h in range(HG):
                        S.op("pool", lambda e, c=c, h=h: e.memset(vaug[c][:, h, DV:DV + 1], 1.0), [], [t_vaug[c][h]])
                        S.op("pool", lambda e, c=c, h=h: e.memset(vaug[c][:, h, DV + 1:DV + 2], 0.0), [], [t_vaug[c][h]])

            if l + 1 < NL:
                emit_casts(l + 1)
            for b in range(NB):
                src = xv if l == 0 else ov
                for c in range(NCH):
                    S.dma("sp", xs[:, c, :], src[b][:, c, :], [t_blkc[b][c]] if l > 0 else [], [t_xsc[c]], t_xsc[c])
                rms_stats(lambda c: xs[:, c, :], NCH, t_xsc, D)
                for c in range(NCH):
                    hn, t_hn = hn_p.next()
                    S.op("dve", lambda e, c=c, hn=hn: e.scalar_tensor_tensor(out=hn[:], in0=xs[:, c, :], scalar=st4[:, 8 + c:9 + c],
                                                                             in1=nwb[:], op0=ALU.mult, op1=ALU.mult),
                         [t_xsc[c], t_st4, t_nwb], [t_hn])
                    ps, t_ps = ps_p.next()
                    psb = ps[:].bitcast(BF16)
                    for kc in range(KC):
                        S.op("pe", lambda e, kc=kc, hn=hn, psb=psb: e.transpose(psb[:, kc * 128:(kc + 1) * 128], hn[:, kc * 128:(kc + 1) * 128], identb[:]),
                             [t_hn, t_identb], [t_ps], signal=(kc == KC - 1))
                    S.op("act", lambda e, c=c, psb=psb: e.activation(out=hT[:, :, c * 128:(c + 1) * 128],
                                                                     in_=psb.rearrange("p (k t) -> p k t", k=KC), func=AF.Copy),
                         [t_ps], [t_hT])

                if STOP < 2:
                    store_all(b)
                    continue
                psg = []
                for gi in range(2):
                    ps, t_ps = ps_p.next()
                    for kc in range(KC):
                        S.op("pe", lambda e, kc=kc, gi=gi, ps=ps: e.matmul(ps[0:8, :], lhsT=wg[:, kc * 16 + gi * 8:kc * 16 + gi * 8 + 8], rhs=hT[:, kc, :],
                                                                          start=(kc == 0), stop=(kc == KC - 1)),
                             [t_wg, t_hT], [t_ps], signal=(kc == KC - 1))
                    psg.append((ps, t_ps))
                R = lambda q: rw[:, q, :]
                Tm = lambda q: rtmp[:, q, :]
                if gdn:
                    (pa, t_pa), (pb, t_pb) = psg
                    S.op("act", lambda e: e.activation(out=Tm(0), in_=pa[0:8, :], func=AF.Identity, bias=gp[:, 0:1]), [t_pa, t_gp], [t_rtmp])
                    S.op("dve", lambda e: e.scalar_tensor_tensor(out=Tm(1), in0=Tm(0), scalar=-1.0, in1=Tm(0), op0=ALU.mult, op1=ALU.max), [t_rtmp], [t_rtmp])
                    S.op("act", lambda e: e.activation(out=Tm(1), in_=Tm(1), func=AF.Exp, scale=-1.0), [t_rtmp], [t_rtmp])
                    S.op("act", lambda e: e.activation(out=Tm(1), in_=Tm(1), func=AF.Ln, bias=1.0), [t_rtmp], [t_rtmp])
                    S.op("dve", lambda e: e.scalar_tensor_tensor(out=Tm(2), in0=Tm(0), scalar=0.0, in1=Tm(1), op0=ALU.max, op1=ALU.add), [t_rtmp], [t_rtmp])
                    S.op("dve", lambda e: e.tensor_scalar_mul(out=Tm(2), in0=Tm(2), scalar1=gp[:, 2:3]), [t_rtmp, t_gp], [t_rtmp])
                    S.op("act", lambda e: e.activation(out=R(2), in_=pb[0:8, :], func=AF.Sigmoid), [t_pb], [t_rw])
                    for c in range(NCH):
                        cs = slice(c * 128, (c + 1) * 128)
                        S.op("dve", lambda e, cs=cs: e.tensor_tensor_scan(out=rw[:, 3, cs], data0=ones[0:8, :], data1=rtmp[:, 2, cs], initial=0.0,
                                                                         op0=ALU.mult, op1=ALU.add), [t_rtmp, t_cst], [t_rw])
                    S.op("act", lambda e: e.activation(out=R(7), in_=R(3), func=AF.Exp), [t_rw], [t_rw])
                    S.op("dve", lambda e: e.tensor_tensor(out=R(0), in0=R(2), in1=R(7), op=ALU.mult), [t_rw], [t_rw])
                    for c in range(NCH):
                        cs = slice(c * 128, (c + 1) * 128)
                        S.op("act", lambda e, cs=cs, c=c: e.activation(out=rw[:, 1, cs], in_=rw[:, 3, cs], func=AF.Exp, scale=-1.0,
                                                                       bias=rw[:, 3, c * 128 + 127:c * 128 + 128]), [t_rw], [t_rw])
                    S.op("dve", lambda e: e.tensor_scalar_mul(out=R(4), in0=R(3), scalar1=-1.0), [t_rw], [t_rw])
                else:
                    (pa, t_pa), (pb, t_pb) = psg
                    S.op("act", lambda e: e.activation(out=Tm(0), in_=pa[0:8, :], func=AF.Identity, bias=gp[:, 0:1]), [t_pa, t_gp], [t_rtmp])
                    S.op("act", lambda e: e.activation(out=Tm(1), in_=pb[0:8, :], func=AF.Identity, bias=gp[:, 1:2]), [t_pb, t_gp], [t_rtmp])
                    S.op("dve", lambda e: e.scalar_tensor_tensor(out=Tm(2), in0=Tm(1), scalar=-1.0, in1=Tm(1), op0=ALU.mult, op1=ALU.max), [t_rtmp], [t_rtmp])
                    S.op("act", lambda e: e.activation(out=Tm(2), in_=Tm(2), func=AF.Exp, scale=-1.0), [t_rtmp], [t_rtmp])
                    S.op("act", lambda e: e.activation(out=Tm(2), in_=Tm(2), func=AF.Ln, bias=1.0), [t_rtmp], [t_rtmp])
                    S.op("dve", lambda e: e.scalar_tensor_tensor(out=Tm(3), in0=Tm(1), scalar=0.0, in1=Tm(2), op0=ALU.min, op1=ALU.subtract), [t_rtmp], [t_rtmp])
                    for c in range(NCH):
                        cs = slice(c * 128, (c + 1) * 128)
                        S.op("dve", lambda e, cs=cs: e.tensor_tensor_scan(out=rtmp[:, 2, cs], data0=ones[0:8, :], data1=rtmp[:, 3, cs], initial=0.0,
                                                                         op0=ALU.mult, op1=ALU.add), [t_rtmp, t_cst], [t_rtmp])
                    S.op("dve", lambda e: e.tensor_tensor(out=Tm(3), in0=Tm(0), in1=Tm(2), op=ALU.subtract), [t_rtmp], [t_rtmp])
                    lns = math.log(DK ** -0.5)
                    for c in range(NCH):
                        cs = slice(c * 128, (c + 1) * 128)
                        S.op("dve", lambda e, cs=cs: e.reduce_max(out=rsm[:, 0:1], in_=rtmp[:, 3, cs], axis=mybir.AxisListType.X), [t_rtmp], [t_rsm])
                        S.op("dve", lambda e: e.tensor_tensor(out=rsm[:, 1:2], in0=rsm[:, 0:1], in1=mprev[:, 0:1], op=ALU.max), [t_rsm, t_mprev], [t_rsm])
                        S.op("dve", lambda e: e.tensor_scalar(out=rsm[:, 2:3], in0=rsm[:, 1:2], scalar1=-1.0, scalar2=lns, op0=ALU.mult, op1=ALU.add), [t_rsm], [t_rsm])
                        S.op("dve", lambda e: e.tensor_tensor(out=rsm[:, 3:4], in0=mprev[:, 0:1], in1=rsm[:, 1:2], op=ALU.subtract), [t_rsm, t_mprev], [t_rsm])
                        S.op("dve", lambda e: e.tensor_scalar_mul(out=rsm[:, 4:5], in0=rsm[:, 1:2], scalar1=-1.0), [t_rsm], [t_rsm])
                        S.op("act", lambda e, cs=cs: e.activation(out=rw[:, 0, cs], in_=rtmp[:, 3, cs], func=AF.Exp, bias=rsm[:, 2:3]), [t_rtmp, t_rsm], [t_rw])
                        S.op("act", lambda e, cs=cs: e.activation(out=rw[:, 1, cs], in_=rtmp[:, 3, cs], func=AF.Exp, scale=0.0, bias=rsm[:, 3:4]), [t_rtmp, t_rsm], [t_rw])
                        S.op("act", lambda e, cs=cs: e.activation(out=rw[:, 2, cs], in_=rtmp[:, 2, cs], func=AF.Exp, scale=-1.0, bias=rsm[:, 4:5]), [t_rtmp, t_rsm], [t_rw])
                        S.op("dve", lambda e, c=c: e.tensor_tensor(out=mprev[:, 0:1], in0=rtmp[:, 2, c * 128 + 127:c * 128 + 128], in1=rsm[:, 1:2], op=ALU.add),
                             [t_rtmp, t_rsm], [t_mprev])

                for hg in range(NHG if STOP >= 3 else 0):
                    heads = list(range(hg * HG, (hg + 1) * HG))
                    fidx = {}
                    for ti, (nm, h, sub) in enumerate(fmt[hg]):
                        gti = hg * nfm_hg + ti
                        fidx[(nm, h, sub)] = ti
                        wt, g_wt = wfm_p.next()
                        S.dma("sp", wt, d["fm"][gti].rearrange("p (k c) -> p k c", k=KC), [d["t_fm"]], g_wt, g_wt[0])
                        ps, t_ps = ps_p.next()
                        for kc in range(KC):
                            S.op("pe", lambda e, kc=kc, wt=wt, ps=ps: e.matmul(ps[:, :], lhsT=wt[:, kc, :], rhs=hT[:, kc, :], start=(kc == 0), stop=(kc == KC - 1)),
                                 g_wt + [t_hT], [t_ps], signal=(kc == KC - 1))
                        pc, t_pc = pc_p.next()
                        acc, t_acc = acc_p.next()
                        S.op("act", lambda e, pc=pc, ps=ps: e.activation(out=pc[:, 3:BLK + 3], in_=ps[:, :], func=AF.Copy), [t_ps], [t_pc])
                        S.op("pool", lambda e, pc=pc, gti=gti: e.tensor_copy(out=pc[:, 0:3], in_=carry[:, gti, :]), [t_carry[gti]], [t_pc])
                        S.op("pool", lambda e, pc=pc, gti=gti: e.tensor_copy(out=carry[:, gti, :], in_=pc[:, BLK:BLK + 3]), [t_pc], [t_carry[gti]])
                        S.op("dve", lambda e, pc=pc, acc=acc, gti=gti: e.tensor_scalar_mul(out=acc[:], in0=pc[:, 3:BLK + 3], scalar1=cwt[:, gti * 4 + 3:gti * 4 + 4]),
                             [t_pc, t_cwt], [t_acc])
                        for j in range(3):
                            S.op("dve", lambda e, pc=pc, acc=acc, gti=gti, j=j: e.scalar_tensor_tensor(out=acc[:], in0=pc[:, j:BLK + j], scalar=cwt[:, gti * 4 + j:gti * 4 + j + 1],
                                                                                                     in1=acc[:], op0=ALU.mult, op1=ALU.add),
                                 [t_pc, t_cwt, t_acc], [t_acc])
                        S.op("act", lambda e, acc=acc, ti=ti: e.activation(out=fmT[ti], in_=acc[:], func=AF.Silu), [t_acc], [t_fmT[ti]])

                    ntm_hg = len(tmt[hg])
                    if STOP < 4:
                        continue
                    for ti, (nm, h0) in enumerate(tmt[hg]):
                        gti = hg * ntm_hg + ti
                        wt, g_wt = wtm_p.next()
                        for hh in range(4):
                            S.dma("sp", wt[:, hh * 2:(hh + 1) * 2, :], d["tm"][gti][:, hh * 1024:(hh + 1) * 1024].rearrange("p (k c) -> p k c", k=2),
                                  [d["t_tm"]], g_wt, g_wt[0])
                        for c in range(NCH):
                            ps, t_ps = ps_p.next()
                            for kc in range(KC):
                                S.op("pe", lambda e, kc=kc, c=c, wt=wt, ps=ps: e.matmul(ps[:, :], lhsT=hT[:, kc, c * 128:(c + 1) * 128], rhs=wt[:, kc, :],
                                                                                       start=(kc == 0), stop=(kc == KC - 1)),
                                     g_wt + [t_hT], [t_ps], signal=(kc == KC - 1))
                            zs = zg[c][:, h0 * DV:(h0 + 2) * DV]
                            tz = [t_zg[c][h0], t_zg[c][h0 + 1]]
                            if nm == "z" and gdn:
                                S.op("act", lambda e, ps=ps, zs=zs: e.activation(out=zs, in_=ps[:, :], func=AF.Silu), [t_ps], tz)
                            elif nm == "v":
                                hl = h0 - hg * HG
                                S.op("act", lambda e, ps=ps, c=c, hl=hl: e.activation(out=vaug[c][:, hl:hl + 2, 0:DV], in_=ps[:, :].rearrange("p (h d) -> p h d", h=2), func=AF.Copy),
                                     [t_ps], [t_vaug[c][hl], t_vaug[c][hl + 1]])
                            elif nm == "o":
                                S.op("act", lambda e, ps=ps, zs=zs: e.activation(out=zs, in_=ps[:, :], func=AF.Sigmoid), [t_ps], tz)
                            else:
                                acc, t_acc = acc_p.next()
                                S.op("act", lambda e, ps=ps, acc=acc: e.activation(out=acc[:], in_=ps[:, :], func=AF.Silu), [t_ps], [t_acc])
                                S.op("dve", lambda e, acc=acc, zs=zs: e.tensor_tensor(out=zs, in0=zs, in1=acc[:], op=ALU.mult), [t_acc] + tz, tz)
                            if nm == "z":
                                for hh in range(2):
                                    zh = zg[c][:, (h0 + hh) * DV:(h0 + hh + 1) * DV]
                                    S.op("pool", lambda e, zh=zh: e.tensor_tensor(out=zh, in0=zh, in1=hnwb[:], op=ALU.mult), [t_hnwb, tz[hh]], [tz[hh]])

                    if STOP < 5:
                        continue
                    if gdn:
                        for qi, nm in ((6, "k"), (5, "q")):
                            ps, t_ps = ps_p.next()
                            for hi, h in enumerate(heads):
                                ti = fidx[(nm, h, 0)]
                                S.op("act", lambda e, ti=ti: e.activation(out=junk[:, 0:BLK], in_=fmT[ti], func=AF.Square), [t_fmT[ti]], [t_junk])
                                S.op("pe", lambda e, h=h, ps=ps: e.matmul(ps[0:8, :], lhsT=ocb[:, h * 8:(h + 1) * 8], rhs=junk[:, 0:BLK], start=(hi == 0), stop=(hi == HG - 1)),
                                     [t_ocb, t_junk], [t_ps])
                            S.op("act", lambda e, ps=ps: e.activation(out=Tm(4), in_=ps[0:8, :], func=AF.Sqrt, bias=1e-6), [t_ps], [t_rtmp])
                            S.op("dve", lambda e, qi=qi: e.reciprocal(out=R(qi), in_=Tm(4)), [t_rtmp], [t_rw])
                        S.op("dve", lambda e: e.tensor_scalar_mul(out=R(5), in0=R(5), scalar1=DK ** -0.5), [t_rw], [t_rw])
                        for hi, h in enumerate(heads):
                            ps, t_ps = ps_p.next()
                            S.op("pe", lambda e, h=h, ps=ps: e.matmul(ps[:, :], lhsT=sel(h), rhs=R(6), start=True, stop=True), [t_cst, t_rw], [t_ps])
                            tk = fidx[("k", h, 0)]
                            S.op("dve", lambda e, hi=hi, tk=tk, ps=ps: e.tensor_tensor(out=khT[hi][:], in0=fmT[tk], in1=ps[:, :], op=ALU.mult),
                                 [t_fmT[tk], t_ps], [t_khT[hi]])
                            ps, t_ps = ps_p.next()
                            S.op("pe", lambda e, h=h, ps=ps: e.matmul(ps[:, :], lhsT=sel(h), rhs=R(7), start=True, stop=True), [t_cst, t_rw], [t_ps])
                            tq = fidx[("q", h, 0)]
                            S.op("dve", lambda e, hi=hi, tq=tq, ps=ps: e.tensor_tensor(out=qgT[hi][:], in0=fmT[tq], in1=ps[:, :], op=ALU.mult),
                                 [t_fmT[tq], t_ps], [t_qgT[hi]])
                    nq = 6 if gdn else 3
                    for c in range(NCH):
                        ps, t_ps = ps_p.next()
                        for q in range(nq):
                            S.op("pe", lambda e, q=q, c=c, ps=ps: e.transpose(ps[:, q * 8:(q + 1) * 8], rw[:, q, c * 128:(c + 1) * 128], cst[0:8, C_ID:C_ID + 8]),
                                 [t_rw, t_cst], [t_ps], signal=(q == nq - 1))
                        S.op("act", lambda e, c=c, ps=ps: e.activation(out=tmsc[c][:, 0:nq, :], in_=ps[:, 0:nq * 8].rearrange("p (q h) -> p q h", q=nq), func=AF.Copy),
                             [t_ps], [t_tmsc[c]])

                    for c in range(NCH if STOP >= 6 else 0):
                        cs = slice(c * 128, (c + 1) * 128)
                        tsc = t_tmsc[c]

                        def scf(q, h, c=c):
                            return tmsc[c][:, q, h:h + 1]

                        HX = [dict(h=h, hi=hi) for hi, h in enumerate(heads)]
                        if gdn:
                            for X in HX:
                                h, hi = X["h"], X["hi"]
                                psL, t_psL = ps_p.next()
                                S.op("pe", lambda e: e.matmul(psL[:, 0:128], lhsT=sel(h), rhs=rw[:, 3, cs], start=True, stop=False), [t_cst, t_rw], [t_psL], signal=False)
                                S.op("pe", lambda e: e.matmul(psL[:, 0:128], lhsT=ident, rhs=cst[:, C_PM:C_PM + 128], start=False, stop=True), [t_cst], [t_psL], signal=False)
                                S.op("pe", lambda e: e.matmul(psL[:, 128:256], lhsT=sel(h), rhs=rw[:, 3, cs], start=True, stop=False), [t_cst, t_rw], [t_psL], signal=False)
                                S.op("pe", lambda e: e.matmul(psL[:, 128:256], lhsT=ident, rhs=cst[:, C_NM:C_NM + 128], start=False, stop=True), [t_cst], [t_psL])
                                Ls, t_Ls = L_p.next()
                                LT, t_LT = L_p.next()
                                dec, t_dec = sm_p.next()
                                S.op("act", lambda e: e.activation(out=Ls[:], in_=psL[:, 0:128], func=AF.Exp, scale=-1.0, bias=scf(3, h)), [t_psL, tsc], [t_Ls])
                                S.op("act", lambda e: e.activation(out=LT[:], in_=psL[:, 128:256], func=AF.Exp, bias=scf(4, h)), [t_psL, tsc], [t_LT])
                                S.op("act", lambda e: e.activation(out=dec[:, 0:1], in_=psL[:, 255:256], func=AF.Exp), [t_psL], [t_dec])
                                X.update(Ls=Ls, t_Ls=t_Ls, LT=LT, t_LT=t_LT, dec=dec, t_dec=t_dec)
                            for X in HX:
                                h, hi = X["h"], X["hi"]
                                tq = fidx[("q", h, 0)]
                                psk, t_psk = ps_p.next()
                                S.op("pe", lambda e: e.matmul(psk[:, 0:128], lhsT=khT[hi][:, cs], rhs=khT[hi][:, cs], start=True, stop=True), [t_khT[hi]], [t_psk], signal=False)
                                S.op("pe", lambda e: e.matmul(psk[:, 128:256], lhsT=khT[hi][:, cs], rhs=fmT[tq][:, cs], start=True, stop=True), [t_khT[hi], t_fmT[tq]], [t_psk])
                                A, t_A = A_p.next()
                                Ls, t_Ls, LT, t_LT = X["Ls"], X["t_Ls"], X["LT"], X["t_LT"]
                                S.op("dve", lambda e: e.scalar_tensor_tensor(out=A[:], in0=psk[:, 0:128], scalar=scf(2, h), in1=Ls[:], op0=ALU.mult, op1=ALU.mult),
                                     [t_psk, tsc, t_Ls], [t_A])
                                S.op("pool", lambda e: e.tensor_tensor(out=A[:], in0=A[:], in1=ident, op=ALU.add), [t_A, t_cst], [t_A])
                                attnT, t_attnT = mbf_p.next()
                                S.op("dve", lambda e: e.tensor_tensor(out=attnT[:], in0=psk[:, 128:256], in1=LT[:], op=ALU.mult), [t_psk, t_LT], [t_attnT])
                                X.update(A=A, t_A=t_A, attnT=attnT, t_attnT=t_attnT, Y=None, t_Y=t_identr, YT=None, t_YT=t_identr)
                            for li, s2 in enumerate(LVLS):
                                lastl = li == len(LVLS) - 1
                                for X in HX:
                                    psP, t_psP = ps_p.next()
                                    Yap = identr[:] if X["Y"] is None else X["Y"][:]
                                    A, t_A = X["A"], X["t_A"]
                                    S.op("pe", lambda e: e.matmul(psP[:, 0:128], lhsT=A[:], rhs=Yap, start=True, stop=True), [t_A, X["t_Y"]], [t_psP])
                                    P, t_P = P_p.next()
                                    if lastl:
                                        S.op("act", lambda e: e.activation(out=P[:], in_=psP[:, 0:128], func=AF.Copy), [t_psP], [t_P])
                                    else:
                                        mk = cst[:, C_LV + li * 128:C_LV + (li + 1) * 128]
                                        S.op("dve", lambda e: e.tensor_tensor(out=P[:], in0=psP[:, 0:128], in1=mk, op=ALU.mult), [t_psP, t_cst], [t_P])
                                    X.update(psP=psP, t_psP=t_psP, P=P, t_P=t_P, Yap=Yap)
                                for X in HX:
                                    psP, t_psP, P, t_P, Yap = X["psP"], X["t_psP"], X["P"], X["t_P"], X["Yap"]
                                    YTap = identr[:] if X["YT"] is None else X["YT"][:]
                                    S.op("pe", lambda e: e.matmul(psP[:, 128:256], lhsT=YTap, rhs=P[:], start=True, stop=True), [X["t_YT"], t_P], [t_psP])
                                    if lastl:
                                        TT, t_TT = mbf_p.next()
                                        S.op("dve", lambda e: e.scalar_tensor_tensor(out=TT[:], in0=Yap, scalar=2.0, in1=psP[:, 128:256], op0=ALU.mult, op1=ALU.subtract),
                                             [X["t_Y"], t_psP], [t_TT])
                                        X.update(TT=TT, t_TT=t_TT)
                                    else:
                                        Yn, t_Yn = Y_p.next()
                                        S.op("dve", lambda e: e.scalar_tensor_tensor(out=Yn[:], in0=Yap, scalar=2.0, in1=psP[:, 128:256], op0=ALU.mult, op1=ALU.subtract),
                                             [X["t_Y"], t_psP], [t_Yn])
                                        X.update(Y=Yn, t_Y=t_Yn)
                                if not lastl:
                                    for X in HX:
                                        psP, t_psP = X["psP"], X["t_psP"]
                                        Yn, t_Yn = X["Y"], X["t_Y"]
                                        psPr = psP[:].bitcast(F32R)
                                        S.op("pe", lambda e: e.transpose(psPr[:, 256:384], Yn[:], identr[:]), [t_Yn, t_identr], [t_psP])
                                        YTn, t_YTn = YT_p.next()
                                        S.op("act", lambda e: e.activation(out=YTn[:], in_=psPr[:, 256:384], func=AF.Copy), [t_psP], [t_YTn])
                                        X.update(YT=YTn, t_YT=t_YTn)
                            for X in HX:
                                h, hi = X["h"], X["hi"]
                                tv0 = fidx[("v", h, 0)]
                                tv1 = fidx[("v", h, 1)]
                                psK, t_psK = ps_p.next()
                                psKb = psK[:].bitcast(BF16)
                                S.op("pe", lambda e: e.transpose(psKb[:, 0:128], khT[hi][:, cs], identb[:]), [t_khT[hi], t_identb], [t_psK], signal=False)
                                S.op("pe", lambda e: e.transpose(psKb[:, 128:256], fmT[tv0][:, cs], identb[:]), [t_fmT[tv0], t_identb], [t_psK], signal=False)
                                S.op("pe", lambda e: e.transpose(psKb[:, 256:384], fmT[tv1][:, cs], identb[:]), [t_fmT[tv1], t_identb], [t_psK])
                                bgK, t_bgK = mbf_p.next()
                                kd, t_kd = mbf_p.next()
                                bV, t_bV = wv_p.next()
                                S.op("act", lambda e: e.activation(out=bgK[:], in_=psKb[:, 0:128], func=AF.Copy, scale=scf(0, h)), [t_psK, tsc], [t_bgK])
                                S.op("act", lambda e: e.activation(out=kd[:], in_=psKb[:, 0:128], func=AF.Copy, scale=scf(1, h)), [t_psK, tsc], [t_kd])
                                S.op("act", lambda e: e.activation(out=bV[:, 0:DV], in_=psKb[:, 128:384], func=AF.Copy, scale=scf(2, h)), [t_psK, tsc], [t_bV])
                                X.update(bgK=bgK, t_bgK=t_bgK, kd=kd, t_kd=t_kd, bV=bV, t_bV=t_bV)
                            for X in HX:
                                psW, t_psW = ps_p.next()
                                bgK, t_bgK, TT, t_TT = X["bgK"], X["t_bgK"], X["TT"], X["t_TT"]
                                S.op("pe", lambda e: e.matmul(psW[:, 0:128], lhsT=bgK[:], rhs=TT[:], start=True, stop=True), [t_bgK, t_TT], [t_psW])
                                nwk, t_nwk = mbf_p.next()
                                S.op("act", lambda e: e.activation(out=nwk[:], in_=psW[:, 0:128], func=AF.Copy, scale=-1.0), [t_psW], [t_nwk])
                                X.update(psW=psW, t_psW=t_psW, nwk=nwk, t_nwk=t_nwk)
                            for X in HX:
                                h = X["h"]
                                psW, t_psW, TT, t_TT, bV, t_bV, nwk, t_nwk = X["psW"], X["t_psW"], X["TT"], X["t_TT"], X["bV"], X["t_bV"], X["nwk"], X["t_nwk"]
                                S.op("pe", lambda e: e.matmul(psW[:, 256:512], lhsT=TT[:], rhs=bV[:, 0:DV], start=True, stop=False), [t_TT, t_bV], [t_psW], signal=False)
                                S.op("pe", lambda e: e.matmul(psW[:, 256:512], lhsT=nwk[:], rhs=Sbf[:, h, 0:DV], start=False, stop=True), [t_nwk, t_Sbf[h]], [t_psW])
                                wv, t_wv = wv_p.next()
                                S.op("act", lambda e: e.activation(out=wv[:, 0:DV], in_=psW[:, 256:512], func=AF.Copy), [t_psW], [t_wv])
                                X.update(wv=wv, t_wv=t_wv)
                            for X in HX:
                                h, hi = X["h"], X["hi"]
                                og_h = zg[c][:, h * DV:(h + 1) * DV]
                                attnT, t_attnT, wv, t_wv, kd, t_kd, dec, t_dec = X["attnT"], X["t_attnT"], X["wv"], X["t_wv"], X["kd"], X["t_kd"], X["dec"], X["t_dec"]
                                psO, t_psO = ps_p.next()
                                S.op("pe", lambda e: e.matmul(psO[:, 0:DV], lhsT=qgT[hi][:, cs], rhs=Sbf[:, h, 0:DV], start=True, stop=False), [t_qgT[hi], t_Sbf[h]], [t_psO], signal=False)
                                S.op("pe", lambda e: e.matmul(psO[:, 0:DV], lhsT=attnT[:], rhs=wv[:, 0:DV], start=False, stop=True), [t_attnT, t_wv], [t_psO], signal=False)
                                S.op("pe", lambda e: e.matmul(psO[:, 256:512], lhsT=kd[:], rhs=wv[:, 0:DV], start=True, stop=True), [t_kd, t_wv], [t_psO])
                                acc, t_acc = acc_p.next()
                                S.op("act", lambda e: e.activation(out=acc[:, 0:DV], in_=psO[:, 256:512], func=AF.Copy), [t_psO], [t_acc])
                                S.op("dve", lambda e: e.scalar_tensor_tensor(out=Sst[:, h, 0:DV], in0=Sst[:, h, 0:DV], scalar=dec[:, 0:1], in1=acc[:, 0:DV], op0=ALU.mult, op1=ALU.add),
                                     [t_S[h], t_dec, t_acc], [t_S[h]])
                                S.op("pool", lambda e: e.tensor_copy(out=Sbf[:, h, 0:DV], in_=Sst[:, h, 0:DV]), [t_S[h]], [t_Sbf[h]])
                                sm, t_sm = sm_p.next()
                                S.op("act", lambda e: e.activation(out=junk[:, 0:DV], in_=psO[:, 0:DV], func=AF.Square, scale=scf(5, h), accum_out=sm[:, 0:1]),
                                     [t_psO, tsc, t_sm], [t_junk, t_sm])
                                S.op("act", lambda e: e.activation(out=sm[:, 1:2], in_=sm[:, 0:1], func=AF.Sqrt, scale=1.0 / DV, bias=EPS), [t_sm], [t_sm])
                                S.op("dve", lambda e: e.reciprocal(out=sm[:, 2:3], in_=sm[:, 1:2]), [t_sm], [t_sm])
                                S.op("dve", lambda e: e.tensor_tensor(out=sm[:, 3:4], in0=sm[:, 2:3], in1=scf(5, h), op=ALU.mult), [t_sm, tsc], [t_sm])
                                S.op("dve", lambda e: e.scalar_tensor_tensor(out=og_h, in0=psO[:, 0:DV], scalar=sm[:, 3:4], in1=og_h, op0=ALU.mult, op1=ALU.mult),
                                     [t_psO, t_sm, t_zg[c][h]], [t_zg[c][h]])
                        else:
                            for X in HX:
                                h, hi = X["h"], X["hi"]
                                tq = fidx[("q", h, 0)]
                                tk = fidx[("k", h, 0)]
                                psA, t_psA = ps_p.next()
                                psT, t_psT = ps_p.next()
                                psAb = psT[:].bitcast(BF16)
                                S.op("pe", lambda e: e.matmul(psA[:, 0:128], lhsT=fmT[tk][:, cs], rhs=fmT[tq][:, cs], start=True, stop=True), [t_fmT[tk], t_fmT[tq]], [t_psA])
                                S.op("pe", lambda e: e.transpose(psAb[:, 0:128], fmT[tk][:, cs], identb[:]), [t_fmT[tk], t_identb], [t_psT])
                                AT, t_AT = mbf_p.next()
                                S.op("dve", lambda e: e.scalar_tensor_tensor(out=AT[:], in0=psA[:, 0:128], scalar=scf(0, h), in1=cst[:, C_MT:C_MT + 128], op0=ALU.mult, op1=ALU.mult),
                                     [t_psA, tsc, t_cst], [t_AT])
                                kw, t_kw = mbf_p.next()
                                S.op("act", lambda e: e.activation(out=kw[:], in_=psAb[:, 0:128], func=AF.Copy, scale=scf(0, h)), [t_psT, tsc], [t_kw])
                                S.op("dve", lambda e: e.tensor_scalar_mul(out=Sst[:, h, :], in0=Sst[:, h, :], scalar1=scf(1, h)), [t_S[h], tsc], [t_S[h]])
                                S.op("pool", lambda e: e.tensor_copy(out=Sbf[:, h, :], in_=Sst[:, h, :]), [t_S[h]], [t_Sbf[h]])
                                X.update(AT=AT, t_AT=t_AT, kw=kw, t_kw=t_kw, tq=tq)
                            for X in HX:
                                h, hi = X["h"], X["hi"]
                                og_h = zg[c][:, h * DV:(h + 1) * DV]
                                AT, t_AT, kw, t_kw, tq = X["AT"], X["t_AT"], X["kw"], X["t_kw"], X["tq"]
                                psO, t_psO = ps_p.next()
                                S.op("pe", lambda e: e.matmul(psO[:, 0:DVA], lhsT=fmT[tq][:, cs], rhs=Sbf[:, h, :], start=True, stop=False), [t_fmT[tq], t_Sbf[h]], [t_psO], signal=False)
                                S.op("pe", lambda e: e.matmul(psO[:, 0:DVA], lhsT=AT[:], rhs=vaug[c][:, hi, :], start=False, stop=True), [t_AT, t_vaug[c][hi]], [t_psO])
                                psS, t_psS = ps_p.next()
                                S.op("pe", lambda e: e.matmul(psS[:, 0:DVA], lhsT=kw[:], rhs=vaug[c][:, hi, :], start=True, stop=True), [t_kw, t_vaug[c][hi]], [t_psS])
                                acc, t_acc = acc_p.next()
                                S.op("act", lambda e: e.activation(out=acc[:, 0:DVA], in_=psS[:, 0:DVA], func=AF.Copy), [t_psS], [t_acc])
                                S.op("dve", lambda e: e.tensor_tensor(out=Sst[:, h, :], in0=Sst[:, h, :], in1=acc[:, 0:DVA], op=ALU.add), [t_S[h], t_acc], [t_S[h]])
                                sm, t_sm = sm_p.next()
                                S.op("act", lambda e: e.activation(out=sm[:, 4:5], in_=psO[:, DV:DV + 1], func=AF.Abs), [t_psO], [t_sm])
                                S.op("dve", lambda e: e.tensor_tensor(out=sm[:, 4:5], in0=sm[:, 4:5], in1=scf(2, h), op=ALU.max), [t_sm, tsc], [t_sm])
                                S.op("dve", lambda e: e.reciprocal(out=sm[:, 5:6], in_=sm[:, 4:5]), [t_sm], [t_sm])
                                S.op("act", lambda e: e.activation(out=junk[:, 0:DV], in_=psO[:, 0:DV], func=AF.Square, scale=sm[:, 5:6], accum_out=sm[:, 0:1]),
                                     [t_psO, t_sm], [t_junk, t_sm])
                                S.op("act", lambda e: e.activation(out=sm[:, 1:2], in_=sm[:, 0:1], func=AF.Sqrt, scale=1.0 / DV, bias=EPS), [t_sm], [t_sm])
                                S.op("dve", lambda e: e.reciprocal(out=sm[:, 2:3], in_=sm[:, 1:2]), [t_sm], [t_sm])
                                S.op("dve", lambda e: e.tensor_tensor(out=sm[:, 3:4], in0=sm[:, 2:3], in1=sm[:, 5:6], op=ALU.mult), [t_sm], [t_sm])
                                S.op("dve", lambda e: e.scalar_tensor_tensor(out=og_h, in0=psO[:, 0:DV], scalar=sm[:, 3:4], in1=og_h, op0=ALU.mult, op1=ALU.mult),
                                     [t_psO, t_sm, t_zg[c][h]], [t_zg[c][h]])

                if STOP < 7:
                    store_all(b)
                    continue
                wov = d["o"].rearrange("p (k n) -> p k n", k=16)
                for n in range(2):
                    wo, g_wo = wo_h[n]
                    for q4 in range(4):
                        S.dma("sp", wo[:, q4 * 4:(q4 + 1) * 4, :], wov[:, q4 * 4:(q4 + 1) * 4, n * 512:(n + 1) * 512], [d["t_o"]], g_wo, g_wo[0])
                for c in range(NCH):
                    ogT = fmA[:, 4 * c:4 * c + 4, :].rearrange("p a (b t) -> p (a b) t", t=128)
                    for half in range(2):
                        ps, t_ps = ps_p.next()
                        psb = ps[:].bitcast(BF16)
                        for k8 in range(8):
                            kc = half * 8 + k8
                            S.op("pe", lambda e, psb=psb, k8=k8, kc=kc, c=c: e.transpose(psb[:, k8 * 128:(k8 + 1) * 128], zg[c][:, kc * 128:(kc + 1) * 128], identb[:]),
                                 [t_zg[c][kc // 2], t_identb], [t_ps], signal=(k8 == 7))
                        S.op("act", lambda e, psb=psb, ogT=ogT, half=half: e.activation(out=ogT[:, half * 8:(half + 1) * 8, :], in_=psb.rearrange("p (k t) -> p k t", k=8), func=AF.Copy),
                             [t_ps], [t_fmT[4 * c + 2 * half], t_fmT[4 * c + 2 * half + 1]])
                for c in range(NCH):
                    for n in range(2):
                        wo, g_wo = wo_h[n]
                        ogT = fmA[:, 4 * c:4 * c + 4, :].rearrange("p a (b t) -> p (a b) t", t=128)
                        ps, t_ps = ps_p.next()
                        for kc in range(16):
                            S.op("pe", lambda e, ps=ps, ogT=ogT, kc=kc: e.matmul(ps[:, :], lhsT=ogT[:, kc, :], rhs=wo[:, kc, :], start=(kc == 0), stop=(kc == 15)),
                                 [t_fmT[4 * c + kc // 4]] + g_wo, [t_ps], signal=(kc == 15))
                        S.op("dve", lambda e, ps=ps, c=c, n=n: e.tensor_tensor(out=xs[:, c, n * 512:(n + 1) * 512], in0=ps[:, :], in1=xs[:, c, n * 512:(n + 1) * 512], op=ALU.add),
                             [t_xsc[c], t_ps], [t_xsc[c]])
                        if n == 1 and not (last and final_norm):
                            S.dma("sp", ov[b][:, c, :], xs[:, c, :], [t_xsc[c]], [t_blkc[b][c]], t_blkc[b][c])
                if last and final_norm:
                    rms_stats(lambda c: xs[:, c, :], NCH, t_xsc, D)
                    for c in range(NCH):
                        S.op("dve", lambda e, c=c: e.scalar_tensor_tensor(out=xs[:, c, :], in0=xs[:, c, :], scalar=st4[:, 8 + c:9 + c], in1=fwb[:], op0=ALU.mult, op1=ALU.mult),
                             [t_xsc[c], t_st4, t_fwb], [t_xsc[c]])
                        S.dma("sp", ov[b][:, c, :], xs[:, c, :], [t_xsc[c]], [t_blkc[b][c]], t_blkc[b][c])

        e = S.eng["sp"]
        for b in range(NB):
            for c in range(NCH):
                S._wait(e, [t_blkc[b][c].w])
        print("semaphores used", S.nsem)
    return nc


def prep_inputs(T, kinds, x_b, p):
    NL = len(kinds)
    m = {"x": np.ascontiguousarray(x_b.reshape(T, D)), "cst": make_consts()}
    nw = np.concatenate([p["norm_w"][:NL], p["final_norm_w"][None, :]], 0).astype(np.float32)
    m["nw"] = np.ascontiguousarray(nw)
    hnw = np.zeros((NL, DV), np.float32)
    gp = np.zeros((NL, 8, 2), np.float32)
    for l, kind in enumerate(kinds):
        j = l // 2
        if kind == "gdn":
            hnw[l] = p["gdn_norm_w"][j]
            gp[l, :, 0] = p["gdn_dt_bias"][j]
            gp[l, :, 1] = p["gdn_a_log"][j]
            fm, tm, g, wo, cw = layout_weights(kind, p["gdn_w_in"][j], p["gdn_conv_w"][j], p["gdn_w_out"][j])
        else:
            hnw[l] = p["mlstm_norm_w"][j]
            gp[l, :, 0] = p["mlstm_i_bias"][j]
            gp[l, :, 1] = p["mlstm_f_bias"][j]
            fm, tm, g, wo, cw = layout_weights(kind, p["mlstm_w_in"][j], p["mlstm_conv_w"][j], p["mlstm_w_out"][j])
        m[f"wfm{l}"], m[f"wtm{l}"], m[f"wg{l}"], m[f"wo{l}"], m[f"cw{l}"] = fm, tm, g, wo, cw
    m["hnw"] = hnw
    m["gp"] = gp
    return m


def run(x, p, kinds, final_norm=True, n_cores=8):
    B, T, _ = x.shape
    nc = build(T, kinds, final_norm)
    base = prep_inputs(T, kinds, x[0], p)
    in_maps = []
    for c in range(n_cores):
        m = dict(base)
        m["x"] = np.ascontiguousarray(x[c % B].reshape(T, D))
        in_maps.append(m)
    res = run_bass_kernel_spmd(nc, in_maps, core_ids=list(range(n_cores)))
    return np.stack([res.results[b]["out"].reshape(T, D) for b in range(B)], 0)


def kernel(**inputs):
    p = {k: np.asarray(v, dtype=np.float32) for k, v in inputs.items()}
    x = p.pop("x")
    kinds = [layer_kind(l) for l in range(4)]
    return run(x, p, kinds).astype(np.float32)
```

```python
import math
from contextlib import ExitStack
import numpy as np
import concourse.bass as bass
import concourse.mybir as mybir
from concourse.bass_utils import run_bass_kernel_spmd

F32 = mybir.dt.float32
BF16 = mybir.dt.bfloat16
F32R = mybir.dt.float32r
AF = mybir.ActivationFunctionType
ALU = mybir.AluOpType

D = 1024
NHEAD = 8
DK = 128
DV = 256
DINNER = 2048
DVA = 258
BLK = 512
NCH = BLK // 128
KC = D // 128
HG = 4
NHG = NHEAD // HG
EPS = 1e-6
LVLS = [2, 4, 8, 16, 32, 64, 128]
import os
STOP = int(os.environ.get('KSTOP', '99'))
SUB = int(os.environ.get('KSUB', '99'))
KX = int(os.environ.get('KX', '0'))

C_ID = 0
C_PM = 128
C_NM = 256
C_MT = 384
C_LV = 512
C_OC = C_LV + 6 * 128
C_SEL = C_OC + 64
C_ONE = C_SEL + 8 * 128
NCST = C_ONE + 128


def make_consts():
    c = np.zeros((128, NCST), np.float32)
    i = np.arange(128)[:, None]
    j = np.arange(128)[None, :]
    c[:, C_ID:C_ID + 128] = (i == j)
    c[:, C_PM:C_PM + 128] = np.where(i > j, 0.0, 1e30)
    c[:, C_NM:C_NM + 128] = np.where(j >= i, 0.0, -1e30)
    c[:, C_MT:C_MT + 128] = (j >= i)
    for li, s in enumerate(LVLS[:-1]):
        c[:, C_LV + li * 128:C_LV + (li + 1) * 128] = ((i // s) == (j // s))
    oc = np.zeros((128, 8, 8), np.float32)
    for h in range(8):
        oc[:, h, h] = 1.0
    c[:, C_OC:C_OC + 64] = oc.reshape(128, 64)
    sel = np.zeros((128, 8, 128), np.float32)
    for h in range(8):
        sel[h, h, :] = 1.0
    c[:, C_SEL:C_SEL + 1024] = sel.reshape(128, 1024)
    c[:, C_ONE:C_ONE + 128] = 1.0
    return c


class Trk:
    __slots__ = ("name", "w", "r", "sem", "semv")

    def __init__(self, name):
        self.name = name
        self.w = None
        self.r = []
        self.sem = None
        self.semv = 0


class Eng:
    def __init__(self, h, sem, name):
        self.h = h
        self.sem = sem
        self.name = name
        self.cnt = 0
        self.waited = {}


class Sched:
    def __init__(self, nc, es):
        self.nc = nc
        self.es = es
        self.eng = {}
        for nm, h in (("pe", nc.tensor), ("act", nc.scalar), ("dve", nc.vector), ("pool", nc.gpsimd), ("sp", nc.sync)):
            sem = es.enter_context(nc.semaphore("s_" + nm))
            self.eng[nm] = Eng(h, sem, nm)
        self.nsem = 5

    def _wait(self, e, evs):
        need = {}
        for ev in evs:
            if ev is None:
                continue
            sem, val = ev
            k = id(sem)
            if k not in need or need[k][1] < val:
                need[k] = (sem, val)
        for k, (sem, val) in need.items():
            if sem is e.sem and e.name == "pe":
                continue
            if e.waited.get(k, 0) >= val:
                continue
            e.h.wait_ge(sem, val)
            e.waited[k] = val

    def _deps(self, reads, writes):
        evs = []
        for t in reads:
            evs.append(t.w)
        for t in writes:
            evs.append(t.w)
            evs.extend(t.r)
        return evs

    def op(self, en, fn, reads=(), writes=(), signal=True):
        e = self.eng[en]
        self._wait(e, self._deps(reads, writes))
        ins = fn(e.h)
        if signal:
            e.cnt += 1
            ins.then_inc(e.sem, 1)
            ev = (e.sem, e.cnt)
        else:
            ev = (e.sem, e.cnt + 1)
        for t in reads:
            t.r.append(ev)
        for t in writes:
            t.w = ev
            t.r = []
        return ins

    def dma(self, en, out, in_, reads, writes, owner):
        e = self.eng[en]
        if owner.sem is None:
            owner.sem = self.es.enter_context(self.nc.semaphore("d_" + owner.name))
            self.nsem += 1
        self._wait(e, self._deps(reads, writes))
        owner.semv += 16
        e.h.dma_start(out=out, in_=in_).then_inc(owner.sem, 16)
        ev = (owner.sem, owner.semv)
        for t in reads:
            t.r.append(ev)
        for t in writes:
            t.w = ev
            t.r = []
        return ev


class Pool:
    def __init__(self, nc, es, name, shape, dtype, n, psum=False):
        self.items = []
        for i in range(n):
            if psum:
                t = es.enter_context(nc.psum_tensor(f"pp_{name}{i}", shape, dtype))
            else:
                t = es.enter_context(nc.sbuf_tensor(f"pl_{name}{i}", shape, dtype))
            self.items.append((t, Trk(f"{name}{i}")))
        self.i = 0

    def next(self):
        it = self.items[self.i % len(self.items)]
        self.i += 1
        return it


def layer_kind(l):
    return "gdn" if l % 2 == 0 else "mlstm"


def fm_tiles(kind):
    out = []
    for hg in range(NHG):
        tl = []
        for h in range(hg * HG, (hg + 1) * HG):
            tl.append(("q", h, 0))
        for h in range(hg * HG, (hg + 1) * HG):
            tl.append(("k", h, 0))
        if kind == "gdn":
            for h in range(hg * HG, (hg + 1) * HG):
                tl.append(("v", h, 0))
                tl.append(("v", h, 1))
        out.append(tl)
    return out


def tm_tiles(kind):
    out = []
    for hg in range(NHG):
        tl = []
        names = ["z"] if kind == "gdn" else ["v", "o", "z"]
        for nm in names:
            for t in range(HG // 2):
                tl.append((nm, hg * HG + 2 * t))
        out.append(tl)
    return out


def col_offsets(kind):
    if kind == "gdn":
        return {"q": 0, "k": 1024, "v": 2048, "z": 4096, "g0": 6144, "g1": 6152}
    return {"q": 0, "k": 1024, "v": 2048, "o": 4096, "z": 6144, "g0": 8192, "g1": 8200}


def layout_weights(kind, w_in, conv_w, w_out):
    co = col_offsets(kind)
    wk = w_in.reshape(KC, 128, -1)
    fm = []
    cws = []
    convoff = {"q": 0, "k": 1024, "v": 2048}
    for tl in fm_tiles(kind):
        for (nm, h, sub) in tl:
            width = 128 if nm in ("q", "k") else 256
            c0 = co[nm] + h * width + sub * 128
            fm.append(np.ascontiguousarray(wk[:, :, c0:c0 + 128].transpose(1, 0, 2)).reshape(128, KC * 128))
            cc = convoff[nm] + h * width + sub * 128
            cws.append(conv_w[:, cc:cc + 128].T)
    tm = []
    for tl in tm_tiles(kind):
        for (nm, h0) in tl:
            c0 = co[nm] + h0 * 256
            tm.append(np.ascontiguousarray(wk[:, :, c0:c0 + 512].transpose(1, 0, 2)).reshape(128, KC * 512))
    g = np.concatenate([wk[:, :, co["g0"]:co["g0"] + 8], wk[:, :, co["g1"]:co["g1"] + 8]], axis=2)
    g = np.ascontiguousarray(g.transpose(1, 0, 2)).reshape(128, KC * 16)
    wo = np.ascontiguousarray(w_out.reshape(16, 128, D).transpose(1, 0, 2)).reshape(128, 16 * D)
    cw = np.ascontiguousarray(np.stack(cws, axis=1)).reshape(128, -1)
    return (np.ascontiguousarray(np.stack(fm, 0)), np.ascontiguousarray(np.stack(tm, 0)),
            np.ascontiguousarray(g), wo, cw.astype(np.float32))


def build(T, kinds, final_norm=True):
    nc = bass.Bass("TRN2", target_bir_lowering=False)
    NL = len(kinds)
    NB = T // BLK
    x_d = nc.dram_tensor("x", [T, D], F32, kind="ExternalInput").ap()
    out_d = nc.dram_tensor("out", [T, D], F32, kind="ExternalOutput").ap()
    cst_d = nc.dram_tensor("cst", [128, NCST], F32, kind="ExternalInput").ap()
    nw_d = nc.dram_tensor("nw", [NL + 1, D], F32, kind="ExternalInput").ap()
    hnw_d = nc.dram_tensor("hnw", [NL, DV], F32, kind="ExternalInput").ap()
    gp_d = nc.dram_tensor("gp", [NL, 8, 2], F32, kind="ExternalInput").ap()
    wd = []
    for l, kind in enumerate(kinds):
        nfm = sum(len(t) for t in fm_tiles(kind))
        ntm = sum(len(t) for t in tm_tiles(kind))
        d = {}
        d["nfm"], d["ntm"] = nfm, ntm
        d["fm32"] = nc.dram_tensor(f"wfm{l}", [nfm, 128, KC * 128], F32, kind="ExternalInput").ap()
        d["tm32"] = nc.dram_tensor(f"wtm{l}", [ntm, 128, KC * 512], F32, kind="ExternalInput").ap()
        d["g32"] = nc.dram_tensor(f"wg{l}", [128, KC * 16], F32, kind="ExternalInput").ap()
        d["o32"] = nc.dram_tensor(f"wo{l}", [128, 16 * D], F32, kind="ExternalInput").ap()
        d["cw"] = nc.dram_tensor(f"cw{l}", [128, nfm * 4], F32, kind="ExternalInput").ap()
        d["fm"] = nc.dram_tensor(f"sfm{l}", [nfm, 128, KC * 128], BF16, kind="Internal").ap()
        d["tm"] = nc.dram_tensor(f"stm{l}", [ntm, 128, KC * 512], BF16, kind="Internal").ap()
        d["g"] = nc.dram_tensor(f"sg{l}", [128, KC * 16], BF16, kind="Internal").ap()
        d["o"] = nc.dram_tensor(f"so{l}", [128, 16 * D], BF16, kind="Internal").ap()
        for k in ("fm", "tm", "g", "o"):
            d["t_" + k] = Trk(f"c{k}{l}")
        wd.append(d)

    with ExitStack() as es:
        E = es.enter_context
        S = Sched(nc, es)

        def sb(name, shape, dt):
            return E(nc.sbuf_tensor("sb_" + name, shape, dt))

        cst = sb("cst", [128, NCST], F32); t_cst = Trk("cst")
        identb = sb("identb", [128, 128], BF16); t_identb = Trk("identb")
        identr = sb("identr", [128, 128], F32R); t_identr = Trk("identr")
        ocb = sb("ocb", [128, 64], BF16); t_ocb = Trk("ocb")
        nwb = sb("nwb", [128, D], F32); t_nwb = Trk("nwb")
        fwb = sb("fwb", [128, D], F32); t_fwb = Trk("fwb")
        hnwb = sb("hnwb", [128, DV], F32); t_hnwb = Trk("hnwb")
        gp = sb("gp", [8, 4], F32); t_gp = Trk("gp")
        cwt = sb("cwt", [128, 32 * 4], F32); t_cwt = Trk("cwt")
        wg = sb("wg", [128, KC * 16], BF16); t_wg = Trk("wg")
        xs = sb("xs", [128, NCH, D], F32); t_xsc = [Trk(f"xs{c}") for c in range(NCH)]
        junk = sb("junk", [128, D], BF16); t_junk = Trk("junk")
        junk32 = sb("junk32", [128, BLK], F32); t_junk32 = Trk("junk32")
        st4 = sb("st4", [128, 16], F32); t_st4 = Trk("st4")
        hT = sb("hT", [128, KC, BLK], BF16); t_hT = Trk("hT")
        carry = sb("carry", [128, 32, 3], F32); t_carry = [Trk(f"carry{i}") for i in range(32)]
        hn_p = Pool(nc, es, "hn", [128, D], BF16, 2)
        wbuf = sb("wbuf", [128, 16384], BF16)
        gtr = [Trk(f"wg{i}") for i in range(16)]

        class Carve:
            def __init__(self, items):
                self.items = items
                self.i = 0

            def next(self):
                it = self.items[self.i % len(self.items)]
                self.i += 1
                return it

        wfm_p = Carve([(wbuf[:, i * 1024:(i + 1) * 1024].rearrange("p (k c) -> p k c", k=KC), [gtr[i]]) for i in range(4)])
        wtm_p = Carve([(wbuf[:, 4096 + j * 4096:4096 + (j + 1) * 4096].rearrange("p (k c) -> p k c", k=KC), gtr[4 + 4 * j:8 + 4 * j]) for j in range(2)])
        wo_h = [(wbuf[:, k * 8192:(k + 1) * 8192].rearrange("p (k c) -> p k c", k=16), gtr[8 * k:8 * k + 8]) for k in range(2)]
        pc_p = Pool(nc, es, "pc", [128, BLK + 3], F32, 2)
        acc_p = Pool(nc, es, "acc", [128, BLK], F32, 2)
        fmA = sb("fmA", [128, 16, BLK], BF16)
        fmT = [fmA[:, i, :] for i in range(16)]
        t_fmT = [Trk(f"fmT{i}") for i in range(16)]
        khT = [sb(f"khT{i}", [128, BLK], BF16) for i in range(HG)]
        t_khT = [Trk(f"khT{i}") for i in range(HG)]
        qgT = [sb(f"qgT{i}", [128, BLK], BF16) for i in range(HG)]
        t_qgT = [Trk(f"qgT{i}") for i in range(HG)]
        vaug = [sb(f"vaug{c}", [128, HG, DVA], BF16) for c in range(NCH)]
        t_vaug = [[Trk(f"vaug{c}_{h}") for h in range(HG)] for c in range(NCH)]
        zg = [sb(f"zg{c}", [128, DINNER], BF16) for c in range(NCH)]
        t_zg = [[Trk(f"zg{c}_{h}") for h in range(NHEAD)] for c in range(NCH)]
        Sst = sb("Sst", [128, NHEAD, DVA], F32); t_S = [Trk(f"S{h}") for h in range(NHEAD)]
        Sbf = sb("Sbf", [128, NHEAD, DVA], BF16); t_Sbf = [Trk(f"Sbf{h}") for h in range(NHEAD)]
        rw = sb("rw", [8, 8, BLK], F32); t_rw = Trk("rw")
        rtmp = sb("rtmp", [8, 5, BLK], F32); t_rtmp = Trk("rtmp")
        rsm = sb("rsm", [8, 16], F32); t_rsm = Trk("rsm")
        mprev = sb("mprev", [8, 2], F32); t_mprev = Trk("mprev")
        tmsc = [sb(f"tmsc{c}", [128, 8, 8], F32) for c in range(NCH)]
        t_tmsc = [Trk(f"tmsc{c}") for c in range(NCH)]
        L_p = Pool(nc, es, "Lm", [128, 128], F32, 8)
        A_p = Pool(nc, es, "Am", [128, 128], F32R, 4)
        P_p = Pool(nc, es, "Pm", [128, 128], F32R, 4)
        Y_p = Pool(nc, es, "Ym", [128, 128], F32R, 8)
        YT_p = Pool(nc, es, "YTm", [128, 128], F32R, 8)
        mbf_p = Pool(nc, es, "mbf", [128, 128], BF16, 20)
        wv_p = Pool(nc, es, "wv", [128, DVA], BF16, 8)
        sm_p = Pool(nc, es, "sm", [128, 8], F32, 8)
        ps_p = Pool(nc, es, "ps", [128, 512], F32, 8, psum=True)
        print("SBUF bytes remaining", nc.sbuf_bytes_remaining)

        ident = cst[:, C_ID:C_ID + 128]
        ones = cst[:, C_ONE:C_ONE + 128]

        def sel(h):
            return cst[0:8, C_SEL + h * 128:C_SEL + (h + 1) * 128]

        def onescol(h):
            return cst[:, C_OC + h * 8:C_OC + (h + 1) * 8]

        S.dma("sp", cst[:], cst_d[:, :], [], [t_cst], t_cst)
        S.op("dve", lambda e: e.tensor_copy(out=identb[:], in_=ident), [t_cst], [t_identb])
        S.op("dve", lambda e: e.tensor_copy(out=identr[:], in_=ident), [t_cst], [t_identr])
        S.op("dve", lambda e: e.tensor_copy(out=ocb[:], in_=cst[:, C_OC:C_OC + 64]), [t_cst], [t_ocb])
        S.dma("sp", fwb[:], nw_d[NL:NL + 1, :].partition_broadcast(128), [], [t_fwb], t_fwb)
        def emit_casts(l):
            d = wd[l]
            for i in range(d["nfm"]):
                S.dma("pool", d["fm"][i], d["fm32"][i], [], [d["t_fm"]], d["t_fm"])
            for i in range(d["ntm"]):
                for hh in range(2):
                    S.dma("pool", d["tm"][i][:, hh * 2048:(hh + 1) * 2048], d["tm32"][i][:, hh * 2048:(hh + 1) * 2048],
                          [], [d["t_tm"]], d["t_tm"])
            S.dma("pool", d["g"], d["g32"], [], [d["t_g"]], d["t_g"])
            for i in range(8):
                S.dma("pool", d["o"][:, i * 2048:(i + 1) * 2048], d["o32"][:, i * 2048:(i + 1) * 2048],
                      [], [d["t_o"]], d["t_o"])
        emit_casts(0)

        t_blkc = [[Trk(f"blk{b}_{c}") for c in range(NCH)] for b in range(NB)]

        def store_all(b):
            for c in range(NCH):
                S.dma("sp", ov[b][:, c, :], xs[:, c, :], [t_xsc[c]], [t_blkc[b][c]], t_blkc[b][c])
        xv = x_d.rearrange("(b c p) d -> b p c d", p=128, c=NCH)
        ov = out_d.rearrange("(b c p) d -> b p c d", p=128, c=NCH)

        def rms_stats(src_ap_fn, n, trk_src, scale_n):
            for c in range(n):
                S.op("act", lambda e, c=c: e.activation(out=junk[:, 0:src_ap_fn(c).shape[-1]], in_=src_ap_fn(c), func=AF.Square,
                                                        accum_out=st4[:, c:c + 1]),
                     [trk_src[c], t_st4], [t_junk, t_st4])
            S.op("act", lambda e: e.activation(out=st4[:, 4:4 + n], in_=st4[:, 0:n], func=AF.Sqrt, scale=1.0 / scale_n, bias=EPS),
                 [t_st4], [t_st4])
            S.op("dve", lambda e: e.reciprocal(out=st4[:, 8:8 + n], in_=st4[:, 4:4 + n]), [t_st4], [t_st4])

        for l, kind in enumerate(kinds):
            d = wd[l]
            gdn = kind == "gdn"
            last = l == NL - 1
            fmt = fm_tiles(kind)
            tmt = tm_tiles(kind)
            nfm_hg = len(fmt[0])
            S.dma("sp", nwb[:], nw_d[l:l + 1, :].partition_broadcast(128), [], [t_nwb], t_nwb)
            S.dma("sp", hnwb[:], hnw_d[l:l + 1, :].partition_broadcast(128), [], [t_hnwb], t_hnwb)
            S.dma("sp", gp[:, 0:2], gp_d[l], [], [t_gp], t_gp)
            S.dma("sp", cwt[:, 0:d["nfm"] * 4], d["cw"][:, :], [], [t_cwt], t_cwt)
            S.dma("sp", wg[:], d["g"][:, :], [d["t_g"]], [t_wg], t_wg)
            if gdn:
                S.op("act", lambda e: e.activation(out=gp[:, 2:3], in_=gp[:, 1:2], func=AF.Exp), [t_gp], [t_gp])
                S.op("dve", lambda e: e.tensor_scalar_mul(out=gp[:, 2:3], in0=gp[:, 2:3], scalar1=-1.0), [t_gp], [t_gp])
            for h in range(NHEAD):
                S.op("pool", lambda e, h=h: e.memset(Sst[:, h, :], 0.0), [], [t_S[h]])
                S.op("pool", lambda e, h=h: e.memset(Sbf[:, h, :], 0.0), [], [t_Sbf[h]])
            for i in range(32):
                S.op("pool", lambda e, i=i: e.memset(carry[:, i, :], 0.0), [], [t_carry[i]])
            S.op("pool", lambda e: e.memset(mprev[:], 0.0), [], [t_mprev])
            if not gdn:
                for c in range(NCH):
                    for h in range(HG):
                        S.op("pool", lambda e, c=c, h=h: e.memset(vaug[c][:, h, DV:DV + 1], 1.0), [], [t_vaug[c][h]])
                        S.op("pool", lambda e, c=c, h=h: e.memset(vaug[c][:, h, DV + 1:DV + 2], 0.0), [], [t_vaug[c][h]])

            if l + 1 < NL:
                emit_casts(l + 1)
            for b in range(NB):
                src = xv if l == 0 else ov
                for c in range(NCH):
                    S.dma("sp", xs[:, c, :], src[b][:, c, :], [t_blkc[b][c]] if l > 0 else [], [t_xsc[c]], t_xsc[c])
                rms_stats(lambda c: xs[:, c, :], NCH, t_xsc, D)
                for c in range(NCH):
                    hn, t_hn = hn_p.next()
                    S.op("dve", lambda e, c=c, hn=hn: e.scalar_tensor_tensor(out=hn[:], in0=xs[:, c, :], scalar=st4[:, 8 + c:9 + c],
                                                                             in1=nwb[:], op0=ALU.mult, op1=ALU.mult),
                         [t_xsc[c], t_st4, t_nwb], [t_hn])
                    ps, t_ps = ps_p.next()
                    psb = ps[:].bitcast(BF16)
                    for kc in range(KC):
                        S.op("pe", lambda e, kc=kc, hn=hn, psb=psb: e.transpose(psb[:, kc * 128:(kc + 1) * 128], hn[:, kc * 128:(kc + 1) * 128], identb[:]),
                             [t_hn, t_identb], [t_ps], signal=(kc == KC - 1))
                    S.op("act", lambda e, c=c, psb=psb: e.activation(out=hT[:, :, c * 128:(c + 1) * 128],
                                                                     in_=psb.rearrange("p (k t) -> p k t", k=KC), func=AF.Copy),
                         [t_ps], [t_hT])

                if STOP < 2:
                    store_all(b)
                    continue
                psg = []
                for gi in range(2):
                    ps, t_ps = ps_p.next()
                    for kc in range(KC):
                        S.op("pe", lambda e, kc=kc, gi=gi, ps=ps: e.matmul(ps[0:8, :], lhsT=wg[:, kc * 16 + gi * 8:kc * 16 + gi * 8 + 8], rhs=hT[:, kc, :],
                                                                          start=(kc == 0), stop=(kc == KC - 1)),
                             [t_wg, t_hT], [t_ps], signal=(kc == KC - 1))
                    psg.append((ps, t_ps))
                R = lambda q: rw[:, q, :]
                Tm = lambda q: rtmp[:, q, :]
                if gdn:
                    (pa, t_pa), (pb, t_pb) = psg
                    S.op("act", lambda e: e.activation(out=Tm(0), in_=pa[0:8, :], func=AF.Identity, bias=gp[:, 0:1]), [t_pa, t_gp], [t_rtmp])
                    S.op("dve", lambda e: e.scalar_tensor_tensor(out=Tm(1), in0=Tm(0), scalar=-1.0, in1=Tm(0), op0=ALU.mult, op1=ALU.max), [t_rtmp], [t_rtmp])
                    S.op("act", lambda e: e.activation(out=Tm(1), in_=Tm(1), func=AF.Exp, scale=-1.0), [t_rtmp], [t_rtmp])
                    S.op("act", lambda e: e.activation(out=Tm(1), in_=Tm(1), func=AF.Ln, bias=1.0), [t_rtmp], [t_rtmp])
                    S.op("dve", lambda e: e.scalar_tensor_tensor(out=Tm(2), in0=Tm(0), scalar=0.0, in1=Tm(1), op0=ALU.max, op1=ALU.add), [t_rtmp], [t_rtmp])
                    S.op("dve", lambda e: e.tensor_scalar_mul(out=Tm(2), in0=Tm(2), scalar1=gp[:, 2:3]), [t_rtmp, t_gp], [t_rtmp])
                    S.op("act", lambda e: e.activation(out=R(2), in_=pb[0:8, :], func=AF.Sigmoid), [t_pb], [t_rw])
                    for c in range(NCH):
                        cs = slice(c * 128, (c + 1) * 128)
                        S.op("dve", lambda e, cs=cs: e.tensor_tensor_scan(out=rw[:, 3, cs], data0=ones[0:8, :], data1=rtmp[:, 2, cs], initial=0.0,
                                                                         op0=ALU.mult, op1=ALU.add), [t_rtmp, t_cst], [t_rw])
                    S.op("act", lambda e: e.activation(out=R(7), in_=R(3), func=AF.Exp), [t_rw], [t_rw])
                    S.op("dve", lambda e: e.tensor_tensor(out=R(0), in0=R(2), in1=R(7), op=ALU.mult), [t_rw], [t_rw])
                    for c in range(NCH):
                        cs = slice(c * 128, (c + 1) * 128)
                        S.op("act", lambda e, cs=cs, c=c: e.activation(out=rw[:, 1, cs], in_=rw[:, 3, cs], func=AF.Exp, scale=-1.0,
                                                                       bias=rw[:, 3, c * 128 + 127:c * 128 + 128]), [t_rw], [t_rw])
                    S.op("dve", lambda e: e.tensor_scalar_mul(out=R(4), in0=R(3), scalar1=-1.0), [t_rw], [t_rw])
                else:
                    (pa, t_pa), (pb, t_pb) = psg
                    S.op("act", lambda e: e.activation(out=Tm(0), in_=pa[0:8, :], func=AF.Identity, bias=gp[:, 0:1]), [t_pa, t_gp], [t_rtmp])
                    S.op("act", lambda e: e.activation(out=Tm(1), in_=pb[0:8, :], func=AF.Identity, bias=gp[:, 1:2]), [t_pb, t_gp], [t_rtmp])
                    S.op("dve", lambda e: e.scalar_tensor_tensor(out=Tm(2), in0=Tm(1), scalar=-1.0, in1=Tm(1), op0=ALU.mult, op1=ALU.max), [t_rtmp], [t_rtmp])
                    S.op("act", lambda e: e.activation(out=Tm(2), in_=Tm(2), func=AF.Exp, scale=-1.0), [t_rtmp], [t_rtmp])
                    S.op("act", lambda e: e.activation(out=Tm(2), in_=Tm(2), func=AF.Ln, bias=1.0), [t_rtmp], [t_rtmp])
                    S.op("dve", lambda e: e.scalar_tensor_tensor(out=Tm(3), in0=Tm(1), scalar=0.0, in1=Tm(2), op0=ALU.min, op1=ALU.subtract), [t_rtmp], [t_rtmp])
                    for c in range(NCH):
                        cs = slice(c * 128, (c + 1) * 128)
                        S.op("dve", lambda e, cs=cs: e.tensor_tensor_scan(out=rtmp[:, 2, cs], data0=ones[0:8, :], data1=rtmp[:, 3, cs], initial=0.0,
                                                                         op0=ALU.mult, op1=ALU.add), [t_rtmp, t_cst], [t_rtmp])
                    S.op("dve", lambda e: e.tensor_tensor(out=Tm(3), in0=Tm(0), in1=Tm(2), op=ALU.subtract), [t_rtmp], [t_rtmp])
                    lns = math.log(DK ** -0.5)
                    for c in range(NCH):
                        cs = slice(c * 128, (c + 1) * 128)
                        S.op("dve", lambda e, cs=cs: e.reduce_max(out=rsm[:, 0:1], in_=rtmp[:, 3, cs], axis=mybir.AxisListType.X), [t_rtmp], [t_rsm])
                        S.op("dve", lambda e: e.tensor_tensor(out=rsm[:, 1:2], in0=rsm[:, 0:1], in1=mprev[:, 0:1], op=ALU.max), [t_rsm, t_mprev], [t_rsm])
                        S.op("dve", lambda e: e.tensor_scalar(out=rsm[:, 2:3], in0=rsm[:, 1:2], scalar1=-1.0, scalar2=lns, op0=ALU.mult, op1=ALU.add), [t_rsm], [t_rsm])
                        S.op("dve", lambda e: e.tensor_tensor(out=rsm[:, 3:4], in0=mprev[:, 0:1], in1=rsm[:, 1:2], op=ALU.subtract), [t_rsm, t_mprev], [t_rsm])
                        S.op("dve", lambda e: e.tensor_scalar_mul(out=rsm[:, 4:5], in0=rsm[:, 1:2], scalar1=-1.0), [t_rsm], [t_rsm])
                        S.op("act", lambda e, cs=cs: e.activation(out=rw[:, 0, cs], in_=rtmp[:, 3, cs], func=AF.Exp, bias=rsm[:, 2:3]), [t_rtmp, t_rsm], [t_rw])
                        S.op("act", lambda e, cs=cs: e.activation(out=rw[:, 1, cs], in_=rtmp[:, 3, cs], func=AF.Exp, scale=0.0, bias=rsm[:, 3:4]), [t_rtmp, t_rsm], [t_rw])
                        S.op("act", lambda e, cs=cs: e.activation(out=rw[:, 2, cs], in_=rtmp[:, 2, cs], func=AF.Exp, scale=-1.0, bias=rsm[:, 4:5]), [t_rtmp, t_rsm], [t_rw])
                        S.op("dve", lambda e, c=c: e.tensor_tensor(out=mprev[:, 0:1], in0=rtmp[:, 2, c * 128 + 127:c * 128 + 128], in1=rsm[:, 1:2], op=ALU.add),
                             [t_rtmp, t_rsm], [t_mprev])

                for hg in range(NHG if STOP >= 3 else 0):
                    heads = list(range(hg * HG, (hg + 1) * HG))
                    fidx = {key: i for i, key in enumerate(fmt[hg])}
                    def pre_gen():
                        if gdn:
                            for qi, nm in ((6, "k"), (5, "q")):
                                ps, t_ps = ps_p.next()
                                for hi, h in enumerate(heads):
                                    ti = fidx[(nm, h, 0)]
                                    S.op("act", lambda e, ti=ti: e.activation(out=junk[:, 0:BLK], in_=fmT[ti], func=AF.Square), [t_fmT[ti]], [t_junk])
                                    S.op("pe", lambda e, h=h, ps=ps: e.matmul(ps[0:8, :], lhsT=ocb[:, h * 8:(h + 1) * 8], rhs=junk[:, 0:BLK], start=(hi == 0), stop=(hi == HG - 1)),
                                         [t_ocb, t_junk], [t_ps])
                                S.op("act", lambda e, ps=ps: e.activation(out=Tm(4), in_=ps[0:8, :], func=AF.Sqrt, bias=1e-6), [t_ps], [t_rtmp])
                                S.op("dve", lambda e, qi=qi: e.reciprocal(out=R(qi), in_=Tm(4)), [t_rtmp], [t_rw])
                                yield
                            S.op("dve", lambda e: e.tensor_scalar_mul(out=R(5), in0=R(5), scalar1=DK ** -0.5), [t_rw], [t_rw])
                            for hi, h in enumerate(heads):
                                ps, t_ps = ps_p.next()
                                S.op("pe", lambda e, h=h, ps=ps: e.matmul(ps[:, :], lhsT=sel(h), rhs=R(6), start=True, stop=True), [t_cst, t_rw], [t_ps])
                                tk = fidx[("k", h, 0)]
                                S.op("dve", lambda e, hi=hi, tk=tk, ps=ps: e.tensor_tensor(out=khT[hi][:], in0=fmT[tk], in1=ps[:, :], op=ALU.mult),
                                     [t_fmT[tk], t_ps], [t_khT[hi]])
                                yield
                                ps, t_ps = ps_p.next()
                                S.op("pe", lambda e, h=h, ps=ps: e.matmul(ps[:, :], lhsT=sel(h), rhs=R(7), start=True, stop=True), [t_cst, t_rw], [t_ps])
                                tq = fidx[("q", h, 0)]
                                S.op("dve", lambda e, hi=hi, tq=tq, ps=ps: e.tensor_tensor(out=qgT[hi][:], in0=fmT[tq], in1=ps[:, :], op=ALU.mult),
                                     [t_fmT[tq], t_ps], [t_qgT[hi]])
                                yield
                        nq = 6 if gdn else 3
                        for c in range(NCH):
                            ps, t_ps = ps_p.next()
                            for q in range(nq):
                                S.op("pe", lambda e, q=q, c=c, ps=ps: e.transpose(ps[:, q * 8:(q + 1) * 8], rw[:, q, c * 128:(c + 1) * 128], cst[0:8, C_ID:C_ID + 8]),
                                     [t_rw, t_cst], [t_ps], signal=(q == nq - 1))
                            S.op("act", lambda e, c=c, ps=ps: e.activation(out=tmsc[c][:, 0:nq, :], in_=ps[:, 0:nq * 8].rearrange("p (q h) -> p q h", q=nq), func=AF.Copy),
                                 [t_ps], [t_tmsc[c]])
                            yield


                    pre_it = pre_gen() if STOP >= 5 else iter(())
                    for ti, (nm, h, sub) in enumerate(fmt[hg]):
                        gti = hg * nfm_hg + ti
                        fidx[(nm, h, sub)] = ti
                        wt, g_wt = wfm_p.next()
                        S.dma("sp", wt, d["fm"][gti].rearrange("p (k c) -> p k c", k=KC), [d["t_fm"]], g_wt, g_wt[0])
                        ps, t_ps = ps_p.next()
                        for kc in range(KC):
                            S.op("pe", lambda e, kc=kc, wt=wt, ps=ps: e.matmul(ps[:, :], lhsT=wt[:, kc, :], rhs=hT[:, kc, :], start=(kc == 0), stop=(kc == KC - 1)),
                                 g_wt + [t_hT], [t_ps], signal=(kc == KC - 1))
                        pc, t_pc = pc_p.next()
                        acc, t_acc = acc_p.next()
                        S.op("act", lambda e, pc=pc, ps=ps: e.activation(out=pc[:, 3:BLK + 3], in_=ps[:, :], func=AF.Copy), [t_ps], [t_pc])
                        S.op("pool", lambda e, pc=pc, gti=gti: e.tensor_copy(out=pc[:, 0:3], in_=carry[:, gti, :]), [t_carry[gti]], [t_pc])
                        S.op("pool", lambda e, pc=pc, gti=gti: e.tensor_copy(out=carry[:, gti, :], in_=pc[:, BLK:BLK + 3]), [t_pc], [t_carry[gti]])
                        S.op("dve", lambda e, pc=pc, acc=acc, gti=gti: e.tensor_scalar_mul(out=acc[:], in0=pc[:, 3:BLK + 3], scalar1=cwt[:, gti * 4 + 3:gti * 4 + 4]),
                             [t_pc, t_cwt], [t_acc])
                        for j in range(3):
                            S.op("dve", lambda e, pc=pc, acc=acc, gti=gti, j=j: e.scalar_tensor_tensor(out=acc[:], in0=pc[:, j:BLK + j], scalar=cwt[:, gti * 4 + j:gti * 4 + j + 1],
                                                                                                     in1=acc[:], op0=ALU.mult, op1=ALU.add),
                                 [t_pc, t_cwt, t_acc], [t_acc])
                        S.op("act", lambda e, acc=acc, ti=ti: e.activation(out=fmT[ti], in_=acc[:], func=AF.Silu), [t_acc], [t_fmT[ti]])

                        if gdn and ti >= 2 * HG:
                            for _ in range(2):
                                next(pre_it, None)

                    ntm_hg = len(tmt[hg])
                    if STOP < 4:
                        continue
                    for ti, (nm, h0) in enumerate(tmt[hg]):
                        gti = hg * ntm_hg + ti
                        wt, g_wt = wtm_p.next()
                        for hh in range(4):
                            S.dma("sp", wt[:, hh * 2:(hh + 1) * 2, :], d["tm"][gti][:, hh * 1024:(hh + 1) * 1024].rearrange("p (k c) -> p k c", k=2),
                                  [d["t_tm"]], g_wt, g_wt[0])
                        for c in range(NCH):
                            ps, t_ps = ps_p.next()
                            for kc in range(KC):
                                S.op("pe", lambda e, kc=kc, c=c, wt=wt, ps=ps: e.matmul(ps[:, :], lhsT=hT[:, kc, c * 128:(c + 1) * 128], rhs=wt[:, kc, :],
                                                                                       start=(kc == 0), stop=(kc == KC - 1)),
                                     g_wt + [t_hT], [t_ps], signal=(kc == KC - 1))
                            zs = zg[c][:, h0 * DV:(h0 + 2) * DV]
                            tz = [t_zg[c][h0], t_zg[c][h0 + 1]]
                            if nm == "z" and gdn:
                                S.op("act", lambda e, ps=ps, zs=zs: e.activation(out=zs, in_=ps[:, :], func=AF.Silu), [t_ps], tz)
                            elif nm == "v":
                                hl = h0 - hg * HG
                                S.op("act", lambda e, ps=ps, c=c, hl=hl: e.activation(out=vaug[c][:, hl:hl + 2, 0:DV], in_=ps[:, :].rearrange("p (h d) -> p h d", h=2), func=AF.Copy),
                                     [t_ps], [t_vaug[c][hl], t_vaug[c][hl + 1]])
                            elif nm == "o":
                                S.op("act", lambda e, ps=ps, zs=zs: e.activation(out=zs, in_=ps[:, :], func=AF.Sigmoid), [t_ps], tz)
                            else:
                                acc, t_acc = acc_p.next()
                                S.op("act", lambda e, ps=ps, acc=acc: e.activation(out=acc[:], in_=ps[:, :], func=AF.Silu), [t_ps], [t_acc])
                                S.op("dve", lambda e, acc=acc, zs=zs: e.tensor_tensor(out=zs, in0=zs, in1=acc[:], op=ALU.mult), [t_acc] + tz, tz)
                            if nm == "z":
                                for hh in range(2):
                                    zh = zg[c][:, (h0 + hh) * DV:(h0 + hh + 1) * DV]
                                    S.op("pool", lambda e, zh=zh: e.tensor_tensor(out=zh, in0=zh, in1=hnwb[:], op=ALU.mult), [t_hnwb, tz[hh]], [tz[hh]])

                    if STOP < 5:
                        continue
                    for _ in pre_it:
                        pass

                    for c in range(NCH if STOP >= 6 else 0):
                        cs = slice(c * 128, (c + 1) * 128)
                        tsc = t_tmsc[c]

                        def scf(q, h, c=c):
                            return tmsc[c][:, q, h:h + 1]

                        HX = [dict(h=h, hi=hi) for hi, h in enumerate(heads)]
                        if gdn:
                            for X in HX:
                                h, hi = X["h"], X["hi"]
                                psL, t_psL = ps_p.next()
                                S.op("pe", lambda e: e.matmul(psL[:, 0:128], lhsT=sel(h), rhs=rw[:, 3, cs], start=True, stop=False), [t_cst, t_rw], [t_psL], signal=False)
                                S.op("pe", lambda e: e.matmul(psL[:, 0:128], lhsT=ident, rhs=cst[:, C_PM:C_PM + 128], start=False, stop=True), [t_cst], [t_psL], signal=False)
                                S.op("pe", lambda e: e.matmul(psL[:, 128:256], lhsT=sel(h), rhs=rw[:, 3, cs], start=True, stop=False), [t_cst, t_rw], [t_psL], signal=False)
                                S.op("pe", lambda e: e.matmul(psL[:, 128:256], lhsT=ident, rhs=cst[:, C_NM:C_NM + 128], start=False, stop=True), [t_cst], [t_psL])
                                Ls, t_Ls = L_p.next()
                                LT, t_LT = L_p.next()
                                dec, t_dec = sm_p.next()
                                S.op("act", lambda e: e.activation(out=Ls[:], in_=psL[:, 0:128], func=AF.Exp, scale=-1.0, bias=scf(3, h)), [t_psL, tsc], [t_Ls])
                                S.op("act", lambda e: e.activation(out=LT[:], in_=psL[:, 128:256], func=AF.Exp, bias=scf(4, h)), [t_psL, tsc], [t_LT])
                                S.op("act", lambda e: e.activation(out=dec[:, 0:1], in_=psL[:, 255:256], func=AF.Exp), [t_psL], [t_dec])
                                X.update(Ls=Ls, t_Ls=t_Ls, LT=LT, t_LT=t_LT, dec=dec, t_dec=t_dec)
                            for X in HX:
                                h, hi = X["h"], X["hi"]
                                tq = fidx[("q", h, 0)]
                                psk, t_psk = ps_p.next()
                                S.op("pe", lambda e: e.matmul(psk[:, 0:128], lhsT=khT[hi][:, cs], rhs=khT[hi][:, cs], start=True, stop=True), [t_khT[hi]], [t_psk], signal=False)
                                S.op("pe", lambda e: e.matmul(psk[:, 128:256], lhsT=khT[hi][:, cs], rhs=fmT[tq][:, cs], start=True, stop=True), [t_khT[hi], t_fmT[tq]], [t_psk])
                                A, t_A = A_p.next()
                                Ls, t_Ls, LT, t_LT = X["Ls"], X["t_Ls"], X["LT"], X["t_LT"]
                                S.op("dve", lambda e: e.scalar_tensor_tensor(out=A[:], in0=psk[:, 0:128], scalar=scf(2, h), in1=Ls[:], op0=ALU.mult, op1=ALU.mult),
                                     [t_psk, tsc, t_Ls], [t_A])
                                S.op("pool", lambda e: e.tensor_tensor(out=A[:], in0=A[:], in1=ident, op=ALU.add), [t_A, t_cst], [t_A])
                                attnT, t_attnT = mbf_p.next()
                                S.op("dve", lambda e: e.tensor_tensor(out=attnT[:], in0=psk[:, 128:256], in1=LT[:], op=ALU.mult), [t_psk, t_LT], [t_attnT])
                                X.update(A=A, t_A=t_A, attnT=attnT, t_attnT=t_attnT, Y=None, t_Y=t_identr, YT=None, t_YT=t_identr)
                            for li, s2 in enumerate(LVLS):
                                lastl = li == len(LVLS) - 1
                                for X in HX:
                                    psP, t_psP = ps_p.next()
                                    Yap = identr[:] if X["Y"] is None else X["Y"][:]
                                    A, t_A = X["A"], X["t_A"]
                                    S.op("pe", lambda e: e.matmul(psP[:, 0:128], lhsT=A[:], rhs=Yap, start=True, stop=True), [t_A, X["t_Y"]], [t_psP])
                                    P, t_P = P_p.next()
                                    if lastl:
                                        S.op("act", lambda e: e.activation(out=P[:], in_=psP[:, 0:128], func=AF.Copy), [t_psP], [t_P])
                                    else:
                                        mk = cst[:, C_LV + li * 128:C_LV + (li + 1) * 128]
                                        S.op("dve", lambda e: e.tensor_tensor(out=P[:], in0=psP[:, 0:128], in1=mk, op=ALU.mult), [t_psP, t_cst], [t_P])
                                    X.update(psP=psP, t_psP=t_psP, P=P, t_P=t_P, Yap=Yap)
                                for X in HX:
                                    psP, t_psP, P, t_P, Yap = X["psP"], X["t_psP"], X["P"], X["t_P"], X["Yap"]
                                    YTap = identr[:] if X["YT"] is None else X["YT"][:]
                                    S.op("pe", lambda e: e.matmul(psP[:, 128:256], lhsT=YTap, rhs=P[:], start=True, stop=True), [X["t_YT"], t_P], [t_psP])
                                    if lastl:
                                        TT, t_TT = mbf_p.next()
                                        S.op("dve", lambda e: e.scalar_tensor_tensor(out=TT[:], in0=Yap, scalar=2.0, in1=psP[:, 128:256], op0=ALU.mult, op1=ALU.subtract),
                                             [X["t_Y"], t_psP], [t_TT])
                                        X.update(TT=TT, t_TT=t_TT)
                                    else:
                                        Yn, t_Yn = Y_p.next()
                                        S.op("dve", lambda e: e.scalar_tensor_tensor(out=Yn[:], in0=Yap, scalar=2.0, in1=psP[:, 128:256], op0=ALU.mult, op1=ALU.subtract),
                                             [X["t_Y"], t_psP], [t_Yn])
                                        X.update(Y=Yn, t_Y=t_Yn)
                                if not lastl:
                                    for X in HX:
                                        psP, t_psP = X["psP"], X["t_psP"]
                                        Yn, t_Yn = X["Y"], X["t_Y"]
                                        psPr = psP[:].bitcast(F32R)
                                        S.op("pe", lambda e: e.transpose(psPr[:, 256:384], Yn[:], identr[:]), [t_Yn, t_identr], [t_psP])
                                        YTn, t_YTn = YT_p.next()
                                        S.op("act", lambda e: e.activation(out=YTn[:], in_=psPr[:, 256:384], func=AF.Copy), [t_psP], [t_YTn])
                                        X.update(YT=YTn, t_YT=t_YTn)
                            for X in HX:
                                h, hi = X["h"], X["hi"]
                                tv0 = fidx[("v", h, 0)]
                                tv1 = fidx[("v", h, 1)]
                                psK, t_psK = ps_p.next()
                                psKb = psK[:].bitcast(BF16)
                                S.op("pe", lambda e: e.transpose(psKb[:, 0:128], khT[hi][:, cs], identb[:]), [t_khT[hi], t_identb], [t_psK], signal=False)
                                S.op("pe", lambda e: e.transpose(psKb[:, 128:256], fmT[tv0][:, cs], identb[:]), [t_fmT[tv0], t_identb], [t_psK], signal=False)
                                S.op("pe", lambda e: e.transpose(psKb[:, 256:384], fmT[tv1][:, cs], identb[:]), [t_fmT[tv1], t_identb], [t_psK])
                                bgK, t_bgK = mbf_p.next()
                                kd, t_kd = mbf_p.next()
                                bV, t_bV = wv_p.next()
                                S.op("act", lambda e: e.activation(out=bgK[:], in_=psKb[:, 0:128], func=AF.Copy, scale=scf(0, h)), [t_psK, tsc], [t_bgK])
                                S.op("act", lambda e: e.activation(out=kd[:], in_=psKb[:, 0:128], func=AF.Copy, scale=scf(1, h)), [t_psK, tsc], [t_kd])
                                S.op("act", lambda e: e.activation(out=bV[:, 0:DV], in_=psKb[:, 128:384], func=AF.Copy, scale=scf(2, h)), [t_psK, tsc], [t_bV])
                                X.update(bgK=bgK, t_bgK=t_bgK, kd=kd, t_kd=t_kd, bV=bV, t_bV=t_bV)
                            for X in HX:
                                psW, t_psW = ps_p.next()
                                bgK, t_bgK, TT, t_TT = X["bgK"], X["t_bgK"], X["TT"], X["t_TT"]
                                S.op("pe", lambda e: e.matmul(psW[:, 0:128], lhsT=bgK[:], rhs=TT[:], start=True, stop=True), [t_bgK, t_TT], [t_psW])
                                nwk, t_nwk = mbf_p.next()
                                S.op("act", lambda e: e.activation(out=nwk[:], in_=psW[:, 0:128], func=AF.Copy, scale=-1.0), [t_psW], [t_nwk])
                                X.update(psW=psW, t_psW=t_psW, nwk=nwk, t_nwk=t_nwk)
                            for X in HX:
                                h = X["h"]
                                psW, t_psW, TT, t_TT, bV, t_bV, nwk, t_nwk = X["psW"], X["t_psW"], X["TT"], X["t_TT"], X["bV"], X["t_bV"], X["nwk"], X["t_nwk"]
                                S.op("pe", lambda e: e.matmul(psW[:, 256:512], lhsT=TT[:], rhs=bV[:, 0:DV], start=True, stop=False), [t_TT, t_bV], [t_psW], signal=False)
                                S.op("pe", lambda e: e.matmul(psW[:, 256:512], lhsT=nwk[:], rhs=Sbf[:, h, 0:DV], start=False, stop=True), [t_nwk, t_Sbf[h]], [t_psW])
                                wv, t_wv = wv_p.next()
                                S.op("act", lambda e: e.activation(out=wv[:, 0:DV], in_=psW[:, 256:512], func=AF.Copy), [t_psW], [t_wv])
                                X.update(wv=wv, t_wv=t_wv)
                            for X in HX:
                                h, hi = X["h"], X["hi"]
                                og_h = zg[c][:, h * DV:(h + 1) * DV]
                                attnT, t_attnT, wv, t_wv, kd, t_kd, dec, t_dec = X["attnT"], X["t_attnT"], X["wv"], X["t_wv"], X["kd"], X["t_kd"], X["dec"], X["t_dec"]
                                psO, t_psO = ps_p.next()
                                S.op("pe", lambda e: e.matmul(psO[:, 0:DV], lhsT=qgT[hi][:, cs], rhs=Sbf[:, h, 0:DV], start=True, stop=False), [t_qgT[hi], t_Sbf[h]], [t_psO], signal=False)
                                S.op("pe", lambda e: e.matmul(psO[:, 0:DV], lhsT=attnT[:], rhs=wv[:, 0:DV], start=False, stop=True), [t_attnT, t_wv], [t_psO], signal=False)
                                S.op("pe", lambda e: e.matmul(psO[:, 256:512], lhsT=kd[:], rhs=wv[:, 0:DV], start=True, stop=True), [t_kd, t_wv], [t_psO])
                                acc, t_acc = acc_p.next()
                                S.op("act", lambda e: e.activation(out=acc[:, 0:DV], in_=psO[:, 256:512], func=AF.Copy), [t_psO], [t_acc])
                                S.op("dve", lambda e: e.scalar_tensor_tensor(out=Sst[:, h, 0:DV], in0=Sst[:, h, 0:DV], scalar=dec[:, 0:1], in1=acc[:, 0:DV], op0=ALU.mult, op1=ALU.add),
                                     [t_S[h], t_dec, t_acc], [t_S[h]])
                                S.op("pool", lambda e: e.tensor_copy(out=Sbf[:, h, 0:DV], in_=Sst[:, h, 0:DV]), [t_S[h]], [t_Sbf[h]])
                                sm, t_sm = sm_p.next()
                                S.op("act", lambda e: e.activation(out=junk[:, 0:DV], in_=psO[:, 0:DV], func=AF.Square, scale=scf(5, h), accum_out=sm[:, 0:1]),
                                     [t_psO, tsc, t_sm], [t_junk, t_sm])
                                S.op("act", lambda e: e.activation(out=sm[:, 1:2], in_=sm[:, 0:1], func=AF.Sqrt, scale=1.0 / DV, bias=EPS), [t_sm], [t_sm])
                                S.op("dve", lambda e: e.reciprocal(out=sm[:, 2:3], in_=sm[:, 1:2]), [t_sm], [t_sm])
                                S.op("dve", lambda e: e.tensor_tensor(out=sm[:, 3:4], in0=sm[:, 2:3], in1=scf(5, h), op=ALU.mult), [t_sm, tsc], [t_sm])
                                S.op("dve", lambda e: e.scalar_tensor_tensor(out=og_h, in0=psO[:, 0:DV], scalar=sm[:, 3:4], in1=og_h, op0=ALU.mult, op1=ALU.mult),
                                     [t_psO, t_sm, t_zg[c][h]], [t_zg[c][h]])
                        else:
                            for X in HX:
                                h, hi = X["h"], X["hi"]
                                tq = fidx[("q", h, 0)]
                                tk = fidx[("k", h, 0)]
                                psA, t_psA = ps_p.next()
                                psT, t_psT = ps_p.next()
                                psAb = psT[:].bitcast(BF16)
                                S.op("pe", lambda e: e.matmul(psA[:, 0:128], lhsT=fmT[tk][:, cs], rhs=fmT[tq][:, cs], start=True, stop=True), [t_fmT[tk], t_fmT[tq]], [t_psA])
                                S.op("pe", lambda e: e.transpose(psAb[:, 0:128], fmT[tk][:, cs], identb[:]), [t_fmT[tk], t_identb], [t_psT])
                                AT, t_AT = mbf_p.next()
                                S.op("dve", lambda e: e.scalar_tensor_tensor(out=AT[:], in0=psA[:, 0:128], scalar=scf(0, h), in1=cst[:, C_MT:C_MT + 128], op0=ALU.mult, op1=ALU.mult),
                                     [t_psA, tsc, t_cst], [t_AT])
                                kw, t_kw = mbf_p.next()
                                S.op("act", lambda e: e.activation(out=kw[:], in_=psAb[:, 0:128], func=AF.Copy, scale=scf(0, h)), [t_psT, tsc], [t_kw])
                                S.op("dve", lambda e: e.tensor_scalar_mul(out=Sst[:, h, :], in0=Sst[:, h, :], scalar1=scf(1, h)), [t_S[h], tsc], [t_S[h]])
                                S.op("pool", lambda e: e.tensor_copy(out=Sbf[:, h, :], in_=Sst[:, h, :]), [t_S[h]], [t_Sbf[h]])
                                X.update(AT=AT, t_AT=t_AT, kw=kw, t_kw=t_kw, tq=tq)
                            for X in HX:
                                h, hi = X["h"], X["hi"]
                                og_h = zg[c][:, h * DV:(h + 1) * DV]
                                AT, t_AT, kw, t_kw, tq = X["AT"], X["t_AT"], X["kw"], X["t_kw"], X["tq"]
                                psO, t_psO = ps_p.next()
                                S.op("pe", lambda e: e.matmul(psO[:, 0:DVA], lhsT=fmT[tq][:, cs], rhs=Sbf[:, h, :], start=True, stop=False), [t_fmT[tq], t_Sbf[h]], [t_psO], signal=False)
                                S.op("pe", lambda e: e.matmul(psO[:, 0:DVA], lhsT=AT[:], rhs=vaug[c][:, hi, :], start=False, stop=True), [t_AT, t_vaug[c][hi]], [t_psO])
                                psS, t_psS = ps_p.next()
                                S.op("pe", lambda e: e.matmul(psS[:, 0:DVA], lhsT=kw[:], rhs=vaug[c][:, hi, :], start=True, stop=True), [t_kw, t_vaug[c][hi]], [t_psS])
                                acc, t_acc = acc_p.next()
                                S.op("act", lambda e: e.activation(out=acc[:, 0:DVA], in_=psS[:, 0:DVA], func=AF.Copy), [t_psS], [t_acc])
                                S.op("dve", lambda e: e.tensor_tensor(out=Sst[:, h, :], in0=Sst[:, h, :], in1=acc[:, 0:DVA], op=ALU.add), [t_S[h], t_acc], [t_S[h]])
                                sm, t_sm = sm_p.next()
                                S.op("act", lambda e: e.activation(out=sm[:, 4:5], in_=psO[:, DV:DV + 1], func=AF.Abs), [t_psO], [t_sm])
                                S.op("dve", lambda e: e.tensor_tensor(out=sm[:, 4:5], in0=sm[:, 4:5], in1=scf(2, h), op=ALU.max), [t_sm, tsc], [t_sm])
                                S.op("dve", lambda e: e.reciprocal(out=sm[:, 5:6], in_=sm[:, 4:5]), [t_sm], [t_sm])
                                S.op("act", lambda e: e.activation(out=junk[:, 0:DV], in_=psO[:, 0:DV], func=AF.Square, scale=sm[:, 5:6], accum_out=sm[:, 0:1]),
                                     [t_psO, t_sm], [t_junk, t_sm])
                                S.op("act", lambda e: e.activation(out=sm[:, 1:2], in_=sm[:, 0:1], func=AF.Sqrt, scale=1.0 / DV, bias=EPS), [t_sm], [t_sm])
                                S.op("dve", lambda e: e.reciprocal(out=sm[:, 2:3], in_=sm[:, 1:2]), [t_sm], [t_sm])
                                S.op("dve", lambda e: e.tensor_tensor(out=sm[:, 3:4], in0=sm[:, 2:3], in1=sm[:, 5:6], op=ALU.mult), [t_sm], [t_sm])
                                S.op("dve", lambda e: e.scalar_tensor_tensor(out=og_h, in0=psO[:, 0:DV], scalar=sm[:, 3:4], in1=og_h, op0=ALU.mult, op1=ALU.mult),
                                     [t_psO, t_sm, t_zg[c][h]], [t_zg[c][h]])

                if STOP < 7:
                    store_all(b)
                    continue
                wov = d["o"].rearrange("p (k n) -> p k n", k=16)
                for n in range(2):
                    wo, g_wo = wo_h[n]
                    for q4 in range(4):
                        S.dma("sp", wo[:, q4 * 4:(q4 + 1) * 4, :], wov[:, q4 * 4:(q4 + 1) * 4, n * 512:(n + 1) * 512], [d["t_o"]], g_wo, g_wo[0])
                for c in range(NCH):
                    ogT = fmA[:, 4 * c:4 * c + 4, :].rearrange("p a (b t) -> p (a b) t", t=128)
                    for half in range(2):
                        ps, t_ps = ps_p.next()
                        psb = ps[:].bitcast(BF16)
                        for k8 in range(8):
                            kc = half * 8 + k8
                            S.op("pe", lambda e, psb=psb, k8=k8, kc=kc, c=c: e.transpose(psb[:, k8 * 128:(k8 + 1) * 128], zg[c][:, kc * 128:(kc + 1) * 128], identb[:]),
                                 [t_zg[c][kc // 2], t_identb], [t_ps], signal=(k8 == 7))
                        S.op("act", lambda e, psb=psb, ogT=ogT, half=half: e.activation(out=ogT[:, half * 8:(half + 1) * 8, :], in_=psb.rearrange("p (k t) -> p k t", k=8), func=AF.Copy),
                             [t_ps], [t_fmT[4 * c + 2 * half], t_fmT[4 * c + 2 * half + 1]])
                for c in range(NCH):
                    for n in range(2):
                        wo, g_wo = wo_h[n]
                        ogT = fmA[:, 4 * c:4 * c + 4, :].rearrange("p a (b t) -> p (a b) t", t=128)
                        ps, t_ps = ps_p.next()
                        for kc in range(16):
                            S.op("pe", lambda e, ps=ps, ogT=ogT, kc=kc: e.matmul(ps[:, :], lhsT=ogT[:, kc, :], rhs=wo[:, kc, :], start=(kc == 0), stop=(kc == 15)),
                                 [t_fmT[4 * c + kc // 4]] + g_wo, [t_ps], signal=(kc == 15))
                        S.op("dve", lambda e, ps=ps, c=c, n=n: e.tensor_tensor(out=xs[:, c, n * 512:(n + 1) * 512], in0=ps[:, :], in1=xs[:, c, n * 512:(n + 1) * 512], op=ALU.add),
                             [t_xsc[c], t_ps], [t_xsc[c]])
                        if n == 1 and not (last and final_norm):
                            S.dma("sp", ov[b][:, c, :], xs[:, c, :], [t_xsc[c]], [t_blkc[b][c]], t_blkc[b][c])
                if last and final_norm:
                    rms_stats(lambda c: xs[:, c, :], NCH, t_xsc, D)
                    for c in range(NCH):
                        S.op("dve", lambda e, c=c: e.scalar_tensor_tensor(out=xs[:, c, :], in0=xs[:, c, :], scalar=st4[:, 8 + c:9 + c], in1=fwb[:], op0=ALU.mult, op1=ALU.mult),
                             [t_xsc[c], t_st4, t_fwb], [t_xsc[c]])
                        S.dma("sp", ov[b][:, c, :], xs[:, c, :], [t_xsc[c]], [t_blkc[b][c]], t_blkc[b][c])

        e = S.eng["sp"]
        for b in range(NB):
            for c in range(NCH):
                S._wait(e, [t_blkc[b][c].w])
        print("semaphores used", S.nsem)
    return nc


def prep_inputs(T, kinds, x_b, p):
    NL = len(kinds)
    m = {"x": np.ascontiguousarray(x_b.reshape(T, D)), "cst": make_consts()}
    nw = np.concatenate([p["norm_w"][:NL], p["final_norm_w"][None, :]], 0).astype(np.float32)
    m["nw"] = np.ascontiguousarray(nw)
    hnw = np.zeros((NL, DV), np.float32)
    gp = np.zeros((NL, 8, 2), np.float32)
    for l, kind in enumerate(kinds):
        j = l // 2
        if kind == "gdn":
            hnw[l] = p["gdn_norm_w"][j]
            gp[l, :, 0] = p["gdn_dt_bias"][j]
            gp[l, :, 1] = p["gdn_a_log"][j]
            fm, tm, g, wo, cw = layout_weights(kind, p["gdn_w_in"][j], p["gdn_conv_w"][j], p["gdn_w_out"][j])
        else:
            hnw[l] = p["mlstm_norm_w"][j]
            gp[l, :, 0] = p["mlstm_i_bias"][j]
            gp[l, :, 1] = p["mlstm_f_bias"][j]
            fm, tm, g, wo, cw = layout_weights(kind, p["mlstm_w_in"][j], p["mlstm_conv_w"][j], p["mlstm_w_out"][j])
        m[f"wfm{l}"], m[f"wtm{l}"], m[f"wg{l}"], m[f"wo{l}"], m[f"cw{l}"] = fm, tm, g, wo, cw
    m["hnw"] = hnw
    m["gp"] = gp
    return m


def run(x, p, kinds, final_norm=True, n_cores=8):
    B, T, _ = x.shape
    nc = build(T, kinds, final_norm)
    base = prep_inputs(T, kinds, x[0], p)
    in_maps = []
    for c in range(n_cores):
        m = dict(base)
        m["x"] = np.ascontiguousarray(x[c % B].reshape(T, D))
        in_maps.append(m)
    res = run_bass_kernel_spmd(nc, in_maps, core_ids=list(range(n_cores)))
    return np.stack([res.results[b]["out"].reshape(T, D) for b in range(B)], 0)


def kernel(**inputs):
    p = {k: np.asarray(v, dtype=np.float32) for k, v in inputs.items()}
    x = p.pop("x")
    kinds = [layer_kind(l) for l in range(4)]
    return run(x, p, kinds).astype(np.float32)
```
